# Optimizing a Trainium2 kernel written in Bass

```python
import math
import jax, jax.numpy as jnp
from jax import lax
import numpy as np

D_MODEL = 1024
BATCH = 4
SEQ = 8192
DEPTH = 2

CHUNK = 64
QBLK = 128
MEM_LEN = 256
EPS = 1e-6

POOL_WIDTH = D_MODEL // 4
POOL_GROUPS = 4
POOL_GDIM = POOL_WIDTH // POOL_GROUPS
POOL_WINDOWS = (2, 4, 8, 16)

RWKV_WIDTH = D_MODEL // 4
RWKV_HEAD = 64
RWKV_HEADS = RWKV_WIDTH // RWKV_HEAD
D_DECAY_LORA = 64
D_AAA_LORA = 64
D_GATE_LORA = 128
RWKV_COLS = 3 * RWKV_WIDTH + D_DECAY_LORA + D_AAA_LORA + D_GATE_LORA
RWKV_SPLITS = tuple(int(i) for i in np.cumsum(
    [RWKV_WIDTH, RWKV_WIDTH, RWKV_WIDTH, D_DECAY_LORA, D_AAA_LORA]))
RWKV_GN_EPS = 64e-5

DIFF_WIDTH = D_MODEL // 2
DIFF_HEADS = 4
DIFF_VDIM = DIFF_WIDTH // DIFF_HEADS
DIFF_QK = DIFF_VDIM // 2
DIFF_COLS = 3 * DIFF_WIDTH

P_IN = POOL_WIDTH + RWKV_COLS + DIFF_COLS

XA_HEADS = 4
XA_HEAD = D_MODEL // XA_HEADS
D_FF = 2816
CONV_W = 3

kernel_name = "hybrid_pool_rwkv7_diffattn_streaming_encoder"

F32 = jnp.float32


def rms_normalize(x, eps=EPS):
    xf = x.astype(F32)
    return xf * lax.rsqrt(jnp.mean(xf * xf, axis=-1, keepdims=True) + eps)


def rmsnorm(x, g):
    return (rms_normalize(x) * g.astype(F32)).astype(x.dtype)


def shift_right(x, n=1):
    return jnp.pad(x, ((0, 0), (n, 0), (0, 0)))[:, :x.shape[1]]


def alibi_slopes(n_heads):
    return jnp.asarray(2.0 ** (-8.0 * (np.arange(n_heads) + 1) / n_heads), dtype=F32)


def pool_mixer(u, w_grp, scale):
    B, S, _ = u.shape
    ug = u.astype(F32).reshape(B, S, POOL_GROUPS, POOL_GDIM)
    t = jnp.arange(S)
    outs = []
    for gi, w in enumerate(POOL_WINDOWS):
        c = jnp.cumsum(ug[:, :, gi], axis=1)
        win_sum = c - shift_right(c, w)
        count = jnp.minimum(t + 1, w).astype(F32)[None, :, None]
        outs.append(win_sum / count - ug[:, :, gi])
    d = jnp.stack(outs, axis=2)
    y = jnp.einsum('bsgc,gcd->bsgd', d, w_grp.astype(F32))
    return (y.reshape(B, S, POOL_WIDTH) * scale.astype(F32)).astype(u.dtype)


def wkv7_scan(r, decay, k, v, a_vec, b_vec):
    B, S, H, N = r.shape

    def step(s, inp):
        r_t, w_t, k_t, v_t, a_t, b_t = inp
        sa = jnp.einsum('bhvk,bhk->bhv', s, a_t)
        s = (s * w_t[:, :, None, :] + sa[..., None] * b_t[:, :, None, :]
             + v_t[..., None] * k_t[:, :, None, :])
        y = jnp.einsum('bhvk,bhk->bhv', s, r_t)
        return s, y

    seq_first = lambda a: jnp.moveaxis(a, 1, 0)
    s0 = jnp.zeros((B, H, N, N), F32)
    _, ys = lax.scan(step, s0, (seq_first(r), seq_first(decay), seq_first(k),
                                seq_first(v), seq_first(a_vec), seq_first(b_vec)))
    return jnp.moveaxis(ys, 0, 1)


def rwkv7_mixer(z, mu, w0, w2, a0, a2, g2, k_k, k_a, r_k, ln_w, ln_b):
    B, S, _ = z.shape
    H, N = RWKV_HEADS, RWKV_HEAD
    zf = z.astype(F32)
    zm = zf + mu.astype(F32) * (shift_right(zf) - zf)
    r, k, v, wd, ad, gd = jnp.split(zm, RWKV_SPLITS, axis=-1)
    w = -jax.nn.softplus(-(w0 + jnp.tanh(wd) @ w2)) - 0.5
    a = jax.nn.sigmoid(a0 + ad @ a2)
    g = jax.nn.sigmoid(gd) @ g2
    heads = lambda t: t.reshape(B, S, H, N)
    r, w, k, v, a = heads(r), heads(w), heads(k), heads(v), heads(a)
    kk = k * k_k.astype(F32).reshape(H, N)
    kk = kk / jnp.maximum(jnp.sqrt(jnp.sum(kk * kk, -1, keepdims=True)), 1e-12)
    k = k * (1.0 + (a - 1.0) * k_a.astype(F32).reshape(H, N))
    decay = jnp.exp(-jnp.exp(w))
    y = wkv7_scan(r, decay, k, v, -kk, kk * a)
    mean = jnp.mean(y, -1, keepdims=True)
    var = jnp.mean(jnp.square(y - mean), -1, keepdims=True)
    y = ((y - mean) * lax.rsqrt(var + RWKV_GN_EPS)).reshape(B, S, RWKV_WIDTH)
    y = y * ln_w.astype(F32) + ln_b.astype(F32)
    bonus = jnp.sum(r * k * r_k.astype(F32), -1, keepdims=True) * v
    y = y + bonus.reshape(B, S, RWKV_WIDTH)
    return (y * g).astype(z.dtype)


def diff_attention(z, q_norm, k_norm, lq1, lk1, lq2, lk2, subln, lambda_init):
    B, S, _ = z.shape
    H = DIFF_HEADS
    q, k, v = jnp.split(z.astype(F32), 3, axis=-1)
    q = rms_normalize(q.reshape(B, S, H, 2, DIFF_QK)) * q_norm.astype(F32)
    k = rms_normalize(k.reshape(B, S, H, 2, DIFF_QK)) * k_norm.astype(F32)
    v = v.reshape(B, S, H, DIFF_VDIM)
    lam = (jnp.exp(jnp.sum(lq1.astype(F32) * lk1.astype(F32)))
           - jnp.exp(jnp.sum(lq2.astype(F32) * lk2.astype(F32))) + lambda_init)
    slopes = alibi_slopes(H)
    nblk = S // QBLK
    qb = q.reshape(B, nblk, QBLK, H, 2, DIFF_QK).transpose(1, 0, 2, 3, 4, 5)
    kpos = jnp.arange(S)
    kchunk = kpos // CHUNK
    scale = DIFF_QK ** -0.5

    def block(args):
        q_blk, start = args
        qpos = start + jnp.arange(QBLK)
        s = jnp.einsum('bqhcd,bkhcd->bhcqk', q_blk, k) * scale
        dist = jnp.abs(qpos[:, None] - kpos[None, :]).astype(F32)
        bias = -slopes[:, None, None, None] * dist
        visible = kchunk[None, :] <= (qpos // CHUNK)[:, None]
        s = jnp.where(visible, s + bias, -jnp.inf)
        p = jax.nn.softmax(s, axis=-1)
        p = p[:, :, 0] - lam * p[:, :, 1]
        return jnp.einsum('bhqk,bkhd->bqhd', p, v)

    starts = jnp.arange(nblk) * QBLK
    o = lax.map(block, (qb, starts))
    o = jnp.moveaxis(o, 0, 1).reshape(B, S, H, DIFF_VDIM)
    o = rms_normalize(o) * subln.astype(F32) * (1.0 - lambda_init)
    return o.reshape(B, S, DIFF_WIDTH).astype(z.dtype)


def memory_cross_attention(xn, memn, wq, wk, wv, wo, q_norm, k_norm):
    B, S, _ = xn.shape
    M = memn.shape[1]
    q = rms_normalize((xn @ wq).reshape(B, S, XA_HEADS, XA_HEAD)) * q_norm.astype(F32)
    k = rms_normalize((memn @ wk).reshape(B, M, XA_HEADS, XA_HEAD)) * k_norm.astype(F32)
    v = (memn @ wv).reshape(B, M, XA_HEADS, XA_HEAD).astype(F32)
    s = jnp.einsum('bshd,bmhd->bhsm', q, k) * (XA_HEAD ** -0.5)
    p = jax.nn.softmax(s, axis=-1)
    o = jnp.einsum('bhsm,bmhd->bshd', p, v).reshape(B, S, D_MODEL).astype(xn.dtype)
    return o @ wo


def conv_ffn(xn, w_up, conv_w, conv_b, w_down):
    h = xn @ w_up
    a, b = jnp.split(h, 2, axis=-1)
    rhs = conv_w.reshape(CONV_W, 1, D_FF).astype(a.dtype)
    c = lax.conv_general_dilated(a, rhs, window_strides=(1,), padding=[(CONV_W - 1, 0)],
                                 dimension_numbers=('NWC', 'WIO', 'NWC'),
                                 feature_group_count=D_FF) + conv_b
    return (jax.nn.gelu(c, approximate=False) * b) @ w_down


def setup_inputs(seed: int = 0) -> dict:
    key = jax.random.key(seed)
    ks = iter(jax.random.split(key, 40))
    nrm = lambda shape, s: jax.random.normal(next(ks), shape, F32) * s
    gain = lambda shape: 1.0 + 0.02 * jax.random.normal(next(ks), shape, F32)
    L = DEPTH
    return {
        "x": nrm((BATCH, SEQ, D_MODEL), 1.0),
        "mem": nrm((BATCH, MEM_LEN, D_MODEL), 1.0),
        "mix_norm_g": gain((L, D_MODEL)),
        "w_in": nrm((L, D_MODEL, P_IN), D_MODEL ** -0.5),
        "pool_w": nrm((L, POOL_GROUPS, POOL_GDIM, POOL_GDIM), POOL_GDIM ** -0.5),
        "pool_scale": gain((L, POOL_WIDTH)),
        "rwkv_mu": jax.random.uniform(next(ks), (L, RWKV_COLS), F32),
        "rwkv_w0": jax.random.uniform(next(ks), (L, RWKV_WIDTH), F32, -5.0, -1.0),
        "rwkv_w2": nrm((L, D_DECAY_LORA, RWKV_WIDTH), 0.5 * D_DECAY_LORA ** -0.5),
        "rwkv_a0": nrm((L, RWKV_WIDTH), 0.1),
        "rwkv_a2": nrm((L, D_AAA_LORA, RWKV_WIDTH), 0.5 * D_AAA_LORA ** -0.5),
        "rwkv_g2": nrm((L, D_GATE_LORA, RWKV_WIDTH), D_GATE_LORA ** -0.5),
        "rwkv_k_k": 0.85 + nrm((L, RWKV_WIDTH), 0.05),
        "rwkv_k_a": 1.0 + nrm((L, RWKV_WIDTH), 0.05),
        "rwkv_r_k": nrm((L, RWKV_HEADS, RWKV_HEAD), 0.1),
        "rwkv_ln_w": gain((L, RWKV_WIDTH)),
        "rwkv_ln_b": nrm((L, RWKV_WIDTH), 0.02),
        "diff_q_norm": gain((L, 2, DIFF_QK)),
        "diff_k_norm": gain((L, 2, DIFF_QK)),
        "diff_lq1": nrm((L, DIFF_QK), 0.1),
        "diff_lk1": nrm((L, DIFF_QK), 0.1),
        "diff_lq2": nrm((L, DIFF_QK), 0.1),
        "diff_lk2": nrm((L, DIFF_QK), 0.1),
        "diff_subln": gain((L, DIFF_VDIM)),
        "w_out": nrm((L, D_MODEL, D_MODEL), D_MODEL ** -0.5),
        "xa_norm_g": gain((L, D_MODEL)),
        "mem_norm_g": gain((L, D_MODEL)),
        "xa_wq": nrm((L, D_MODEL, D_MODEL), D_MODEL ** -0.5),
        "xa_wk": nrm((L, D_MODEL, D_MODEL), D_MODEL ** -0.5),
        "xa_wv": nrm((L, D_MODEL, D_MODEL), D_MODEL ** -0.5),
        "xa_wo": nrm((L, D_MODEL, D_MODEL), D_MODEL ** -0.5),
        "xa_q_norm": gain((L, XA_HEAD)),
        "xa_k_norm": gain((L, XA_HEAD)),
        "ffn_norm_g": gain((L, D_MODEL)),
        "ffn_w_up": nrm((L, D_MODEL, 2 * D_FF), D_MODEL ** -0.5),
        "ffn_conv_w": nrm((L, CONV_W, D_FF), CONV_W ** -0.5),
        "ffn_conv_b": nrm((L, D_FF), 0.02),
        "ffn_w_down": nrm((L, D_FF, D_MODEL), D_FF ** -0.5),
    }


def reference(x, mem, mix_norm_g, w_in, pool_w, pool_scale,
              rwkv_mu, rwkv_w0, rwkv_w2, rwkv_a0, rwkv_a2, rwkv_g2,
              rwkv_k_k, rwkv_k_a, rwkv_r_k, rwkv_ln_w, rwkv_ln_b,
              diff_q_norm, diff_k_norm, diff_lq1, diff_lk1, diff_lq2, diff_lk2, diff_subln,
              w_out, xa_norm_g, mem_norm_g, xa_wq, xa_wk, xa_wv, xa_wo, xa_q_norm, xa_k_norm,
              ffn_norm_g, ffn_w_up, ffn_conv_w, ffn_conv_b, ffn_w_down):
    h = x
    for l in range(DEPTH):
        lambda_init = 0.8 - 0.6 * math.exp(-0.3 * l)
        z = rmsnorm(h, mix_norm_g[l]) @ w_in[l]
        z_pool = z[..., :POOL_WIDTH]
        z_rwkv = z[..., POOL_WIDTH:POOL_WIDTH + RWKV_COLS]
        z_diff = z[..., POOL_WIDTH + RWKV_COLS:]
        y_pool = pool_mixer(z_pool, pool_w[l], pool_scale[l])
        y_rwkv = rwkv7_mixer(z_rwkv, rwkv_mu[l], rwkv_w0[l], rwkv_w2[l], rwkv_a0[l],
                             rwkv_a2[l], rwkv_g2[l], rwkv_k_k[l], rwkv_k_a[l],
                             rwkv_r_k[l], rwkv_ln_w[l], rwkv_ln_b[l])
        y_diff = diff_attention(z_diff, diff_q_norm[l], diff_k_norm[l], diff_lq1[l],
                                diff_lk1[l], diff_lq2[l], diff_lk2[l], diff_subln[l],
                                lambda_init)
        y = jnp.concatenate([y_pool, y_rwkv, y_diff], axis=-1)
        h = h + y @ w_out[l]
        h = h + memory_cross_attention(rmsnorm(h, xa_norm_g[l]), rmsnorm(mem, mem_norm_g[l]),
                                       xa_wq[l], xa_wk[l], xa_wv[l], xa_wo[l],
                                       xa_q_norm[l], xa_k_norm[l])
        h = h + conv_ffn(rmsnorm(h, ffn_norm_g[l]), ffn_w_up[l], ffn_conv_w[l],
                         ffn_conv_b[l], ffn_w_down[l])
    return h
```

```python
import math
import numpy as np
from contextlib import ExitStack
import concourse.bass as bass
import concourse.mybir as mybir
from concourse.bass_utils import run_bass_kernel_spmd

F32 = mybir.dt.float32
BF16 = mybir.dt.bfloat16
AF = mybir.ActivationFunctionType
ALU = mybir.AluOpType
AX = mybir.AxisListType

ENGS = ("pe", "act", "dve", "pool", "sp")
SAME_ENG_SYNC = True


class T:
    def __init__(self, P, h, name, space):
        self.P = P
        self.h = h
        self.name = name
        self.space = space
        self.w = None
        self.r = {}
        self.dsem = None
        self.dcnt = 0

    def __getitem__(self, idx):
        return V(self, self.h[idx])

    @property
    def v(self):
        return V(self, self.h[:])


class V:
    __slots__ = ("t", "ap")

    def __init__(self, t, ap):
        self.t = t
        self.ap = ap

    def __getitem__(self, idx):
        return V(self.t, self.ap[idx])

    def rearrange(self, pat, **kw):
        return V(self.t, self.ap.rearrange(pat, **kw))

    def bitcast(self, dt):
        return V(self.t, self.ap.bitcast(dt))


class Prog:
    def __init__(self, nc):
        self.nc = nc
        self.es = ExitStack()
        self.streams = {e: [] for e in ENGS}
        self.sems = {}
        self.cnt = {e: 0 for e in ENGS}
        self.known = {e: {} for e in ENGS}
        self.out_tokens = []
        self.ntile = 0
        for e in ("pe", "act", "dve", "pool"):
            self.sems[e] = self.es.enter_context(nc.semaphore("s_" + e))

    def sb(self, name, shape, dt=F32):
        self.ntile += 1
        h = self.es.enter_context(self.nc.sbuf_tensor(f"{name}_{self.ntile}", list(shape), dt))
        return T(self, h, name, "sb")

    def ps(self, name, shape, dt=F32):
        self.ntile += 1
        h = self.es.enter_context(self.nc.psum_tensor(f"{name}_{self.ntile}", list(shape), dt))
        return T(self, h, name, "ps")

    def dram(self, name, shape, dt=F32, kind="Internal"):
        h = self.nc.dram_tensor(name, list(shape), dt, kind=kind)
        return T(self, h.ap(), name, "dram")

    def scope(self):
        return _Scope(self)

    def _need(self, eng, tok, waits):
        if tok is None:
            return
        k, v = tok
        if k == eng and (eng == "pe" or not SAME_ENG_SYNC):
            return
        if self.known[eng].get(k, 0) >= v:
            return
        waits[k] = max(waits.get(k, 0), v)

    def _deps(self, eng, reads, writes):
        waits = {}
        for t in reads:
            self._need(eng, t.w, waits)
        for t in writes:
            self._need(eng, t.w, waits)
            for k, v in t.r.items():
                self._need(eng, (k, v), waits)
        for k, v in waits.items():
            self.known[eng][k] = v
        return list(waits.items())

    def _mark(self, tok, reads, writes):
        k, v = tok
        for t in writes:
            t.w = tok
            t.r = {}
        for t in reads:
            if t in writes:
                continue
            t.r[k] = max(t.r.get(k, 0), v)

    def op(self, eng, fn, reads=(), writes=()):
        reads = [x.t if isinstance(x, V) else x for x in reads]
        writes = [x.t if isinstance(x, V) else x for x in writes]
        waits = self._deps(eng, reads, writes)
        self.cnt[eng] += 1
        tok = (eng, self.cnt[eng])
        self.streams[eng].append((waits, fn, eng, 1))
        self._mark(tok, reads, writes)
        return tok

    def dma(self, q, out, in_, **kw):
        sbt = out.t if out.t.space != "dram" else in_.t
        if sbt.dsem is None:
            sbt.dsem = "d%d" % len(self.sems)
            self.sems[sbt.dsem] = self.es.enter_context(self.nc.semaphore(sbt.dsem))
        waits = self._deps(q, [in_.t], [out.t])
        sbt.dcnt += 16
        tok = (sbt.dsem, sbt.dcnt)
        oap, iap = out.ap, in_.ap
        self.streams[q].append((waits, lambda e: e.dma_start(out=oap, in_=iap, **kw), sbt.dsem, 16))
        self._mark(tok, [in_.t], [out.t])
        if out.t.space == "dram":
            self.out_tokens.append(tok)
        return tok

    def emit(self):
        nc = self.nc
        fin = {}
        for k, v in self.out_tokens:
            fin[k] = max(fin.get(k, 0), v)
        for e in ("pe", "act", "dve", "pool"):
            if self.cnt[e]:
                fin[e] = self.cnt[e]
        final_waits = list(fin.items())
        sems = self.sems
        streams = self.streams

        def replay(e, name):
            eng = {"pe": nc.tensor, "act": nc.scalar, "dve": nc.vector, "pool": nc.gpsimd, "sp": nc.sync}[name]
            for waits, fn, isem, iv in streams[name]:
                for k, v in waits:
                    eng.wait_ge(sems[k], v)
                ins = fn(eng)
                ins.then_inc(sems[isem], iv)
            if name == "sp":
                for k, v in final_waits:
                    eng.wait_ge(sems[k], v)

        with nc.Block() as block:
            @block.sync
            def _(e):
                replay(e, "sp")

            @block.tensor
            def _(e):
                replay(e, "pe")

            @block.scalar
            def _(e):
                replay(e, "act")

            @block.vector
            def _(e):
                replay(e, "dve")

            @block.gpsimd
            def _(e):
                replay(e, "pool")
        self.es.close()

    def matmul(self, out, lhsT, rhs, start=True, stop=True, **kw):
        o, l, r = out.ap, lhsT.ap, rhs.ap
        return self.op("pe", lambda e: e.matmul(o, l, r, start=start, stop=stop, **kw),
                       reads=[lhsT, rhs], writes=[out])

    def transpose(self, out, in_, ident):
        o, i, d = out.ap, in_.ap, ident.ap
        return self.op("pe", lambda e: e.transpose(o, i, d), reads=[in_, ident], writes=[out])

    def act(self, out, in_, func, bias=None, scale=1.0, accum_out=None, eng="act"):
        o, i = out.ap, in_.ap
        reads = [in_]
        writes = [out]
        kw = {}
        if bias is not None:
            if isinstance(bias, V):
                reads.append(bias)
                kw["bias"] = bias.ap
            else:
                kw["bias"] = bias
        if isinstance(scale, V):
            reads.append(scale)
            kw["scale"] = scale.ap
        else:
            kw["scale"] = scale
        if accum_out is not None:
            writes.append(accum_out)
            kw["accum_out"] = accum_out.ap
        return self.op(eng, lambda e: e.activation(o, i, func, **kw), reads=reads, writes=writes)

    def tt(self, out, in0, in1, op, eng="dve"):
        o, a, b = out.ap, in0.ap, in1.ap
        return self.op(eng, lambda e: e.tensor_tensor(o, a, b, op), reads=[in0, in1], writes=[out])

    def ts(self, out, in0, s1, op0, s2=None, op1=None, accum_out=None, eng="dve"):
        o, a = out.ap, in0.ap
        reads = [in0]
        writes = [out]
        s1a = s1.ap if isinstance(s1, V) else s1
        s2a = s2.ap if isinstance(s2, V) else s2
        if isinstance(s1, V):
            reads.append(s1)
        if isinstance(s2, V):
            reads.append(s2)
        kw = {}
        if op1 is not None:
            kw["op1"] = op1
        if accum_out is not None:
            writes.append(accum_out)
            kw["accum_out"] = accum_out.ap
        return self.op(eng, lambda e: e.tensor_scalar(o, a, s1a, s2a, op0, **kw), reads=reads, writes=writes)

    def stt(self, out, in0, scalar, in1, op0, op1, eng="dve"):
        o, a, b = out.ap, in0.ap, in1.ap
        reads = [in0, in1]
        sa = scalar.ap if isinstance(scalar, V) else scalar
        if isinstance(scalar, V):
            reads.append(scalar)
        return self.op(eng, lambda e: e.scalar_tensor_tensor(o, a, sa, b, op0, op1), reads=reads, writes=[out])

    def copy(self, out, in_, eng="dve"):
        o, i = out.ap, in_.ap
        if eng == "act":
            return self.op("act", lambda e: e.copy(o, i), reads=[in_], writes=[out])
        return self.op(eng, lambda e: e.tensor_copy(o, i), reads=[in_], writes=[out])

    def memset(self, out, val, eng="dve"):
        o = out.ap
        return self.op(eng, lambda e: e.memset(o, val), reads=[], writes=[out])

    def reduce(self, out, in_, op=None, axis=None, eng="dve"):
        o, i = out.ap, in_.ap
        op = op or ALU.add
        axis = axis or AX.X
        return self.op(eng, lambda e: e.tensor_reduce(o, i, axis, op), reads=[in_], writes=[out])


class _Scope:
    def __init__(self, P):
        self.P = P

    def __enter__(self):
        self.saved = self.P.es
        self.P.es = ExitStack()
        return self

    def __exit__(self, *a):
        self.P.es.close()
        self.P.es = self.saved
        return False


def _recip(self, out, in_):
    o, i = out.ap, in_.ap
    return self.op("dve", lambda e: e.reciprocal(o, i), reads=[in_], writes=[out])


Prog.recip = _recip


Prog.recip = _recip

D = 1024
NT = 512
P_IN = 2816
D_FF = 2816
EPS = 1e-6


def new_nc():
    return bass.Bass("TRN2", target_bir_lowering=False)


class Rot:
    def __init__(self, tiles):
        self.tiles = tiles
        self.i = 0

    def next(self):
        t = self.tiles[self.i % len(self.tiles)]
        self.i += 1
        return t


def load_weight_bf(P, wd, K, N, dst, gain=None, stages=None, ncol=2816, q=("sp", "pool")):
    i = 0
    for kc in range(K // 128):
        for c0 in range(0, N, ncol):
            cw = min(ncol, N - c0)
            st = stages.next()
            P.dma(q[i % len(q)], st[:, 0:cw], wd[kc * 128:(kc + 1) * 128, c0:c0 + cw])
            eng = "dve" if i % 2 == 0 else "pool"
            if gain is not None:
                P.ts(dst[:, kc, c0:c0 + cw], st[:, 0:cw], gain[:, kc:kc + 1], ALU.mult, eng=eng)
            else:
                P.copy(dst[:, kc, c0:c0 + cw], st[:, 0:cw], eng=eng)
            i += 1


def rms_rstd(P, chunks, rows, n, inv_d, eps, ones, sqrot, ps, rstd, tmp):
    nch = len(chunks)
    for i, ch in enumerate(chunks):
        sq = sqrot.next()
        P.act(sq[0:rows, 0:n], ch, AF.Square)
        P.matmul(ps[:, 0:n], ones[0:rows, :], sq[0:rows, 0:n], start=(i == 0), stop=(i == nch - 1))
    M = ones.ap.shape[-1] if hasattr(ones.ap, "shape") else 128
    P.act(tmp[:, 0:n], ps[:, 0:n], AF.Ln, bias=eps, scale=inv_d)
    P.act(rstd, tmp[:, 0:n], AF.Exp, scale=-0.5)


def build_A():
    nc = new_nc()
    P = Prog(nc)
    TOK = 4096
    hT = P.dram("hT", [D, TOK], F32, kind="ExternalInput")
    w_in = P.dram("w_in", [D, P_IN], F32, kind="ExternalInput")
    cA = P.dram("cA", [128, 8], F32, kind="ExternalInput")
    zT = P.dram("zT", [P_IN, TOK], F32, kind="ExternalOutput")
    consts = P.sb("consts", [128, 8])
    P.dma("sp", consts.v, cA.v)
    ones = P.sb("ones", [128, 128])
    P.memset(ones.v, 1.0)
    wbf = P.sb("wbf", [128, 8, P_IN], BF16)
    stages = Rot([P.sb("stg", [128, 2816]) for _ in range(2)])
    load_weight_bf(P, wd=w_in, K=D, N=P_IN, dst=wbf, gain=consts, stages=stages)
    hts = Rot([P.sb("ht", [128, 8, NT]) for _ in range(2)])
    sqrot = Rot([P.sb("sq", [128, NT]) for _ in range(2)])
    ps_ss = P.ps("ps_ss", [128, NT])
    ps_z = Rot([P.ps("ps_z", [128, NT]) for _ in range(3)])
    rstd = P.sb("rstd", [128, NT])
    tmp = P.sb("tmp", [128, NT])
    xn = Rot([P.sb("xn", [128, 8, NT], BF16) for _ in range(2)])
    zo = Rot([P.sb("zo", [128, NT]) for _ in range(4)])
    for ti in range(TOK // NT):
        t0 = ti * NT
        ht = hts.next()
        P.dma("sp", ht.v, hT[:, t0:t0 + NT].rearrange("(c p) n -> p c n", p=128))
        rms_rstd(P, [ht[:, c, :] for c in range(8)], 128, NT, 1.0 / D, EPS, ones.v, sqrot, ps_ss, rstd.v, tmp)
        x = xn.next()
        for c in range(8):
            P.tt(x[:, c, :], ht[:, c, :], rstd.v, ALU.mult, eng="dve" if c % 2 == 0 else "pool")
        for oc in range(P_IN // 128):
            ps = ps_z.next()
            for kc in range(8):
                P.matmul(ps.v, wbf[:, kc, oc * 128:(oc + 1) * 128], x[:, kc, :], start=(kc == 0), stop=(kc == 7))
            o = zo.next()
            if oc % 2 == 0:
                P.copy(o.v, ps.v, eng="act")
            else:
                P.copy(o.v, ps.v, eng="dve")
            P.dma("sp" if oc % 2 == 0 else "pool", zT[oc * 128:(oc + 1) * 128, t0:t0 + NT], o.v)
    P.emit()
    return nc


POOL_WINDOWS = (2, 4, 8, 16)


def build_Bp():
    nc = new_nc()
    P = Prog(nc)
    TOK = 4096
    H = 16
    zpT = P.dram("zpT", [256, H + TOK], F32, kind="ExternalInput")
    wblk = P.dram("wblk", [2, 128, 128], F32, kind="ExternalInput")
    cP = P.dram("cP", [128, 2 + 2 + 32], F32, kind="ExternalInput")
    ypT = P.dram("ypT", [256, TOK], F32, kind="ExternalOutput")
    cs = P.sb("cs", [128, 36])
    P.dma("sp", cs.v, cP.v)
    wb = [P.sb("wb", [128, 128]) for _ in range(2)]
    for i in range(2):
        P.dma("sp", wb[i].v, wblk[i])
    W = H + NT
    us = Rot([P.sb("u", [128, W]) for _ in range(2)])
    s2 = P.sb("s2", [128, W]); s4 = P.sb("s4", [128, W]); s8 = P.sb("s8", [128, W]); s16 = P.sb("s16", [128, W])
    dd = Rot([P.sb("dd", [128, W]) for _ in range(2)])
    win = P.sb("win", [128, W])
    ps = Rot([P.ps("ps", [128, NT]) for _ in range(2)])
    yo = Rot([P.sb("yo", [128, NT]) for _ in range(2)])
    for ti in range(TOK // NT):
        t0 = ti * NT
        for gp in range(2):
            u = us.next()
            P.dma("sp", u.v, zpT[gp * 128:(gp + 1) * 128, t0:t0 + W])
            P.tt(s2[:, 1:W], u[:, 1:W], u[:, 0:W - 1], ALU.add)
            P.tt(s4[:, 3:W], s2[:, 3:W], s2[:, 1:W - 2], ALU.add, eng="pool")
            if gp == 0:
                lo, hi = s2, s4
            else:
                P.tt(s8[:, 7:W], s4[:, 7:W], s4[:, 3:W - 4], ALU.add)
                P.tt(s16[:, 15:W], s8[:, 15:W], s8[:, 7:W - 8], ALU.add, eng="pool")
                lo, hi = s8, s16
            d = dd.next()
            for (a, b, src) in ((0, 64, lo), (64, 128, hi)):
                wsrc = src
                if ti == 0:
                    P.tt(win[a:b, H:H + 16], src[a:b, H:H + 16], cs[a:b, 4 + gp * 16:4 + gp * 16 + 16], ALU.mult)
                    P.stt(d[a:b, H:H + 16], win[a:b, H:H + 16], cs[a:b, 2 + gp:3 + gp], u[a:b, H:H + 16], ALU.mult, ALU.subtract)
                    P.stt(d[a:b, H + 16:W], src[a:b, H + 16:W], cs[a:b, 2 + gp:3 + gp], u[a:b, H + 16:W], ALU.mult, ALU.subtract)
                else:
                    P.stt(d[a:b, H:W], src[a:b, H:W], cs[a:b, 2 + gp:3 + gp], u[a:b, H:W], ALU.mult, ALU.subtract)
            p = ps.next()
            P.matmul(p.v, wb[gp].v, d[:, H:W])
            y = yo.next()
            P.ts(y.v, p.v, cs[:, gp:gp + 1], ALU.mult)
            P.dma("pool", ypT[gp * 128:(gp + 1) * 128, t0:t0 + NT], y.v)
    P.emit()
    return nc


def prep_Bp(z_pool_b, hh, pool_w, pool_scale):
    TOK = 4096
    zp = np.zeros((256, 16 + TOK), np.float32)
    if hh == 0:
        zp[:, 16:] = z_pool_b[0:TOK].T
    else:
        zp[:, :] = z_pool_b[TOK - 16:2 * TOK].T
    wblk = np.zeros((2, 128, 128), np.float32)
    for g in range(4):
        i, o = g // 2, (g % 2) * 64
        wblk[i, o:o + 64, o:o + 64] = pool_w[g]
    cP = np.zeros((128, 36), np.float32)
    for gp in range(2):
        cP[:, gp] = pool_scale[gp * 128:(gp + 1) * 128]
        for half in range(2):
            w = POOL_WINDOWS[gp * 2 + half]
            cP[half * 64:(half + 1) * 64, 2 + gp] = 1.0 / w
            t = np.arange(16)
            corr = (w / np.minimum(t + 1, w)) if hh == 0 else np.ones(16)
            cP[half * 64:(half + 1) * 64, 4 + gp * 16:4 + gp * 16 + 16] = corr[None, :]
    return {"zpT": zp, "wblk": wblk, "cP": cP}


SEQ = 8192
ALIBI = [2.0 ** (-8.0 * (h + 1) / 4) for h in range(4)]
NEG = -30000.0


def build_Bd():
    nc = new_nc()
    P = Prog(nc)
    S = SEQ
    NKB = S // 128
    qin = P.dram("qin", [2, 2, 64, S], F32, kind="ExternalInput")
    kin = P.dram("kin", [2, 2, 64, S], F32, kind="ExternalInput")
    vin = P.dram("vin", [2, S, 128], F32, kind="ExternalInput")
    qrow = P.dram("qrow", [2, 1, S], BF16, kind="ExternalInput")
    btab = P.dram("btab", [2, 128, 64], F32, kind="ExternalInput")
    dtab = P.dram("dtab", [2, 128, 128], BF16, kind="ExternalInput")
    identb = P.dram("identb", [128, 128], BF16, kind="ExternalInput")
    cD = P.dram("cD", [128, 8], F32, kind="ExternalInput")
    lvec = P.dram("lvec", [128, 4, 64], F32, kind="ExternalInput")
    subln = P.dram("subln", [128, 128], F32, kind="ExternalInput")
    yd = P.dram("yd", [S, 256], F32, kind="ExternalOutput")

    cs = P.sb("cs", [128, 8]); P.dma("sp", cs.v, cD.v)
    lv = P.sb("lv", [128, 4, 64]); P.dma("sp", lv.v, lvec.v)
    sl = P.sb("sl", [128, 128]); P.dma("sp", sl.v, subln.v)
    idb = P.sb("idb", [128, 128], BF16); P.dma("sp", idb.v, identb.v)
    ones = P.sb("ones", [128, 128]); P.memset(ones.v, 1.0)
    lt = P.sb("lt", [128, 64]); e1 = P.sb("e1", [128, 1]); e2 = P.sb("e2", [128, 1]); nlam = P.sb("nlam", [128, 1])
    P.tt(lt.v, lv[:, 0, :], lv[:, 1, :], ALU.mult); P.reduce(e1.v, lt.v); P.act(e1.v, e1.v, AF.Exp)
    P.tt(lt.v, lv[:, 2, :], lv[:, 3, :], ALU.mult); P.reduce(e2.v, lt.v); P.act(e2.v, e2.v, AF.Exp)
    P.tt(nlam.v, e2.v, e1.v, ALU.subtract)
    P.tt(nlam.v, nlam.v, cs[:, 4:5], ALU.subtract)
    sls = P.sb("sls", [128, 128])
    P.ts(sls.v, sl.v, cs[:, 5:6], ALU.mult)

    QT = [P.sb("QT", [65, S], BF16) for _ in range(2)]
    KT = [P.sb("KT", [65, S], BF16) for _ in range(2)]
    VA = P.sb("VA", [128, NKB, 129], BF16)
    BT = P.sb("BT", [128, 64]); DT = P.sb("DT", [128, 128], BF16)
    qst = Rot([P.sb("qst", [64, NT]) for _ in range(3)])
    sqr = Rot([P.sb("sqr", [64, NT]) for _ in range(2)])
    tmpr = P.sb("tmpr", [64, NT]); rstd = P.sb("rstd", [64, NT])
    vst = Rot([P.sb("vst", [128, 16, 128]) for _ in range(2)])
    ps_pre = P.ps("ps_pre", [64, NT])
    ps_s = [Rot([P.ps("ps_s", [128, NT]) for _ in range(2)]) for _ in range(2)]
    accs = [P.ps("acc", [128, 3, 129]) for _ in range(3)]
    pT = [Rot([P.sb("pT", [128, NT], BF16) for _ in range(3)]) for _ in range(2)]
    accsb = [P.sb("accsb", [128, 3, 129]) for _ in range(3)]
    fin = {k: Rot([P.sb(k, s) for _ in range(2)]) for k, s in
           (("r0", [128, 1]), ("r1", [128, 1]), ("o0", [128, 128]), ("o", [128, 128]), ("sqo", [128, 128]),
            ("ss", [128, 1]), ("out", [128, 128]))}

    def acc_slot(c, j):
        idx = c * 4 + j
        return idx // 3, idx % 3

    for i in range(2):
        P.dma("sp", BT.v, btab[i]); P.dma("sp", DT.v, dtab[i])
        for c in range(2):
            P.dma("pool", QT[c][64:65, :], qrow[i])
            P.memset(KT[c][64:65, :], 1.0)
            for (src, dst, gcol) in ((qin, QT[c], c), (kin, KT[c], 2 + c)):
                for ti in range(S // NT):
                    t0 = ti * NT
                    st = qst.next()
                    P.dma("sp", st.v, src[i, c, :, t0:t0 + NT])
                    rms_rstd(P, [st.v], 64, NT, 1.0 / 64, EPS, ones[0:64, 0:64], sqr, ps_pre, rstd.v, tmpr)
                    P.stt(dst[0:64, t0:t0 + NT], st.v, cs[0:64, gcol:gcol + 1], rstd.v, ALU.mult, ALU.mult)
        P.memset(VA[:, :, 128:129], 1.0)
        for vc in range(4):
            st = vst.next()
            P.dma("pool", st.v, vin[i, vc * 2048:(vc + 1) * 2048, :].rearrange("(kb p) d -> p kb d", p=128))
            P.copy(VA[:, vc * 16:(vc + 1) * 16, 0:128], st.v, eng="pool")
        for QB in range(S // NT):
            q0 = QB * NT
            tiles = []
            for kb in range(4 * QB + 4):
                i2 = max(0, kb - 4 * QB)
                tiles.append((kb, i2))
            started = set()
            prev = None

            def emit_pv(info):
                kb, i2, pts = info
                for c in range(2):
                    for j in range(i2, 4):
                        bank, slot = acc_slot(c, j)
                        st_flag = bank not in started
                        started.add(bank)
                        P.matmul(accs[bank][:, slot, :], pts[c][:, (j - i2) * 128:(j - i2 + 1) * 128], VA[:, kb, :],
                                 start=st_flag, stop=False, skip_group_check=True)

            for (kb, i2) in tiles:
                N = (4 - i2) * 128
                diag = kb >= 4 * QB
                m = 4 * QB - kb + 3
                pts = []
                for c in range(2):
                    ps = ps_s[c].next()
                    P.matmul(ps[:, 0:N], KT[c][0:65, kb * 128:(kb + 1) * 128], QT[c][0:65, q0 + i2 * 128:q0 + NT],
                             start=True, stop=not diag, skip_group_check=True)
                    if diag:
                        P.matmul(ps[:, 0:128], idb.v, DT.v, start=False, stop=True, skip_group_check=True)
                    pt = pT[c].next()
                    P.act(pt[:, 0:N], ps[:, 0:N], AF.Exp, bias=BT[:, m:m + 1], scale=0.125)
                    pts.append(pt)
                if prev is not None:
                    emit_pv(prev)
                prev = (kb, i2, pts)
            emit_pv(prev)
            for b3 in range(3):
                P.copy(accsb[b3].v, accs[b3].v, eng="dve" if b3 != 1 else "act")
            for j in range(4):
                b0, s0 = acc_slot(0, j); b1, s1 = acc_slot(1, j)
                r0 = fin["r0"].next(); r1 = fin["r1"].next(); o0 = fin["o0"].next(); o = fin["o"].next()
                sqo = fin["sqo"].next(); ss = fin["ss"].next(); out = fin["out"].next()
                P.recip(r0.v, accsb[b0][:, s0, 128:129])
                P.recip(r1.v, accsb[b1][:, s1, 128:129])
                P.tt(r1.v, r1.v, nlam.v, ALU.mult)
                P.ts(o0.v, accsb[b0][:, s0, 0:128], r0.v, ALU.mult)
                P.stt(o.v, accsb[b1][:, s1, 0:128], r1.v, o0.v, ALU.mult, ALU.add)
                P.act(sqo.v, o.v, AF.Square, accum_out=ss.v)
                P.act(ss.v, ss.v, AF.Ln, bias=EPS, scale=1.0 / 128)
                P.act(ss.v, ss.v, AF.Exp, scale=-0.5)
                P.stt(out.v, o.v, ss.v, sls.v, ALU.mult, ALU.mult)
                P.dma("pool", yd[q0 + j * 128:q0 + (j + 1) * 128, i * 128:(i + 1) * 128], out.v)
    P.emit()
    return nc


def prep_Bd(z_b, hh, l, prm):
    import ml_dtypes
    S = SEQ
    base = 256 + 1024
    q = z_b[:, base:base + 512].reshape(S, 4, 2, 64)
    k = z_b[:, base + 512:base + 1024].reshape(S, 4, 2, 64)
    v = z_b[:, base + 1024:base + 1536].reshape(S, 4, 128)
    hs = [2 * hh, 2 * hh + 1]
    qin = np.ascontiguousarray(q[:, hs].transpose(1, 2, 3, 0))
    kin = np.ascontiguousarray(k[:, hs].transpose(1, 2, 3, 0))
    vin = np.ascontiguousarray(v[:, hs].transpose(1, 0, 2))
    li = 0.8 - 0.6 * math.exp(-0.3 * l)
    qrow = np.zeros((2, 1, S), np.float32)
    btab = np.zeros((2, 128, 64), np.float32)
    dtab = np.zeros((2, 128, 128), np.float32)
    ki = np.arange(128)[:, None]
    qi = np.arange(128)[None, :]
    for i, h in enumerate(hs):
        sl = ALIBI[h]
        j = (np.arange(S) // 128) % 4
        qrow[i, 0] = -sl * 128.0 * j * 8.0
        mm = np.arange(64)[None, :]
        btab[i] = sl * (ki - 128.0 * (mm - 3))
        vis = (ki // 64) <= (qi // 64)
        dpr = np.where(qi >= ki, 0.0, 2.0 * sl * (qi - ki))
        dtab[i] = np.where(vis, dpr * 8.0, NEG * 8.0)
    cD = np.zeros((128, 8), np.float32)
    cD[0:64, 0:2] = prm["diff_q_norm"].T
    cD[0:64, 2:4] = prm["diff_k_norm"].T
    cD[:, 4] = li
    cD[:, 5] = 1.0 - li
    lvec = np.stack([np.broadcast_to(prm[n], (128, 64)) for n in ("diff_lq1", "diff_lk1", "diff_lq2", "diff_lk2")], 1)
    return {"qin": qin, "kin": kin, "vin": vin, "qrow": qrow.astype(ml_dtypes.bfloat16), "btab": btab,
            "dtab": dtab.astype(ml_dtypes.bfloat16), "identb": np.eye(128, dtype=np.float32).astype(ml_dtypes.bfloat16),
            "cD": cD, "lvec": np.ascontiguousarray(lvec.astype(np.float32)),
            "subln": np.ascontiguousarray(np.broadcast_to(prm["diff_subln"], (128, 128)).astype(np.float32))}


RWKV_GN_EPS = 64e-5


def build_Br():
    nc = new_nc()
    P = Prog(nc)
    S = SEQ
    SC = 512
    NCH = SC // 128
    rkv = P.dram("rkv", [2, 3, 64, 1 + S], F32, kind="ExternalInput")
    lw = P.dram("lw", [64, 1 + S], F32, kind="ExternalInput")
    la = P.dram("la", [64, 1 + S], F32, kind="ExternalInput")
    lg = P.dram("lg", [128, 1 + S], F32, kind="ExternalInput")
    cR = P.dram("cR", [128, 32], F32, kind="ExternalInput")
    w2 = P.dram("w2", [2, 64, 64], F32, kind="ExternalInput")
    a2 = P.dram("a2", [2, 64, 64], F32, kind="ExternalInput")
    g2 = P.dram("g2", [2, 128, 64], F32, kind="ExternalInput")
    msk = P.dram("msk", [128, 384], F32, kind="ExternalInput")
    idn = P.dram("idn", [128, 128], F32, kind="ExternalInput")
    rmask = P.dram("rmask", [64, SC], F32, kind="ExternalInput")
    yr = P.dram("yr", [128, S], F32, kind="ExternalOutput")

    cs = P.sb("cs", [128, 32]); P.dma("sp", cs.v, cR.v)
    mk = P.sb("mk", [128, 384]); P.dma("sp", mk.v, msk.v)
    ident = P.sb("ident", [128, 128]); P.dma("sp", ident.v, idn.v)
    rm = P.sb("rm", [64, SC]); P.dma("sp", rm.v, rmask.v)
    w2s = [P.sb("w2s", [64, 64]) for _ in range(2)]
    a2s = [P.sb("a2s", [64, 64]) for _ in range(2)]
    g2s = [P.sb("g2s", [128, 64]) for _ in range(2)]
    for i in range(2):
        P.dma("sp", w2s[i].v, w2[i]); P.dma("sp", a2s[i].v, a2[i]); P.dma("sp", g2s[i].v, g2[i])
    ones = P.sb("ones", [64, 64]); P.memset(ones.v, 1.0)
    o64 = P.sb("o64", [64, 64]); P.memset(o64.v, 1.0 / 64)
    omka = P.sb("omka", [64, 2])
    for i in range(2):
        b = 4 + 12 * i
        P.ts(omka[:, i:i + 1], cs[0:64, b + 6:b + 7], -1.0, ALU.mult, 1.0, ALU.add)

    PS = Rot([P.ps("ps", [128, 512]) for _ in range(8)])

    def R(name, shape, n=2, dt=F32):
        return Rot([P.sb(name, shape, dt) for _ in range(n)])

    A = {}
    for nm in ("zr", "zk", "zv"):
        A[nm] = R(nm, [64, SC + 1], 2)
    A["zw"] = R("zw", [64, SC + 1], 2); A["za"] = R("za", [64, SC + 1], 2); A["zg"] = R("zg", [128, SC + 1], 2)
    for nm in ("d64", "rm_", "km_", "vm_", "tw", "adm", "ld", "asig", "g", "kk", "sq", "t1", "t2", "kkn", "k2", "av",
               "bv", "Lc", "eL", "eLp", "enL", "ex", "YT", "dlt", "yn", "bon", "yo"):
        A[nm] = R(nm, [64, SC], 2)
    A["d128"] = R("d128", [128, SC], 2); A["sg"] = R("sg", [128, SC], 2); A["gdm"] = R("gdm", [128, SC], 2)
    A["AR"] = R("AR", [64, NCH, 2, 128], 2); A["BK"] = R("BK", [64, NCH, 2, 128], 2); A["BKH"] = R("BKH", [64, 2, SC], 2)
    A["eLC"] = R("eLC", [64, NCH], 2)
    TM = R("TM", [128, 4, 64], 3); SB1 = R("SB1", [128, 256], 3); SB2 = R("SB2", [128, 256], 3)
    PX = R("PX", [128, 256], 4); PTt = R("PTt", [128, 128], 3); MC = R("MC", [64, 64], 3); QH = R("QH", [64, 128], 3)
    ST = [[P.sb("ST", [64, 64]) for _ in range(2)] for _ in range(2)]
    sti = [0, 0]
    for i in range(2):
        P.memset(ST[i][0].v, 0.0)
    flip = [0]

    def ev():
        flip[0] += 1
        return "dve" if flip[0] % 2 else "act"

    def shift_mix(zt, rows, mucol, dtmp, out):
        P.tt(dtmp[0:rows, :], zt[0:rows, 0:SC], zt[0:rows, 1:SC + 1], ALU.subtract)
        P.stt(out[0:rows, :], dtmp[0:rows, :], cs[0:rows, mucol:mucol + 1], zt[0:rows, 1:SC + 1], ALU.mult, ALU.add)

    for sc in range(S // SC):
        t0 = sc * SC
        zw = A["zw"].next(); za = A["za"].next(); zg = A["zg"].next()
        P.dma("sp", zw.v, lw[:, t0:t0 + SC + 1]); P.dma("sp", za.v, la[:, t0:t0 + SC + 1]); P.dma("sp", zg.v, lg[:, t0:t0 + SC + 1])
        d64 = A["d64"].next(); d128 = A["d128"].next()
        tw = A["tw"].next(); adm = A["adm"].next(); sg = A["sg"].next(); gdm = A["gdm"].next()
        shift_mix(zw, 64, 0, d64, tw); P.act(tw.v, tw.v, AF.Tanh)
        shift_mix(za, 64, 1, d64, adm)
        shift_mix(zg, 128, 2, d128, gdm); P.act(sg.v, gdm.v, AF.Sigmoid)
        for i in range(2):
            b = 4 + 12 * i
            zr = A["zr"].next(); zk = A["zk"].next(); zv = A["zv"].next()
            P.dma("pool", zr.v, rkv[i, 0, :, t0:t0 + SC + 1]); P.dma("pool", zk.v, rkv[i, 1, :, t0:t0 + SC + 1])
            P.dma("pool", zv.v, rkv[i, 2, :, t0:t0 + SC + 1])
            rm_ = A["rm_"].next(); km_ = A["km_"].next(); vm_ = A["vm_"].next()
            shift_mix(zr, 64, b + 0, d64, rm_); shift_mix(zk, 64, b + 1, d64, km_); shift_mix(zv, 64, b + 2, d64, vm_)
            ps = PS.next(); ld = A["ld"].next()
            P.matmul(ps[0:64, :], w2s[i].v, tw.v)
            P.act(ld.v, ps[0:64, :], AF.Sigmoid, bias=cs[0:64, b + 3:b + 4])
            P.ts(ld.v, ld.v, -math.exp(-0.5), ALU.mult)
            ps = PS.next(); asig = A["asig"].next()
            P.matmul(ps[0:64, :], a2s[i].v, adm.v)
            P.act(asig.v, ps[0:64, :], AF.Sigmoid, bias=cs[0:64, b + 4:b + 5])
            ps = PS.next(); g = A["g"].next()
            P.matmul(ps[0:64, :], g2s[i].v, sg.v)
            P.copy(g.v, ps[0:64, :], eng="act")
            kk = A["kk"].next(); sq = A["sq"].next(); t1 = A["t1"].next(); kkn = A["kkn"].next()
            P.ts(kk.v, km_.v, cs[0:64, b + 5:b + 6], ALU.mult)
            P.tt(sq.v, kk.v, kk.v, ALU.mult, eng="pool")
            ps = PS.next()
            P.matmul(ps[0:64, :], ones.v, sq.v)
            P.ts(t1.v, ps[0:64, :], 1e-24, ALU.max)
            P.act(t1.v, t1.v, AF.Ln); P.act(t1.v, t1.v, AF.Exp, scale=-0.5)
            P.tt(kkn.v, kk.v, t1.v, ALU.mult)
            t2 = A["t2"].next(); k2 = A["k2"].next()
            P.ts(t2.v, asig.v, cs[0:64, b + 6:b + 7], ALU.mult, omka[:, i:i + 1], ALU.add)
            P.tt(k2.v, km_.v, t2.v, ALU.mult, eng="pool")
            av = A["av"].next(); bv = A["bv"].next()
            P.ts(av.v, kkn.v, -1.0, ALU.mult, eng="pool")
            P.tt(bv.v, kkn.v, asig.v, ALU.mult, eng="pool")
            Lc = A["Lc"].next(); eL = A["eL"].next(); eLp = A["eLp"].next(); enL = A["enL"].next()
            lo, m_, x_ = Lc.h[:], rm.h[:], ld.h[:]
            P.op("dve", lambda e, lo=lo, m_=m_, x_=x_: e.tensor_tensor_scan(lo, m_, x_, 0.0, ALU.mult, ALU.add),
                 reads=[rm, ld], writes=[Lc])
            P.act(eL.v, Lc.v, AF.Exp)
            P.act(enL.v, Lc.v, AF.Exp, scale=-1.0)
            P.tt(eLp.v, Lc.v, ld.v, ALU.subtract)
            P.act(eLp.v, eLp.v, AF.Exp)
            AR = A["AR"].next(); BK = A["BK"].next(); BKH = A["BKH"].next(); eLC = A["eLC"].next()
            v3 = lambda t: t.v.rearrange("p (c t) -> p c t", t=128)
            P.tt(AR[:, :, 0, :], v3(av), v3(eLp), ALU.mult)
            P.tt(AR[:, :, 1, :], v3(rm_), v3(eL), ALU.mult, eng="pool")
            P.tt(BK[:, :, 0, :], v3(bv), v3(enL), ALU.mult)
            P.tt(BK[:, :, 1, :], v3(k2), v3(enL), ALU.mult, eng="pool")
            P.copy(eLC.v, v3(eL)[:, :, 127])
            ex = A["ex"].next()
            for c in range(NCH):
                P.act(ex[:, c * 128:(c + 1) * 128], Lc[:, c * 128:(c + 1) * 128], AF.Exp,
                      bias=Lc[:, c * 128 + 127:c * 128 + 128], scale=-1.0)
            P.tt(BKH[:, 0, :], bv.v, ex.v, ALU.mult)
            P.tt(BKH[:, 1, :], k2.v, ex.v, ALU.mult, eng="pool")
            YT = A["YT"].next()
            for c in range(NCH):
                cs_ = slice(c * 128, (c + 1) * 128)
                ARc = AR[:, c].rearrange("p a b -> p (a b)")
                ps = PS.next(); tm = TM.next()
                P.transpose(ps[:, 0:64], AR[:, c, 0, :], ident[0:64, 0:64])
                P.transpose(ps[:, 64:128], vm_[:, cs_], ident[0:64, 0:64])
                P.transpose(ps[:, 128:192], BKH[:, 0, cs_], ident[0:64, 0:64])
                P.transpose(ps[:, 192:256], BKH[:, 1, cs_], ident[0:64, 0:64])
                P.copy(tm.v.rearrange("p a b -> p (a b)"), ps[:, 0:256], eng=ev())
                sb1 = SB1.next(); sb2 = SB2.next(); px = PX.next(); pt = PTt.next()
                ps = PS.next()
                P.matmul(ps[:, 0:256], BK[:, c, 0, :], ARc)
                P.tt(sb1.v, ps[:, 0:256], mk[:, 0:256], ALU.mult)
                ps = PS.next()
                P.matmul(ps[:, 0:256], BK[:, c, 1, :], ARc)
                P.tt(sb2.v, ps[:, 0:256], mk[:, 0:256], ALU.mult)
                ps = PS.next()
                P.matmul(ps[:, 0:128], AR[:, c, 0, :], BK[:, c, 0, :])
                P.tt(px[:, 0:128], ps[:, 0:128], mk[:, 256:384], ALU.mult)
                ps = PS.next()
                P.matmul(ps[:, 0:64], sb2[:, 0:128], tm[:, 1, :])
                P.copy(px[:, 192:256], ps[:, 0:64], eng=ev())
                P.copy(px[:, 128:192], tm[:, 0, :], eng="pool")
                P.copy(pt.v, sb1[:, 0:128], eng="pool")
                for j in range(7):
                    ps = PS.next()
                    px2 = PX.next()
                    if j < 5:
                        P.matmul(ps[:, 0:256], pt.v, px.v)
                        P.copy(px2[:, 0:128], ps[:, 0:128], eng="act")
                    else:
                        P.matmul(ps[:, 128:256], pt.v, px[:, 128:256])
                    P.tt(px2[:, 128:256], ps[:, 128:256], px[:, 128:256], ALU.add)
                    if j < 6:
                        ps = PS.next(); pt2 = PTt.next()
                        P.matmul(ps[:, 0:128], px[:, 0:128], pt.v)
                        P.copy(pt2.v, ps[:, 0:128], eng="act" if j % 2 else "dve")
                        pt = pt2
                    px = px2
                Ua = px[:, 128:192]; Uv = px[:, 192:256]
                mc = MC.next(); qh = QH.next()
                ps = PS.next()
                P.matmul(ps[0:64, 0:64], Ua, tm[:, 2, :])
                P.copy(mc.v, ps[0:64, 0:64], eng=ev())
                ps = PS.next()
                P.matmul(ps[0:64, 0:128], Ua, sb1[:, 128:256])
                P.tt(qh.v, ps[0:64, 0:128], AR[:, c, 1, :], ALU.add)
                st_old = ST[i][sti[i] % 2]; st_new = ST[i][(sti[i] + 1) % 2]; sti[i] += 1
                ps = PS.next()
                P.matmul(ps[0:64, 0:128], st_old.v, qh.v, start=True, stop=False)
                P.matmul(ps[0:64, 0:128], Uv, sb1[:, 128:256], start=False, stop=False)
                P.matmul(ps[0:64, 0:128], tm[:, 1, :], sb2[:, 128:256], start=False, stop=True)
                P.copy(YT[:, cs_], ps[0:64, 0:128], eng="act")
                ps = PS.next()
                P.matmul(ps[0:64, 0:64], mc.v, st_old.v, start=True, stop=False)
                P.matmul(ps[0:64, 0:64], tm[:, 2, :], Uv, start=False, stop=False)
                P.matmul(ps[0:64, 0:64], tm[:, 3, :], tm[:, 1, :], start=False, stop=True)
                P.stt(st_new.v, st_old.v, eLC[:, c:c + 1], ps[0:64, 0:64], ALU.mult, ALU.add)
            dlt = A["dlt"].next(); yn = A["yn"].next(); bon = A["bon"].next(); yo = A["yo"].next()
            ps = PS.next()
            P.matmul(ps[0:64, :], o64.v, YT.v)
            P.tt(dlt.v, YT.v, ps[0:64, :], ALU.subtract)
            sq2 = A["sq"].next()
            P.tt(sq2.v, dlt.v, dlt.v, ALU.mult, eng="pool")
            ps = PS.next()
            P.matmul(ps[0:64, :], o64.v, sq2.v)
            t3 = A["t1"].next()
            P.act(t3.v, ps[0:64, :], AF.Ln, bias=RWKV_GN_EPS)
            P.act(t3.v, t3.v, AF.Exp, scale=-0.5)
            P.tt(yn.v, dlt.v, t3.v, ALU.mult)
            P.ts(yn.v, yn.v, cs[0:64, b + 9:b + 10], ALU.mult, cs[0:64, b + 10:b + 11], ALU.add)
            t4 = A["t2"].next()
            P.stt(t4.v, rm_.v, cs[0:64, b + 8:b + 9], k2.v, ALU.mult, ALU.mult)
            ps = PS.next()
            P.matmul(ps[0:64, :], ones.v, t4.v)
            P.tt(bon.v, ps[0:64, :], vm_.v, ALU.mult)
            P.tt(yn.v, yn.v, bon.v, ALU.add, eng="pool")
            P.tt(yo.v, yn.v, g.v, ALU.mult, eng="pool")
            P.dma("sp", yr[i * 64:(i + 1) * 64, t0:t0 + SC], yo.v)
    P.emit()
    return nc


def prep_Br(z_b, hh, prm):
    S = SEQ
    zr = z_b[:, 256:1280]
    hs = [2 * hh, 2 * hh + 1]

    def fm(cols):
        out = np.zeros((cols.shape[1], 1 + S), np.float32)
        out[:, 1:] = cols.T
        return out
    rkv = np.stack([np.stack([fm(zr[:, o + h * 64:o + (h + 1) * 64]) for o in (0, 256, 512)]) for h in hs])
    mu = prm["rwkv_mu"]
    cR = np.zeros((128, 32), np.float32)
    cR[0:64, 0] = mu[768:832]; cR[0:64, 1] = mu[832:896]; cR[:, 2] = mu[896:1024]
    for i, h in enumerate(hs):
        b = 4 + 12 * i
        s = slice(h * 64, (h + 1) * 64)
        cR[0:64, b + 0] = mu[0:256][s]; cR[0:64, b + 1] = mu[256:512][s]; cR[0:64, b + 2] = mu[512:768][s]
        cR[0:64, b + 3] = prm["rwkv_w0"][s]; cR[0:64, b + 4] = prm["rwkv_a0"][s]
        cR[0:64, b + 5] = prm["rwkv_k_k"][s]; cR[0:64, b + 6] = prm["rwkv_k_a"][s]
        cR[0:64, b + 8] = prm["rwkv_r_k"][h]; cR[0:64, b + 9] = prm["rwkv_ln_w"][s]; cR[0:64, b + 10] = prm["rwkv_ln_b"][s]
    t = np.arange(128)
    su = (t[None, :] > t[:, None]).astype(np.float32)
    iu = (t[None, :] >= t[:, None]).astype(np.float32)
    slm = (t[:, None] > t[None, :]).astype(np.float32)
    rmask = np.ones((64, 512), np.float32); rmask[:, ::128] = 0.0
    return {"rkv": np.ascontiguousarray(rkv), "lw": fm(zr[:, 768:832]), "la": fm(zr[:, 832:896]), "lg": fm(zr[:, 896:1024]),
            "cR": cR,
            "w2": np.ascontiguousarray(np.stack([prm["rwkv_w2"][:, h * 64:(h + 1) * 64] for h in hs])),
            "a2": np.ascontiguousarray(np.stack([prm["rwkv_a2"][:, h * 64:(h + 1) * 64] for h in hs])),
            "g2": np.ascontiguousarray(np.stack([prm["rwkv_g2"][:, h * 64:(h + 1) * 64] for h in hs])),
            "msk": np.concatenate([su, iu, slm], 1), "idn": np.eye(128, dtype=np.float32), "rmask": rmask}


def build_C1():
    nc = new_nc()
    P = Prog(nc)
    TOK = 4096
    n = 256
    M = 256
    hT = P.dram("hT", [D, TOK], F32, kind="ExternalInput")
    yT = P.dram("yT", [D, TOK], F32, kind="ExternalInput")
    memT = P.dram("memT", [D, M], F32, kind="ExternalInput")
    wd = {k: P.dram(k, [D, D], F32, kind="ExternalInput") for k in ("w_out", "wq", "wk", "wv", "wo")}
    cC = P.dram("cC", [128, 20], F32, kind="ExternalInput")
    h2T = P.dram("h2T", [D, TOK], F32, kind="ExternalOutput")
    cs = P.sb("cs", [128, 20]); P.dma("sp", cs.v, cC.v)
    ones = P.sb("ones", [128, 128]); P.memset(ones.v, 1.0)
    onesb = P.sb("onesb", [128, 128], BF16); P.memset(onesb.v, 1.0)
    stages = Rot([P.sb("stg", [128, 1024]) for _ in range(2)])
    bufA = P.sb("bufA", [128, 8, D], BF16); bufB = P.sb("bufB", [128, 8, D], BF16); bufC = P.sb("bufC", [128, 8, D], BF16)
    PS = Rot([P.ps("ps", [128, 512]) for _ in range(7)])
    ps_ss = P.ps("ps_ss", [128, 512])
    sqrot = Rot([P.sb("sq", [128, n]) for _ in range(2)])
    rstd = P.sb("rstd", [128, n]); tmp = P.sb("tmp", [128, n])
    flip = [0]

    def ev():
        flip[0] += 1
        return "dve" if flip[0] % 2 else "act"

    load_weight_bf(P, wd["wk"], D, D, bufA, gain=cs[:, 8:16], stages=stages, ncol=1024)
    load_weight_bf(P, wd["wv"], D, D, bufB, gain=cs[:, 8:16], stages=stages, ncol=1024)
    load_weight_bf(P, wd["wq"], D, D, bufC, gain=cs[:, 0:8], stages=stages, ncol=1024)
    mt = P.sb("mt", [128, 8, M]); P.dma("sp", mt.v, memT.v.rearrange("(c p) m -> p c m", p=128))
    memn = P.sb("memn", [128, 8, M], BF16)
    rms_rstd(P, [mt[:, c, :] for c in range(8)], 128, M, 1.0 / D, EPS, ones.v, sqrot, ps_ss, rstd[:, 0:M], tmp)
    for c in range(8):
        P.tt(memn[:, c, :], mt[:, c, :], rstd[:, 0:M], ALU.mult, eng="dve" if c % 2 else "pool")
    kraw = P.sb("kraw", [128, 8, M]); kn = P.sb("kn", [128, 8, M], BF16); vb = P.sb("vb", [128, 2, D], BF16)
    for oc in range(8):
        ps = PS.next()
        for kc in range(8):
            P.matmul(ps[:, 0:M], bufA[:, kc, oc * 128:(oc + 1) * 128], memn[:, kc, :], start=(kc == 0), stop=(kc == 7))
        P.copy(kraw[:, oc, :], ps[:, 0:M], eng=ev())
    for hd in range(4):
        rms_rstd(P, [kraw[:, 2 * hd, :], kraw[:, 2 * hd + 1, :]], 128, M, 1.0 / 256, EPS, ones.v, sqrot, ps_ss, rstd[:, 0:M], tmp)
        for dc in range(2):
            P.stt(kn[:, 2 * hd + dc, :], kraw[:, 2 * hd + dc, :], cs[:, 18 + dc:19 + dc], rstd[:, 0:M], ALU.mult, ALU.mult)
    for mc in range(2):
        for half in range(2):
            ps = PS.next()
            for kc in range(8):
                P.matmul(ps.v, memn[:, kc, mc * 128:(mc + 1) * 128], bufB[:, kc, half * 512:(half + 1) * 512],
                         start=(kc == 0), stop=(kc == 7))
            P.copy(vb[:, mc, half * 512:(half + 1) * 512], ps.v, eng=ev())
    load_weight_bf(P, wd["w_out"], D, D, bufA, gain=None, stages=stages, ncol=1024)
    load_weight_bf(P, wd["wo"], D, D, bufB, gain=None, stages=stages, ncol=1024)
    hts = Rot([P.sb("ht", [128, 8, n]) for _ in range(3)])
    yts = Rot([P.sb("yt", [128, 8, n]) for _ in range(2)])
    ybf = P.sb("ybf", [128, 8, n], BF16); xn = P.sb("xn", [128, 8, n], BF16)
    qraw = P.sb("qraw", [128, 8, n]); qn = P.sb("qn", [128, 8, n], BF16); ob = P.sb("ob", [128, 8, n], BF16)
    pTs = Rot([P.sb("pT", [128, n], BF16) for _ in range(4)])
    rs = Rot([P.sb("rs", [128, n]) for _ in range(2)])
    for ti in range(TOK // n):
        t0 = ti * n
        ht = hts.next(); yt = yts.next()
        P.dma("sp", ht.v, hT[:, t0:t0 + n].rearrange("(c p) n -> p c n", p=128))
        P.dma("pool", yt.v, yT[:, t0:t0 + n].rearrange("(c p) n -> p c n", p=128))
        for c in range(8):
            P.copy(ybf[:, c, :], yt[:, c, :], eng="pool" if c % 2 else "dve")
        for oc in range(8):
            ps = PS.next()
            for kc in range(8):
                P.matmul(ps[:, 0:n], bufA[:, kc, oc * 128:(oc + 1) * 128], ybf[:, kc, :], start=(kc == 0), stop=(kc == 7))
            P.tt(ht[:, oc, :], ps[:, 0:n], ht[:, oc, :], ALU.add)
        rms_rstd(P, [ht[:, c, :] for c in range(8)], 128, n, 1.0 / D, EPS, ones.v, sqrot, ps_ss, rstd.v, tmp)
        for c in range(8):
            P.tt(xn[:, c, :], ht[:, c, :], rstd.v, ALU.mult, eng="dve" if c % 2 else "pool")
        for oc in range(8):
            ps = PS.next()
            for kc in range(8):
                P.matmul(ps[:, 0:n], bufC[:, kc, oc * 128:(oc + 1) * 128], xn[:, kc, :], start=(kc == 0), stop=(kc == 7))
            P.copy(qraw[:, oc, :], ps[:, 0:n], eng=ev())
        for hd in range(4):
            rms_rstd(P, [qraw[:, 2 * hd, :], qraw[:, 2 * hd + 1, :]], 128, n, 1.0 / 256, EPS, ones.v, sqrot, ps_ss, rstd.v, tmp)
            for dc in range(2):
                P.stt(qn[:, 2 * hd + dc, :], qraw[:, 2 * hd + dc, :], cs[:, 16 + dc:17 + dc], rstd.v, ALU.mult, ALU.mult)
            pts = []
            for mc in range(2):
                ps = PS.next()
                for dc in range(2):
                    P.matmul(ps[:, 0:n], kn[:, 2 * hd + dc, mc * 128:(mc + 1) * 128], qn[:, 2 * hd + dc, :],
                             start=(dc == 0), stop=(dc == 1))
                pt = pTs.next()
                P.act(pt.v, ps[:, 0:n], AF.Exp, scale=1.0 / 16)
                pts.append(pt)
            ps = PS.next()
            for mc in range(2):
                P.matmul(ps[:, 0:n], onesb.v, pts[mc].v, start=(mc == 0), stop=(mc == 1))
            r = rs.next()
            P.recip(r.v, ps[:, 0:n])
            for dc in range(2):
                ps = PS.next()
                for mc in range(2):
                    P.matmul(ps[:, 0:n], vb[:, mc, (2 * hd + dc) * 128:(2 * hd + dc + 1) * 128], pts[mc].v,
                             start=(mc == 0), stop=(mc == 1))
                P.tt(ob[:, 2 * hd + dc, :], ps[:, 0:n], r.v, ALU.mult)
        for oc in range(8):
            ps = PS.next()
            for kc in range(8):
                P.matmul(ps[:, 0:n], bufB[:, kc, oc * 128:(oc + 1) * 128], ob[:, kc, :], start=(kc == 0), stop=(kc == 7))
            P.tt(ht[:, oc, :], ps[:, 0:n], ht[:, oc, :], ALU.add)
        P.dma("sp", h2T[:, t0:t0 + n].rearrange("(c p) n -> p c n", p=128), ht.v)
    P.emit()
    return nc


def vec_pc(v):
    return np.ascontiguousarray(np.asarray(v, np.float32).reshape(-1, 128).T)


def prep_C1(hT, yT, mem_b, prm):
    cC = np.concatenate([vec_pc(prm["xa_norm_g"]), vec_pc(prm["mem_norm_g"]), vec_pc(prm["xa_q_norm"]),
                         vec_pc(prm["xa_k_norm"])], 1)
    return {"hT": hT, "yT": yT, "memT": np.ascontiguousarray(mem_b.T), "w_out": prm["w_out"], "wq": prm["xa_wq"],
            "wk": prm["xa_wk"], "wv": prm["xa_wv"], "wo": prm["xa_wo"], "cC": np.ascontiguousarray(cC)}


def build_C2():
    nc = new_nc()
    P = Prog(nc)
    TOK = 4096
    n = 256
    NF = D_FF // 128
    h2T = P.dram("h2T", [D, 2 + TOK], F32, kind="ExternalInput")
    w_up = P.dram("w_up", [D, 2 * D_FF], F32, kind="ExternalInput")
    w_dn = P.dram("w_dn", [D_FF, D], F32, kind="ExternalInput")
    cF = P.dram("cF", [128, 8 + NF * 4], F32, kind="ExternalInput")
    h3T = P.dram("h3T", [D, TOK], F32, kind="ExternalOutput")
    cs = P.sb("cs", [128, 8 + NF * 4]); P.dma("sp", cs.v, cF.v)
    ones = P.sb("ones", [128, 128]); P.memset(ones.v, 1.0)
    stages = Rot([P.sb("stg", [128, 1408]) for _ in range(2)])
    wup = P.sb("wup", [128, 8, 2 * D_FF], BF16)
    wdn = P.sb("wdn", [128, NF, D], BF16)
    load_weight_bf(P, w_up, D, 2 * D_FF, wup, gain=cs[:, 0:8], stages=stages, ncol=1408)
    load_weight_bf(P, w_dn, D_FF, D, wdn, gain=None, stages=stages, ncol=1024)
    PS = Rot([P.ps("ps", [128, 512]) for _ in range(7)])
    ps_ss = P.ps("ps_ss", [128, 512])
    sqrot = Rot([P.sb("sq", [128, n]) for _ in range(2)])
    rstd = P.sb("rstd", [128, n]); tmp = P.sb("tmp", [128, n])
    hts = Rot([P.sb("ht", [128, 8, n]) for _ in range(3)])
    xn = P.sb("xn", [128, 8, n], BF16)
    asb = Rot([P.sb("asb", [128, 2 + n]) for _ in range(3)])
    cc = Rot([P.sb("cc", [128, n]) for _ in range(2)])
    gl = Rot([P.sb("gl", [128, n]) for _ in range(2)])
    gbf = P.sb("gbf", [128, NF, n], BF16)
    aprev = P.sb("aprev", [128, NF, 2])

    def do_tile(c0, w, full):
        ht = hts.next()
        P.dma("sp", ht[:, :, 0:w], h2T[:, c0:c0 + w].rearrange("(c p) n -> p c n", p=128))
        rms_rstd(P, [ht[:, c, 0:w] for c in range(8)], 128, w, 1.0 / D, EPS, ones.v, sqrot, ps_ss, rstd[:, 0:w], tmp)
        for c in range(8):
            P.tt(xn[:, c, 0:w], ht[:, c, 0:w], rstd[:, 0:w], ALU.mult, eng="dve" if c % 2 else "pool")
        for f in range(NF):
            psa = PS.next()
            for kc in range(8):
                P.matmul(psa[:, 0:w], wup[:, kc, f * 128:(f + 1) * 128], xn[:, kc, 0:w], start=(kc == 0), stop=(kc == 7))
            if not full:
                P.copy(aprev[:, f, :], psa[:, 0:2], eng="act")
                continue
            psb = PS.next()
            for kc in range(8):
                P.matmul(psb[:, 0:w], wup[:, kc, D_FF + f * 128:D_FF + (f + 1) * 128], xn[:, kc, 0:w],
                         start=(kc == 0), stop=(kc == 7))
            a = asb.next()
            P.copy(a[:, 2:2 + w], psa[:, 0:w], eng="act")
            P.copy(a[:, 0:2], aprev[:, f, :], eng="pool")
            P.copy(aprev[:, f, :], a[:, w:w + 2], eng="pool")
            cb = 8 + f * 4
            c = cc.next()
            P.ts(c.v, a[:, 2:2 + w], cs[:, cb + 2:cb + 3], ALU.mult, cs[:, cb + 3:cb + 4], ALU.add)
            P.stt(c.v, a[:, 1:1 + w], cs[:, cb + 1:cb + 2], c.v, ALU.mult, ALU.add)
            P.stt(c.v, a[:, 0:w], cs[:, cb + 0:cb + 1], c.v, ALU.mult, ALU.add)
            g = gl.next()
            P.act(g.v, c.v, AF.Gelu)
            P.tt(gbf[:, f, :], psb[:, 0:w], g.v, ALU.mult)
        if not full:
            return
        for oc in range(8):
            ps = PS.next()
            for f in range(NF):
                P.matmul(ps[:, 0:w], wdn[:, f, oc * 128:(oc + 1) * 128], gbf[:, f, :], start=(f == 0), stop=(f == NF - 1))
            P.tt(ht[:, oc, :], ps[:, 0:w], ht[:, oc, :], ALU.add)
        P.dma("pool", h3T[:, c0 - 2:c0 - 2 + w].rearrange("(c p) n -> p c n", p=128), ht.v)

    do_tile(0, 2, False)
    for ti in range(TOK // n):
        do_tile(2 + ti * n, n, True)
    P.emit()
    return nc


def prep_C2(h2T_halo, prm):
    NF = D_FF // 128
    cw = prm["ffn_conv_w"]
    cols = [vec_pc(prm["ffn_norm_g"])]
    per = np.zeros((128, NF, 4), np.float32)
    for j in range(3):
        per[:, :, j] = cw[j].reshape(NF, 128).T
    per[:, :, 3] = prm["ffn_conv_b"].reshape(NF, 128).T
    cF = np.concatenate([cols[0], per.reshape(128, NF * 4)], 1)
    return {"h2T": h2T_halo, "w_up": prm["ffn_w_up"], "w_dn": prm["ffn_w_down"], "cF": np.ascontiguousarray(cF)}


_NC_CACHE = {}


def _get_nc(name):
    if name not in _NC_CACHE:
        _NC_CACHE[name] = {"A": build_A, "Bp": build_Bp, "Bd": build_Bd, "Br": build_Br, "C1": build_C1,
                           "C2": build_C2}[name]()
    return _NC_CACHE[name]


def _run(name, in_maps):
    nc = _get_nc(name)
    res = run_bass_kernel_spmd(nc, in_maps, core_ids=list(range(8)))
    return res.results


PARAM_KEYS = ("mix_norm_g", "w_in", "pool_w", "pool_scale", "rwkv_mu", "rwkv_w0", "rwkv_w2", "rwkv_a0", "rwkv_a2",
              "rwkv_g2", "rwkv_k_k", "rwkv_k_a", "rwkv_r_k", "rwkv_ln_w", "rwkv_ln_b", "diff_q_norm", "diff_k_norm",
              "diff_lq1", "diff_lk1", "diff_lq2", "diff_lk2", "diff_subln", "w_out", "xa_norm_g", "mem_norm_g",
              "xa_wq", "xa_wk", "xa_wv", "xa_wo", "xa_q_norm", "xa_k_norm", "ffn_norm_g", "ffn_w_up", "ffn_conv_w",
              "ffn_conv_b", "ffn_w_down")


def kernel(x, mem, **params):
    x = np.asarray(x, np.float32)
    mem = np.asarray(mem, np.float32)
    B, S, _ = x.shape
    TOK = S // 2
    depth = np.asarray(params["w_in"]).shape[0]
    hT = [np.ascontiguousarray(x[c // 2, (c % 2) * TOK:(c % 2 + 1) * TOK].T) for c in range(8)]
    for l in range(depth):
        prm = {k: np.ascontiguousarray(np.asarray(params[k][l], np.float32)) for k in PARAM_KEYS}
        cA = vec_pc(prm["mix_norm_g"])
        rA = _run("A", [{"hT": hT[c], "w_in": prm["w_in"], "cA": cA} for c in range(8)])
        zb = [np.concatenate([rA[2 * b]["zT"].T, rA[2 * b + 1]["zT"].T], axis=0) for b in range(B)]
        del rA
        rP = _run("Bp", [prep_Bp(zb[c // 2][:, 0:256], c % 2, prm["pool_w"], prm["pool_scale"]) for c in range(8)])
        rR = _run("Br", [prep_Br(zb[c // 2], c % 2, prm) for c in range(8)])
        rD = _run("Bd", [prep_Bd(zb[c // 2], c % 2, l, prm) for c in range(8)])
        del zb
        yT = []
        for c in range(8):
            b, hh = c // 2, c % 2
            y = np.empty((D, TOK), np.float32)
            y[0:256] = rP[c]["ypT"]
            for h2 in range(2):
                cc = 2 * b + h2
                y[256 + h2 * 128:256 + (h2 + 1) * 128] = rR[cc]["yr"][:, hh * TOK:(hh + 1) * TOK]
                y[512 + h2 * 256:512 + (h2 + 1) * 256] = rD[cc]["yd"][hh * TOK:(hh + 1) * TOK].T
            yT.append(y)
        del rP, rR, rD
        r1 = _run("C1", [prep_C1(hT[c], yT[c], mem[c // 2], prm) for c in range(8)])
        del yT
        h2 = [r1[c]["h2T"] for c in range(8)]
        del r1
        ins = []
        for c in range(8):
            hh = c % 2
            hal = np.zeros((D, 2 + TOK), np.float32)
            hal[:, 2:] = h2[c]
            if hh == 1:
                hal[:, 0:2] = h2[c - 1][:, TOK - 2:TOK]
            ins.append(prep_C2(hal, prm))
        r2 = _run("C2", ins)
        hT = [np.ascontiguousarray(r2[c]["h3T"]) for c in range(8)]
        del r2, h2, ins
    out = np.empty((B, S, D), np.float32)
    for c in range(8):
        out[c // 2, (c % 2) * TOK:(c % 2 + 1) * TOK] = hT[c].T
    return out
```

```python
import math
import numpy as np
from contextlib import ExitStack
import concourse.bass as bass
import concourse.mybir as mybir
from concourse.bass_utils import run_bass_kernel_spmd

F32 = mybir.dt.float32
BF16 = mybir.dt.bfloat16
AF = mybir.ActivationFunctionType
ALU = mybir.AluOpType
AX = mybir.AxisListType

ENGS = ("pe", "act", "dve", "pool", "sp")
SAME_ENG_SYNC = True


class T:
    def __init__(self, P, h, name, space):
        self.P = P
        self.h = h
        self.name = name
        self.space = space
        self.w = None
        self.r = {}
        self.dsem = None
        self.dcnt = 0

    def __getitem__(self, idx):
        return V(self, self.h[idx])

    @property
    def v(self):
        return V(self, self.h[:])


class V:
    __slots__ = ("t", "ap")

    def __init__(self, t, ap):
        self.t = t
        self.ap = ap

    def __getitem__(self, idx):
        return V(self.t, self.ap[idx])

    def rearrange(self, pat, **kw):
        return V(self.t, self.ap.rearrange(pat, **kw))

    def bitcast(self, dt):
        return V(self.t, self.ap.bitcast(dt))


class Prog:
    def __init__(self, nc):
        self.nc = nc
        self.es = ExitStack()
        self.streams = {e: [] for e in ENGS}
        self.sems = {}
        self.cnt = {e: 0 for e in ENGS}
        self.known = {e: {} for e in ENGS}
        self.out_tokens = []
        self.ntile = 0
        for e in ("pe", "act", "dve", "pool"):
            self.sems[e] = self.es.enter_context(nc.semaphore("s_" + e))

    def sb(self, name, shape, dt=F32):
        self.ntile += 1
        h = self.es.enter_context(self.nc.sbuf_tensor(f"{name}_{self.ntile}", list(shape), dt))
        return T(self, h, name, "sb")

    def ps(self, name, shape, dt=F32):
        self.ntile += 1
        h = self.es.enter_context(self.nc.psum_tensor(f"{name}_{self.ntile}", list(shape), dt))
        return T(self, h, name, "ps")

    def dram(self, name, shape, dt=F32, kind="Internal"):
        h = self.nc.dram_tensor(name, list(shape), dt, kind=kind)
        return T(self, h.ap(), name, "dram")

    def scope(self):
        return _Scope(self)

    def _need(self, eng, tok, waits):
        if tok is None:
            return
        k, v = tok
        if k == eng and (eng == "pe" or not SAME_ENG_SYNC):
            return
        if self.known[eng].get(k, 0) >= v:
            return
        waits[k] = max(waits.get(k, 0), v)

    def _deps(self, eng, reads, writes):
        waits = {}
        for t in reads:
            self._need(eng, t.w, waits)
        for t in writes:
            self._need(eng, t.w, waits)
            for k, v in t.r.items():
                self._need(eng, (k, v), waits)
        for k, v in waits.items():
            self.known[eng][k] = v
        return list(waits.items())

    def _mark(self, tok, reads, writes):
        k, v = tok
        for t in writes:
            t.w = tok
            t.r = {}
        for t in reads:
            if t in writes:
                continue
            t.r[k] = max(t.r.get(k, 0), v)

    def op(self, eng, fn, reads=(), writes=()):
        reads = [x.t if isinstance(x, V) else x for x in reads]
        writes = [x.t if isinstance(x, V) else x for x in writes]
        waits = self._deps(eng, reads, writes)
        self.cnt[eng] += 1
        tok = (eng, self.cnt[eng])
        self.streams[eng].append((waits, fn, eng, 1))
        self._mark(tok, reads, writes)
        return tok

    def dma(self, q, out, in_, **kw):
        sbt = out.t if out.t.space != "dram" else in_.t
        if sbt.dsem is None:
            sbt.dsem = "d%d" % len(self.sems)
            self.sems[sbt.dsem] = self.es.enter_context(self.nc.semaphore(sbt.dsem))
        waits = self._deps(q, [in_.t], [out.t])
        sbt.dcnt += 16
        tok = (sbt.dsem, sbt.dcnt)
        oap, iap = out.ap, in_.ap
        self.streams[q].append((waits, lambda e: e.dma_start(out=oap, in_=iap, **kw), sbt.dsem, 16))
        self._mark(tok, [in_.t], [out.t])
        if out.t.space == "dram":
            self.out_tokens.append(tok)
        return tok

    def emit(self):
        nc = self.nc
        fin = {}
        for k, v in self.out_tokens:
            fin[k] = max(fin.get(k, 0), v)
        for e in ("pe", "act", "dve", "pool"):
            if self.cnt[e]:
                fin[e] = self.cnt[e]
        final_waits = list(fin.items())
        sems = self.sems
        streams = self.streams

        def replay(e, name):
            eng = {"pe": nc.tensor, "act": nc.scalar, "dve": nc.vector, "pool": nc.gpsimd, "sp": nc.sync}[name]
            for waits, fn, isem, iv in streams[name]:
                for k, v in waits:
                    eng.wait_ge(sems[k], v)
                ins = fn(eng)
                ins.then_inc(sems[isem], iv)
            if name == "sp":
                for k, v in final_waits:
                    eng.wait_ge(sems[k], v)

        with nc.Block() as block:
            @block.sync
            def _(e):
                replay(e, "sp")

            @block.tensor
            def _(e):
                replay(e, "pe")

            @block.scalar
            def _(e):
                replay(e, "act")

            @block.vector
            def _(e):
                replay(e, "dve")

            @block.gpsimd
            def _(e):
                replay(e, "pool")
        self.es.close()

    def matmul(self, out, lhsT, rhs, start=True, stop=True, **kw):
        o, l, r = out.ap, lhsT.ap, rhs.ap
        return self.op("pe", lambda e: e.matmul(o, l, r, start=start, stop=stop, **kw),
                       reads=[lhsT, rhs], writes=[out])

    def transpose(self, out, in_, ident):
        o, i, d = out.ap, in_.ap, ident.ap
        return self.op("pe", lambda e: e.transpose(o, i, d), reads=[in_, ident], writes=[out])

    def act(self, out, in_, func, bias=None, scale=1.0, accum_out=None, eng="act"):
        o, i = out.ap, in_.ap
        reads = [in_]
        writes = [out]
        kw = {}
        if bias is not None:
            if isinstance(bias, V):
                reads.append(bias)
                kw["bias"] = bias.ap
            else:
                kw["bias"] = bias
        if isinstance(scale, V):
            reads.append(scale)
            kw["scale"] = scale.ap
        else:
            kw["scale"] = scale
        if accum_out is not None:
            writes.append(accum_out)
            kw["accum_out"] = accum_out.ap
        return self.op(eng, lambda e: e.activation(o, i, func, **kw), reads=reads, writes=writes)

    def tt(self, out, in0, in1, op, eng="dve"):
        o, a, b = out.ap, in0.ap, in1.ap
        return self.op(eng, lambda e: e.tensor_tensor(o, a, b, op), reads=[in0, in1], writes=[out])

    def ts(self, out, in0, s1, op0, s2=None, op1=None, accum_out=None, eng="dve"):
        o, a = out.ap, in0.ap
        reads = [in0]
        writes = [out]
        s1a = s1.ap if isinstance(s1, V) else s1
        s2a = s2.ap if isinstance(s2, V) else s2
        if isinstance(s1, V):
            reads.append(s1)
        if isinstance(s2, V):
            reads.append(s2)
        kw = {}
        if op1 is not None:
            kw["op1"] = op1
        if accum_out is not None:
            writes.append(accum_out)
            kw["accum_out"] = accum_out.ap
        return self.op(eng, lambda e: e.tensor_scalar(o, a, s1a, s2a, op0, **kw), reads=reads, writes=writes)

    def stt(self, out, in0, scalar, in1, op0, op1, eng="dve"):
        o, a, b = out.ap, in0.ap, in1.ap
        reads = [in0, in1]
        sa = scalar.ap if isinstance(scalar, V) else scalar
        if isinstance(scalar, V):
            reads.append(scalar)
        return self.op(eng, lambda e: e.scalar_tensor_tensor(o, a, sa, b, op0, op1), reads=reads, writes=[out])

    def copy(self, out, in_, eng="dve"):
        o, i = out.ap, in_.ap
        if eng == "act":
            return self.op("act", lambda e: e.copy(o, i), reads=[in_], writes=[out])
        return self.op(eng, lambda e: e.tensor_copy(o, i), reads=[in_], writes=[out])

    def memset(self, out, val, eng="dve"):
        o = out.ap
        return self.op(eng, lambda e: e.memset(o, val), reads=[], writes=[out])

    def reduce(self, out, in_, op=None, axis=None, eng="dve"):
        o, i = out.ap, in_.ap
        op = op or ALU.add
        axis = axis or AX.X
        return self.op(eng, lambda e: e.tensor_reduce(o, i, axis, op), reads=[in_], writes=[out])


class _Scope:
    def __init__(self, P):
        self.P = P

    def __enter__(self):
        self.saved = self.P.es
        self.P.es = ExitStack()
        return self

    def __exit__(self, *a):
        self.P.es.close()
        self.P.es = self.saved
        return False


def _recip(self, out, in_):
    o, i = out.ap, in_.ap
    return self.op("dve", lambda e: e.reciprocal(o, i), reads=[in_], writes=[out])


Prog.recip = _recip


Prog.recip = _recip

D = 1024
NT = 512
P_IN = 2816
D_FF = 2816
EPS = 1e-6


def new_nc():
    return bass.Bass("TRN2", target_bir_lowering=False)


class Rot:
    def __init__(self, tiles):
        self.tiles = tiles
        self.i = 0

    def next(self):
        t = self.tiles[self.i % len(self.tiles)]
        self.i += 1
        return t


def load_weight_bf(P, wd, K, N, dst, gain=None, stages=None, ncol=2816, q=("sp", "act", "pool"),
                   engs=("dve", "act", "dve", "act", "pool")):
    i = 0
    for kc in range(K // 128):
        for c0 in range(0, N, ncol):
            cw = min(ncol, N - c0)
            st = stages.next()
            P.dma(q[i % len(q)], st[:, 0:cw], wd[kc * 128:(kc + 1) * 128, c0:c0 + cw])
            eng = engs[i % len(engs)]
            if gain is not None:
                if eng == "act":
                    P.act(dst[:, kc, c0:c0 + cw], st[:, 0:cw], AF.Copy, scale=gain[:, kc:kc + 1])
                else:
                    P.ts(dst[:, kc, c0:c0 + cw], st[:, 0:cw], gain[:, kc:kc + 1], ALU.mult, eng=eng)
            else:
                P.copy(dst[:, kc, c0:c0 + cw], st[:, 0:cw], eng=eng)
            i += 1


def rms_rstd(P, chunks, rows, n, inv_d, eps, ones, sqrot, ps, rstd, tmp):
    nch = len(chunks)
    for i, ch in enumerate(chunks):
        sq = sqrot.next()
        P.act(sq[0:rows, 0:n], ch, AF.Square)
        P.matmul(ps[:, 0:n], ones[0:rows, :], sq[0:rows, 0:n], start=(i == 0), stop=(i == nch - 1))
    M = ones.ap.shape[-1] if hasattr(ones.ap, "shape") else 128
    P.act(tmp[:, 0:n], ps[:, 0:n], AF.Ln, bias=eps, scale=inv_d)
    P.act(rstd, tmp[:, 0:n], AF.Exp, scale=-0.5)


def build_A():
    nc = new_nc()
    P = Prog(nc)
    TOK = 4096
    hT = P.dram("hT", [D, TOK], F32, kind="ExternalInput")
    w_in = P.dram("w_in", [D, P_IN], F32, kind="ExternalInput")
    cA = P.dram("cA", [128, 8], F32, kind="ExternalInput")
    zT = P.dram("zT", [P_IN, TOK], F32, kind="ExternalOutput")
    consts = P.sb("consts", [128, 8])
    P.dma("sp", consts.v, cA.v)
    ones = P.sb("ones", [128, 128])
    P.memset(ones.v, 1.0)
    wbf = P.sb("wbf", [128, 8, P_IN], BF16)
    stages = Rot([P.sb("stg", [128, 2816]) for _ in range(2)])
    load_weight_bf(P, wd=w_in, K=D, N=P_IN, dst=wbf, gain=consts, stages=stages)
    hts = Rot([P.sb("ht", [128, 8, NT]) for _ in range(2)])
    sqrot = Rot([P.sb("sq", [128, NT]) for _ in range(2)])
    ps_ss = P.ps("ps_ss", [128, NT])
    ps_z = Rot([P.ps("ps_z", [128, NT]) for _ in range(4)])
    rstd = P.sb("rstd", [128, NT])
    tmp = P.sb("tmp", [128, NT])
    xn = Rot([P.sb("xn", [128, 8, NT], BF16) for _ in range(2)])
    zo = Rot([P.sb("zo", [128, NT]) for _ in range(4)])

    def norm(ti):
        t0 = ti * NT
        ht = hts.next()
        P.dma("sp", ht.v, hT[:, t0:t0 + NT].rearrange("(c p) n -> p c n", p=128))
        rms_rstd(P, [ht[:, c, :] for c in range(8)], 128, NT, 1.0 / D, EPS, ones.v, sqrot, ps_ss, rstd.v, tmp)
        x = xn.next()
        for c in range(8):
            P.tt(x[:, c, :], ht[:, c, :], rstd.v, ALU.mult, eng="dve" if c % 2 == 0 else "pool")
        return x

    ntl = TOK // NT
    x = norm(0)
    for ti in range(ntl):
        t0 = ti * NT
        xnext = None
        for oc in range(P_IN // 128):
            ps = ps_z.next()
            for kc in range(8):
                P.matmul(ps.v, wbf[:, kc, oc * 128:(oc + 1) * 128], x[:, kc, :], start=(kc == 0), stop=(kc == 7))
            if oc == 2 and ti + 1 < ntl:
                xnext = norm(ti + 1)
            o = zo.next()
            if oc % 2 == 0:
                P.copy(o.v, ps.v, eng="act")
            else:
                P.copy(o.v, ps.v, eng="dve")
            P.dma("sp" if oc % 2 == 0 else "pool", zT[oc * 128:(oc + 1) * 128, t0:t0 + NT], o.v)
        x = xnext
    P.emit()
    return nc


POOL_WINDOWS = (2, 4, 8, 16)


def build_Bp():
    nc = new_nc()
    P = Prog(nc)
    TOK = 4096
    H = 16
    zpT = P.dram("zpT", [256, H + TOK], F32, kind="ExternalInput")
    wblk = P.dram("wblk", [2, 128, 128], F32, kind="ExternalInput")
    cP = P.dram("cP", [128, 2 + 2 + 32], F32, kind="ExternalInput")
    ypT = P.dram("ypT", [256, TOK], F32, kind="ExternalOutput")
    cs = P.sb("cs", [128, 36])
    P.dma("sp", cs.v, cP.v)
    wb = [P.sb("wb", [128, 128]) for _ in range(2)]
    for i in range(2):
        P.dma("sp", wb[i].v, wblk[i])
    W = H + NT
    us = Rot([P.sb("u", [128, W]) for _ in range(2)])
    s2 = P.sb("s2", [128, W]); s4 = P.sb("s4", [128, W]); s8 = P.sb("s8", [128, W]); s16 = P.sb("s16", [128, W])
    dd = Rot([P.sb("dd", [128, W]) for _ in range(2)])
    win = P.sb("win", [128, W])
    ps = Rot([P.ps("ps", [128, NT]) for _ in range(2)])
    yo = Rot([P.sb("yo", [128, NT]) for _ in range(2)])
    for ti in range(TOK // NT):
        t0 = ti * NT
        for gp in range(2):
            u = us.next()
            P.dma("sp", u.v, zpT[gp * 128:(gp + 1) * 128, t0:t0 + W])
            P.tt(s2[:, 1:W], u[:, 1:W], u[:, 0:W - 1], ALU.add)
            P.tt(s4[:, 3:W], s2[:, 3:W], s2[:, 1:W - 2], ALU.add, eng="pool")
            if gp == 0:
                lo, hi = s2, s4
            else:
                P.tt(s8[:, 7:W], s4[:, 7:W], s4[:, 3:W - 4], ALU.add)
                P.tt(s16[:, 15:W], s8[:, 15:W], s8[:, 7:W - 8], ALU.add, eng="pool")
                lo, hi = s8, s16
            d = dd.next()
            for (a, b, src) in ((0, 64, lo), (64, 128, hi)):
                wsrc = src
                if ti == 0:
                    P.tt(win[a:b, H:H + 16], src[a:b, H:H + 16], cs[a:b, 4 + gp * 16:4 + gp * 16 + 16], ALU.mult)
                    P.stt(d[a:b, H:H + 16], win[a:b, H:H + 16], cs[a:b, 2 + gp:3 + gp], u[a:b, H:H + 16], ALU.mult, ALU.subtract)
                    P.stt(d[a:b, H + 16:W], src[a:b, H + 16:W], cs[a:b, 2 + gp:3 + gp], u[a:b, H + 16:W], ALU.mult, ALU.subtract)
                else:
                    P.stt(d[a:b, H:W], src[a:b, H:W], cs[a:b, 2 + gp:3 + gp], u[a:b, H:W], ALU.mult, ALU.subtract)
            p = ps.next()
            P.matmul(p.v, wb[gp].v, d[:, H:W])
            y = yo.next()
            P.ts(y.v, p.v, cs[:, gp:gp + 1], ALU.mult)
            P.dma("pool", ypT[gp * 128:(gp + 1) * 128, t0:t0 + NT], y.v)
    P.emit()
    return nc


def prep_Bp(z_pool_b, hh, pool_w, pool_scale):
    TOK = 4096
    zp = np.zeros((256, 16 + TOK), np.float32)
    if hh == 0:
        zp[:, 16:] = z_pool_b[0:TOK].T
    else:
        zp[:, :] = z_pool_b[TOK - 16:2 * TOK].T
    wblk = np.zeros((2, 128, 128), np.float32)
    for g in range(4):
        i, o = g // 2, (g % 2) * 64
        wblk[i, o:o + 64, o:o + 64] = pool_w[g]
    cP = np.zeros((128, 36), np.float32)
    for gp in range(2):
        cP[:, gp] = pool_scale[gp * 128:(gp + 1) * 128]
        for half in range(2):
            w = POOL_WINDOWS[gp * 2 + half]
            cP[half * 64:(half + 1) * 64, 2 + gp] = 1.0 / w
            t = np.arange(16)
            corr = (w / np.minimum(t + 1, w)) if hh == 0 else np.ones(16)
            cP[half * 64:(half + 1) * 64, 4 + gp * 16:4 + gp * 16 + 16] = corr[None, :]
    return {"zpT": zp, "wblk": wblk, "cP": cP}


SEQ = 8192
ALIBI = [2.0 ** (-8.0 * (h + 1) / 4) for h in range(4)]
NEG = -30000.0
ALIBI_WIN = 12


def diff_heads(hh):
    return [hh, 3 - hh]


def build_Bd():
    nc = new_nc()
    P = Prog(nc)
    S = SEQ
    NKB = S // 128
    qin = P.dram("qin", [2, 2, 64, S], F32, kind="ExternalInput")
    kin = P.dram("kin", [2, 2, 64, S], F32, kind="ExternalInput")
    vin = P.dram("vin", [2, S, 128], F32, kind="ExternalInput")
    qrow = P.dram("qrow", [2, 1, S], BF16, kind="ExternalInput")
    btab = P.dram("btab", [2, 128, 64], F32, kind="ExternalInput")
    dtab = P.dram("dtab", [2, 128, 128], BF16, kind="ExternalInput")
    identb = P.dram("identb", [128, 128], BF16, kind="ExternalInput")
    cD = P.dram("cD", [128, 8], F32, kind="ExternalInput")
    lvec = P.dram("lvec", [128, 4, 64], F32, kind="ExternalInput")
    subln = P.dram("subln", [128, 128], F32, kind="ExternalInput")
    yd = P.dram("yd", [S, 256], F32, kind="ExternalOutput")

    cs = P.sb("cs", [128, 8]); P.dma("sp", cs.v, cD.v)
    lv = P.sb("lv", [128, 4, 64]); P.dma("sp", lv.v, lvec.v)
    sl = P.sb("sl", [128, 128]); P.dma("sp", sl.v, subln.v)
    idb = P.sb("idb", [128, 128], BF16); P.dma("sp", idb.v, identb.v)
    ones = P.sb("ones", [128, 128]); P.memset(ones.v, 1.0)
    lt = P.sb("lt", [128, 64]); e1 = P.sb("e1", [128, 1]); e2 = P.sb("e2", [128, 1]); nlam = P.sb("nlam", [128, 1])
    P.tt(lt.v, lv[:, 0, :], lv[:, 1, :], ALU.mult); P.reduce(e1.v, lt.v); P.act(e1.v, e1.v, AF.Exp)
    P.tt(lt.v, lv[:, 2, :], lv[:, 3, :], ALU.mult); P.reduce(e2.v, lt.v); P.act(e2.v, e2.v, AF.Exp)
    P.tt(nlam.v, e2.v, e1.v, ALU.subtract)
    P.tt(nlam.v, nlam.v, cs[:, 4:5], ALU.subtract)
    sls = P.sb("sls", [128, 128])
    P.ts(sls.v, sl.v, cs[:, 5:6], ALU.mult)

    QT = [P.sb("QT", [65, S], BF16) for _ in range(2)]
    KT = [P.sb("KT", [65, S], BF16) for _ in range(2)]
    VA = P.sb("VA", [128, NKB, 129], BF16)
    BT = P.sb("BT", [128, 64]); DT = P.sb("DT", [128, 128], BF16)
    qst = Rot([P.sb("qst", [64, NT]) for _ in range(8)])
    sqr = Rot([P.sb("sqr", [64, NT]) for _ in range(4)])
    tmpr = Rot([P.sb("tmpr", [64, NT]) for _ in range(4)]); rstdr = Rot([P.sb("rstd", [64, NT]) for _ in range(4)])
    vst = Rot([P.sb("vst", [128, 16, 128]) for _ in range(2)])
    ps_pre = P.ps("ps_pre", [64, NT])
    ps_s = Rot([P.ps("ps_s", [128, 2, NT]) for _ in range(2)])
    accs = [P.ps("acc", [128, 3, 129]) for _ in range(3)]
    pT = Rot([P.sb("pT", [128, 2, NT], BF16) for _ in range(3)])
    accsb = [P.sb("accsb", [128, 3, 129]) for _ in range(3)]
    fin = {k: Rot([P.sb(k, s) for _ in range(2)]) for k, s in
           (("r0", [128, 1]), ("r1", [128, 1]), ("o0", [128, 128]), ("o", [128, 128]), ("sqo", [128, 128]),
            ("ss", [128, 1]), ("out", [128, 128]))}

    def acc_slot(c, j):
        idx = c * 4 + j
        return idx // 3, idx % 3

    for i in range(2):
        P.dma("sp", BT.v, btab[i]); P.dma("sp", DT.v, dtab[i])
        its = []
        for c in range(2):
            P.dma("pool", QT[c][64:65, :], qrow[i])
            P.memset(KT[c][64:65, :], 1.0)
            for (src, dst, gcol) in ((qin, QT[c], c), (kin, KT[c], 2 + c)):
                for ti in range(S // NT):
                    its.append((src, dst, gcol, c, ti * NT))
        for g0 in range(0, len(its), 4):
            grp = its[g0:g0 + 4]
            sts = []
            for k_, (src, dst, gcol, c, t0) in enumerate(grp):
                st = qst.next()
                P.dma("sp" if k_ % 2 == 0 else "act", st.v, src[i, c, :, t0:t0 + NT])
                sts.append(st)
            sqs = []
            for k_, st in enumerate(sts):
                sq = sqr.next()
                P.tt(sq.v, st.v, st.v, ALU.mult, eng="pool" if k_ % 2 else "dve")
                sqs.append(sq)
            pss = []
            for k_, sq in enumerate(sqs):
                ps = ps_s.next() if k_ % 2 == 0 else pss[-1][0]
                view = ps[0:64, k_ % 2, :]
                P.matmul(view, ones[0:64, 0:64], sq.v)
                pss.append((ps, view))
            tms = []
            for (ps, view) in pss:
                tm_ = tmpr.next()
                P.act(tm_.v, view, AF.Ln, bias=EPS, scale=1.0 / 64)
                tms.append(tm_)
            rss = []
            for tm_ in tms:
                r_ = rstdr.next()
                P.act(r_.v, tm_.v, AF.Exp, scale=-0.5)
                rss.append(r_)
            for (src, dst, gcol, c, t0), st, r_ in zip(grp, sts, rss):
                P.stt(dst[0:64, t0:t0 + NT], st.v, cs[0:64, gcol:gcol + 1], r_.v, ALU.mult, ALU.mult)
        P.memset(VA[:, :, 128:129], 1.0)
        for vc in range(4):
            st = vst.next()
            P.dma("pool", st.v, vin[i, vc * 2048:(vc + 1) * 2048, :].rearrange("(kb p) d -> p kb d", p=128))
            P.copy(VA[:, vc * 16:(vc + 1) * 16, 0:128], st.v, eng="pool")
        for QB in range(S // NT):
            q0 = QB * NT
            tiles = []
            kb_lo = max(0, 4 * QB - ALIBI_WIN) if i == 0 else 0
            for kb in range(kb_lo, 4 * QB + 4):
                i2 = max(0, kb - 4 * QB)
                tiles.append((kb, i2))
            started = set()
            prev = None

            def emit_pv(info):
                kb, i2, pt = info
                for c in range(2):
                    for j in range(i2, 4):
                        bank, slot = acc_slot(c, j)
                        st_flag = bank not in started
                        started.add(bank)
                        P.matmul(accs[bank][:, slot, :], pt[:, c, (j - i2) * 128:(j - i2 + 1) * 128], VA[:, kb, :],
                                 start=st_flag, stop=False, skip_group_check=True)

            for (kb, i2) in tiles:
                N = (4 - i2) * 128
                diag = kb >= 4 * QB
                m = 4 * QB - kb + 3
                ps = ps_s.next()
                for c in range(2):
                    P.matmul(ps[:, c, 0:N], KT[c][0:65, kb * 128:(kb + 1) * 128], QT[c][0:65, q0 + i2 * 128:q0 + NT],
                             start=True, stop=not diag, skip_group_check=True)
                    if diag:
                        P.matmul(ps[:, c, 0:128], idb.v, DT.v, start=False, stop=True, skip_group_check=True)
                pt = pT.next()
                P.act(pt[:, :, 0:N], ps[:, :, 0:N], AF.Exp, bias=BT[:, m:m + 1], scale=0.125)
                if prev is not None:
                    emit_pv(prev)
                prev = (kb, i2, pt)
            emit_pv(prev)
            for b3 in range(3):
                P.copy(accsb[b3].v, accs[b3].v, eng="dve" if b3 != 1 else "act")
            for j in range(4):
                b0, s0 = acc_slot(0, j); b1, s1 = acc_slot(1, j)
                r0 = fin["r0"].next(); r1 = fin["r1"].next(); o0 = fin["o0"].next(); o = fin["o"].next()
                sqo = fin["sqo"].next(); ss = fin["ss"].next(); out = fin["out"].next()
                P.recip(r0.v, accsb[b0][:, s0, 128:129])
                P.recip(r1.v, accsb[b1][:, s1, 128:129])
                P.tt(r1.v, r1.v, nlam.v, ALU.mult)
                P.ts(o0.v, accsb[b0][:, s0, 0:128], r0.v, ALU.mult)
                P.stt(o.v, accsb[b1][:, s1, 0:128], r1.v, o0.v, ALU.mult, ALU.add)
                P.act(sqo.v, o.v, AF.Square, accum_out=ss.v)
                P.act(ss.v, ss.v, AF.Ln, bias=EPS, scale=1.0 / 128)
                P.act(ss.v, ss.v, AF.Exp, scale=-0.5)
                P.stt(out.v, o.v, ss.v, sls.v, ALU.mult, ALU.mult)
                P.dma("pool", yd[q0 + j * 128:q0 + (j + 1) * 128, i * 128:(i + 1) * 128], out.v)
    P.emit()
    return nc


def prep_Bd(z_b, hh, l, prm):
    import ml_dtypes
    S = SEQ
    base = 256 + 1024
    q = z_b[:, base:base + 512].reshape(S, 4, 2, 64)
    k = z_b[:, base + 512:base + 1024].reshape(S, 4, 2, 64)
    v = z_b[:, base + 1024:base + 1536].reshape(S, 4, 128)
    hs = diff_heads(hh)
    qin = np.ascontiguousarray(q[:, hs].transpose(1, 2, 3, 0))
    kin = np.ascontiguousarray(k[:, hs].transpose(1, 2, 3, 0))
    vin = np.ascontiguousarray(v[:, hs].transpose(1, 0, 2))
    li = 0.8 - 0.6 * math.exp(-0.3 * l)
    qrow = np.zeros((2, 1, S), np.float32)
    btab = np.zeros((2, 128, 64), np.float32)
    dtab = np.zeros((2, 128, 128), np.float32)
    ki = np.arange(128)[:, None]
    qi = np.arange(128)[None, :]
    for i, h in enumerate(hs):
        sl = ALIBI[h]
        j = (np.arange(S) // 128) % 4
        qrow[i, 0] = -sl * 128.0 * j * 8.0
        mm = np.arange(64)[None, :]
        btab[i] = sl * (ki - 128.0 * (mm - 3))
        vis = (ki // 64) <= (qi // 64)
        dpr = np.where(qi >= ki, 0.0, 2.0 * sl * (qi - ki))
        dtab[i] = np.where(vis, dpr * 8.0, NEG * 8.0)
    cD = np.zeros((128, 8), np.float32)
    cD[0:64, 0:2] = prm["diff_q_norm"].T
    cD[0:64, 2:4] = prm["diff_k_norm"].T
    cD[:, 4] = li
    cD[:, 5] = 1.0 - li
    lvec = np.stack([np.broadcast_to(prm[n], (128, 64)) for n in ("diff_lq1", "diff_lk1", "diff_lq2", "diff_lk2")], 1)
    return {"qin": qin, "kin": kin, "vin": vin, "qrow": qrow.astype(ml_dtypes.bfloat16), "btab": btab,
            "dtab": dtab.astype(ml_dtypes.bfloat16), "identb": np.eye(128, dtype=np.float32).astype(ml_dtypes.bfloat16),
            "cD": cD, "lvec": np.ascontiguousarray(lvec.astype(np.float32)),
            "subln": np.ascontiguousarray(np.broadcast_to(prm["diff_subln"], (128, 128)).astype(np.float32))}


RWKV_GN_EPS = 64e-5


def build_Br():
    nc = new_nc()
    P = Prog(nc)
    S = SEQ
    SC = 512
    NCH = SC // 128
    rkv = P.dram("rkv", [2, 3, 64, 1 + S], F32, kind="ExternalInput")
    lw = P.dram("lw", [64, 1 + S], F32, kind="ExternalInput")
    la = P.dram("la", [64, 1 + S], F32, kind="ExternalInput")
    lg = P.dram("lg", [128, 1 + S], F32, kind="ExternalInput")
    cR = P.dram("cR", [128, 32], F32, kind="ExternalInput")
    w2 = P.dram("w2", [2, 64, 64], F32, kind="ExternalInput")
    a2 = P.dram("a2", [2, 64, 64], F32, kind="ExternalInput")
    g2 = P.dram("g2", [2, 128, 64], F32, kind="ExternalInput")
    msk = P.dram("msk", [128, 384], F32, kind="ExternalInput")
    idn = P.dram("idn", [128, 128], F32, kind="ExternalInput")
    rmask = P.dram("rmask", [64, SC], F32, kind="ExternalInput")
    yr = P.dram("yr", [128, S], F32, kind="ExternalOutput")

    cs = P.sb("cs", [128, 32]); P.dma("sp", cs.v, cR.v)
    mk = P.sb("mk", [128, 384]); P.dma("sp", mk.v, msk.v)
    ident = P.sb("ident", [128, 128]); P.dma("sp", ident.v, idn.v)
    rm = P.sb("rm", [64, SC]); P.dma("sp", rm.v, rmask.v)
    w2s = [P.sb("w2s", [64, 64]) for _ in range(2)]
    a2s = [P.sb("a2s", [64, 64]) for _ in range(2)]
    g2s = [P.sb("g2s", [128, 64]) for _ in range(2)]
    for i in range(2):
        P.dma("sp", w2s[i].v, w2[i]); P.dma("sp", a2s[i].v, a2[i]); P.dma("sp", g2s[i].v, g2[i])
    ones = P.sb("ones", [64, 64]); P.memset(ones.v, 1.0)
    o64 = P.sb("o64", [64, 64]); P.memset(o64.v, 1.0 / 64)
    omka = P.sb("omka", [64, 2])
    for i in range(2):
        b = 4 + 12 * i
        P.ts(omka[:, i:i + 1], cs[0:64, b + 6:b + 7], -1.0, ALU.mult, 1.0, ALU.add)

    PS = Rot([P.ps("ps", [128, 512]) for _ in range(8)])

    def R(name, shape, n=2, dt=F32):
        return Rot([P.sb(name, shape, dt) for _ in range(n)])

    A = {}
    for nm in ("zr", "zk", "zv"):
        A[nm] = R(nm, [64, SC + 1], 2)
    A["zw"] = R("zw", [64, SC + 1], 2); A["za"] = R("za", [64, SC + 1], 2); A["zg"] = R("zg", [128, SC + 1], 2)
    for nm in ("d64", "rm_", "km_", "vm_", "tw", "adm", "ld", "asig", "g", "kk", "sq", "t1", "t2", "kkn", "k2", "av",
               "bv", "Lc", "eL", "eLp", "enL", "ex", "YT", "dlt", "yn", "bon", "yo"):
        A[nm] = R(nm, [64, SC], 1 if nm in ("d64", "km_", "ld", "asig", "kk", "kkn", "av", "bv", "Lc", "eL", "eLp", "enL", "ex") else 2)
    A["d128"] = R("d128", [128, SC], 2); A["sg"] = R("sg", [128, SC], 2); A["gdm"] = R("gdm", [128, SC], 2)
    A["AR"] = R("AR", [64, NCH, 2, 128], 2); A["BK"] = R("BK", [64, NCH, 2, 128], 2); A["BKH"] = R("BKH", [64, 2, SC], 2)
    A["eLC"] = R("eLC", [64, NCH], 2)
    TM = R("TM", [128, 4, 64], 8); SB1 = R("SB1", [128, 256], 8); SB2 = R("SB2", [128, 256], 8)
    PX = R("PX", [128, 256], 8); UX = R("UX", [128, 256], 8); PTt = R("PTt", [128, 128], 8); MC = R("MC", [64, 64], 8); QH = R("QH", [64, 128], 8)
    ST = [[P.sb("ST", [64, 64]) for _ in range(2)] for _ in range(2)]
    sti = [0, 0]
    for i in range(2):
        P.memset(ST[i][0].v, 0.0)
    flip = [0]

    def ev():
        flip[0] += 1
        return "dve" if flip[0] % 2 else "act"

    def shift_mix(zt, rows, mucol, dtmp, out):
        P.tt(dtmp[0:rows, :], zt[0:rows, 0:SC], zt[0:rows, 1:SC + 1], ALU.subtract)
        P.stt(out[0:rows, :], dtmp[0:rows, :], cs[0:rows, mucol:mucol + 1], zt[0:rows, 1:SC + 1], ALU.mult, ALU.add)

    for sc in range(S // SC):
        t0 = sc * SC
        zw = A["zw"].next(); za = A["za"].next(); zg = A["zg"].next()
        P.dma("sp", zw.v, lw[:, t0:t0 + SC + 1]); P.dma("sp", za.v, la[:, t0:t0 + SC + 1]); P.dma("sp", zg.v, lg[:, t0:t0 + SC + 1])
        d64 = A["d64"].next(); d128 = A["d128"].next()
        tw = A["tw"].next(); adm = A["adm"].next(); sg = A["sg"].next(); gdm = A["gdm"].next()
        shift_mix(zw, 64, 0, d64, tw); P.act(tw.v, tw.v, AF.Tanh)
        shift_mix(za, 64, 1, d64, adm)
        shift_mix(zg, 128, 2, d128, gdm); P.act(sg.v, gdm.v, AF.Sigmoid)
        H = []
        for i in range(2):
            b = 4 + 12 * i
            zr = A["zr"].next(); zk = A["zk"].next(); zv = A["zv"].next()
            P.dma("pool", zr.v, rkv[i, 0, :, t0:t0 + SC + 1]); P.dma("pool", zk.v, rkv[i, 1, :, t0:t0 + SC + 1])
            P.dma("pool", zv.v, rkv[i, 2, :, t0:t0 + SC + 1])
            rm_ = A["rm_"].next(); km_ = A["km_"].next(); vm_ = A["vm_"].next()
            shift_mix(zr, 64, b + 0, d64, rm_); shift_mix(zk, 64, b + 1, d64, km_); shift_mix(zv, 64, b + 2, d64, vm_)
            ps = PS.next(); ld = A["ld"].next()
            P.matmul(ps[0:64, :], w2s[i].v, tw.v)
            P.act(ld.v, ps[0:64, :], AF.Sigmoid, bias=cs[0:64, b + 3:b + 4])
            P.ts(ld.v, ld.v, -math.exp(-0.5), ALU.mult)
            ps = PS.next(); asig = A["asig"].next()
            P.matmul(ps[0:64, :], a2s[i].v, adm.v)
            P.act(asig.v, ps[0:64, :], AF.Sigmoid, bias=cs[0:64, b + 4:b + 5])
            ps = PS.next(); g = A["g"].next()
            P.matmul(ps[0:64, :], g2s[i].v, sg.v)
            P.copy(g.v, ps[0:64, :], eng="act")
            kk = A["kk"].next(); sq = A["sq"].next(); t1 = A["t1"].next(); kkn = A["kkn"].next()
            P.ts(kk.v, km_.v, cs[0:64, b + 5:b + 6], ALU.mult)
            P.tt(sq.v, kk.v, kk.v, ALU.mult, eng="pool")
            ps = PS.next()
            P.matmul(ps[0:64, :], ones.v, sq.v)
            P.ts(t1.v, ps[0:64, :], 1e-24, ALU.max)
            P.act(t1.v, t1.v, AF.Ln); P.act(t1.v, t1.v, AF.Exp, scale=-0.5)
            P.tt(kkn.v, kk.v, t1.v, ALU.mult)
            t2 = A["t2"].next(); k2 = A["k2"].next()
            P.ts(t2.v, asig.v, cs[0:64, b + 6:b + 7], ALU.mult, omka[:, i:i + 1], ALU.add)
            P.tt(k2.v, km_.v, t2.v, ALU.mult, eng="pool")
            av = A["av"].next(); bv = A["bv"].next()
            P.ts(av.v, kkn.v, -1.0, ALU.mult, eng="pool")
            P.tt(bv.v, kkn.v, asig.v, ALU.mult, eng="pool")
            Lc = A["Lc"].next(); eL = A["eL"].next(); eLp = A["eLp"].next(); enL = A["enL"].next()
            lo, m_, x_ = Lc.h[:], rm.h[:], ld.h[:]
            P.op("dve", lambda e, lo=lo, m_=m_, x_=x_: e.tensor_tensor_scan(lo, m_, x_, 0.0, ALU.mult, ALU.add),
                 reads=[rm, ld], writes=[Lc])
            P.act(eL.v, Lc.v, AF.Exp)
            P.act(enL.v, Lc.v, AF.Exp, scale=-1.0)
            P.tt(eLp.v, Lc.v, ld.v, ALU.subtract)
            P.act(eLp.v, eLp.v, AF.Exp)
            AR = A["AR"].next(); BK = A["BK"].next(); BKH = A["BKH"].next(); eLC = A["eLC"].next()
            v3 = lambda t: t.v.rearrange("p (c t) -> p c t", t=128)
            P.tt(AR[:, :, 0, :], v3(av), v3(eLp), ALU.mult)
            P.tt(AR[:, :, 1, :], v3(rm_), v3(eL), ALU.mult, eng="pool")
            P.tt(BK[:, :, 0, :], v3(bv), v3(enL), ALU.mult)
            P.tt(BK[:, :, 1, :], v3(k2), v3(enL), ALU.mult, eng="pool")
            P.copy(eLC.v, v3(eL)[:, :, 127])
            ex = A["ex"].next()
            for c in range(NCH):
                P.act(ex[:, c * 128:(c + 1) * 128], Lc[:, c * 128:(c + 1) * 128], AF.Exp,
                      bias=Lc[:, c * 128 + 127:c * 128 + 128], scale=-1.0)
            P.tt(BKH[:, 0, :], bv.v, ex.v, ALU.mult)
            P.tt(BKH[:, 1, :], k2.v, ex.v, ALU.mult, eng="pool")
            YT = A["YT"].next()
            H.append(dict(i=i, b=b, AR=AR, BK=BK, BKH=BKH, eLC=eLC, vm_=vm_, rm_=rm_, k2=k2, g=g, YT=YT))
        for hd in H:
            i = hd["i"]; AR = hd["AR"]; BK = hd["BK"]; BKH = hd["BKH"]; vm_ = hd["vm_"]
            CH = []
            for c in range(NCH):
                cs_ = slice(c * 128, (c + 1) * 128)
                ch = dict(c=c, cs_=cs_, ARc=AR[:, c].rearrange("p a b -> p (a b)"), tm=TM.next(), sb1=SB1.next(),
                          sb2=SB2.next(), px=PX.next(), pt=PTt.next(), mc=MC.next(), qh=QH.next())
                CH.append(ch)
            for ch in CH:
                c = ch["c"]; cs_ = ch["cs_"]; ps = PS.next()
                P.transpose(ps[:, 0:64], AR[:, c, 0, :], ident[0:64, 0:64])
                P.transpose(ps[:, 64:128], vm_[:, cs_], ident[0:64, 0:64])
                P.transpose(ps[:, 128:192], BKH[:, 0, cs_], ident[0:64, 0:64])
                P.transpose(ps[:, 192:256], BKH[:, 1, cs_], ident[0:64, 0:64])
                P.copy(ch["tm"].v.rearrange("p a b -> p (a b)"), ps[:, 0:256], eng=ev())
            for ch in CH:
                c = ch["c"]; ps = PS.next()
                P.matmul(ps[:, 0:256], BK[:, c, 0, :], ch["ARc"])
                P.tt(ch["sb1"].v, ps[:, 0:256], mk[:, 0:256], ALU.mult)
            for ch in CH:
                c = ch["c"]; ps = PS.next()
                P.matmul(ps[:, 0:256], BK[:, c, 1, :], ch["ARc"])
                P.tt(ch["sb2"].v, ps[:, 0:256], mk[:, 0:256], ALU.mult)
            for ch in CH:
                c = ch["c"]; ps = PS.next()
                P.matmul(ps[:, 0:128], AR[:, c, 0, :], BK[:, c, 0, :])
                P.tt(ch["px"][:, 0:128], ps[:, 0:128], mk[:, 256:384], ALU.mult)
            for ch in CH:
                ps = PS.next()
                P.matmul(ps[:, 0:64], ch["sb2"][:, 0:128], ch["tm"][:, 1, :])
                P.copy(ch["px"][:, 192:256], ps[:, 0:64], eng=ev())
                P.copy(ch["px"][:, 128:192], ch["tm"][:, 0, :], eng="pool")
                P.copy(ch["pt"].v, ch["sb1"][:, 0:128], eng="pool")
            for j in range(7):
                for ch in CH:
                    px = ch["px"]; pt = ch["pt"]
                    ps = PS.next(); px2 = PX.next() if j < 6 else UX.next()
                    if j < 5:
                        P.matmul(ps[:, 0:256], pt.v, px.v)
                        P.copy(px2[:, 0:128], ps[:, 0:128], eng="act")
                    else:
                        P.matmul(ps[:, 128:256], pt.v, px[:, 128:256])
                    P.tt(px2[:, 128:256], ps[:, 128:256], px[:, 128:256], ALU.add)
                    ch["px2"] = px2
                for ch in CH:
                    if j < 6:
                        ps = PS.next(); pt2 = PTt.next()
                        P.matmul(ps[:, 0:128], ch["px"][:, 0:128], ch["pt"].v)
                        P.copy(pt2.v, ps[:, 0:128], eng=ev())
                        ch["pt"] = pt2
                    ch["px"] = ch["px2"]
            for ch in CH:
                c = ch["c"]
                Ua = ch["px"][:, 128:192]
                ps = PS.next()
                P.matmul(ps[0:64, 0:64], Ua, ch["tm"][:, 2, :])
                P.copy(ch["mc"].v, ps[0:64, 0:64], eng=ev())
                ps = PS.next()
                P.matmul(ps[0:64, 0:128], Ua, ch["sb1"][:, 128:256])
                P.tt(ch["qh"].v, ps[0:64, 0:128], AR[:, c, 1, :], ALU.add)
            hd["CH"] = CH
        for c in range(NCH):
            for hd in H:
                i = hd["i"]; ch = hd["CH"][c]; tm = ch["tm"]; sb1 = ch["sb1"]; sb2 = ch["sb2"]
                Uv = ch["px"][:, 192:256]
                st_old = ST[i][sti[i] % 2]; st_new = ST[i][(sti[i] + 1) % 2]; sti[i] += 1
                ps = PS.next()
                P.matmul(ps[0:64, 0:128], st_old.v, ch["qh"].v, start=True, stop=False)
                P.matmul(ps[0:64, 0:128], Uv, sb1[:, 128:256], start=False, stop=False)
                P.matmul(ps[0:64, 0:128], tm[:, 1, :], sb2[:, 128:256], start=False, stop=True)
                P.copy(hd["YT"][:, ch["cs_"]], ps[0:64, 0:128], eng="act")
                ps = PS.next()
                P.matmul(ps[0:64, 0:64], ch["mc"].v, st_old.v, start=True, stop=False)
                P.matmul(ps[0:64, 0:64], tm[:, 2, :], Uv, start=False, stop=False)
                P.matmul(ps[0:64, 0:64], tm[:, 3, :], tm[:, 1, :], start=False, stop=True)
                P.stt(st_new.v, st_old.v, hd["eLC"][:, c:c + 1], ps[0:64, 0:64], ALU.mult, ALU.add)
        for hd in H:
            i = hd["i"]; b = hd["b"]; YT = hd["YT"]; rm_ = hd["rm_"]; k2 = hd["k2"]; vm_ = hd["vm_"]; g = hd["g"]
            dlt = A["dlt"].next(); yn = A["yn"].next(); bon = A["bon"].next(); yo = A["yo"].next()
            ps = PS.next()
            P.matmul(ps[0:64, :], o64.v, YT.v)
            P.tt(dlt.v, YT.v, ps[0:64, :], ALU.subtract)
            sq2 = A["sq"].next()
            P.tt(sq2.v, dlt.v, dlt.v, ALU.mult, eng="pool")
            ps = PS.next()
            P.matmul(ps[0:64, :], o64.v, sq2.v)
            t3 = A["t1"].next()
            P.act(t3.v, ps[0:64, :], AF.Ln, bias=RWKV_GN_EPS)
            P.act(t3.v, t3.v, AF.Exp, scale=-0.5)
            P.tt(yn.v, dlt.v, t3.v, ALU.mult)
            P.ts(yn.v, yn.v, cs[0:64, b + 9:b + 10], ALU.mult, cs[0:64, b + 10:b + 11], ALU.add)
            t4 = A["t2"].next()
            P.stt(t4.v, rm_.v, cs[0:64, b + 8:b + 9], k2.v, ALU.mult, ALU.mult)
            ps = PS.next()
            P.matmul(ps[0:64, :], ones.v, t4.v)
            P.tt(bon.v, ps[0:64, :], vm_.v, ALU.mult)
            P.tt(yn.v, yn.v, bon.v, ALU.add, eng="pool")
            P.tt(yo.v, yn.v, g.v, ALU.mult, eng="pool")
            P.dma("sp", yr[i * 64:(i + 1) * 64, t0:t0 + SC], yo.v)
    P.emit()
    return nc


def prep_Br(z_b, hh, prm):
    S = SEQ
    zr = z_b[:, 256:1280]
    hs = [2 * hh, 2 * hh + 1]

    def fm(cols):
        out = np.zeros((cols.shape[1], 1 + S), np.float32)
        out[:, 1:] = cols.T
        return out
    rkv = np.stack([np.stack([fm(zr[:, o + h * 64:o + (h + 1) * 64]) for o in (0, 256, 512)]) for h in hs])
    mu = prm["rwkv_mu"]
    cR = np.zeros((128, 32), np.float32)
    cR[0:64, 0] = mu[768:832]; cR[0:64, 1] = mu[832:896]; cR[:, 2] = mu[896:1024]
    for i, h in enumerate(hs):
        b = 4 + 12 * i
        s = slice(h * 64, (h + 1) * 64)
        cR[0:64, b + 0] = mu[0:256][s]; cR[0:64, b + 1] = mu[256:512][s]; cR[0:64, b + 2] = mu[512:768][s]
        cR[0:64, b + 3] = prm["rwkv_w0"][s]; cR[0:64, b + 4] = prm["rwkv_a0"][s]
        cR[0:64, b + 5] = prm["rwkv_k_k"][s]; cR[0:64, b + 6] = prm["rwkv_k_a"][s]
        cR[0:64, b + 8] = prm["rwkv_r_k"][h]; cR[0:64, b + 9] = prm["rwkv_ln_w"][s]; cR[0:64, b + 10] = prm["rwkv_ln_b"][s]
    t = np.arange(128)
    su = (t[None, :] > t[:, None]).astype(np.float32)
    iu = (t[None, :] >= t[:, None]).astype(np.float32)
    slm = (t[:, None] > t[None, :]).astype(np.float32)
    rmask = np.ones((64, 512), np.float32); rmask[:, ::128] = 0.0
    return {"rkv": np.ascontiguousarray(rkv), "lw": fm(zr[:, 768:832]), "la": fm(zr[:, 832:896]), "lg": fm(zr[:, 896:1024]),
            "cR": cR,
            "w2": np.ascontiguousarray(np.stack([prm["rwkv_w2"][:, h * 64:(h + 1) * 64] for h in hs])),
            "a2": np.ascontiguousarray(np.stack([prm["rwkv_a2"][:, h * 64:(h + 1) * 64] for h in hs])),
            "g2": np.ascontiguousarray(np.stack([prm["rwkv_g2"][:, h * 64:(h + 1) * 64] for h in hs])),
            "msk": np.concatenate([su, iu, slm], 1), "idn": np.eye(128, dtype=np.float32), "rmask": rmask}


def build_C1():
    nc = new_nc()
    P = Prog(nc)
    TOK = 4096
    n = 256
    M = 256
    hT = P.dram("hT", [D, TOK], F32, kind="ExternalInput")
    yT = P.dram("yT", [D, TOK], F32, kind="ExternalInput")
    memT = P.dram("memT", [D, M], F32, kind="ExternalInput")
    wd = {k: P.dram(k, [D, D], F32, kind="ExternalInput") for k in ("w_out", "wq", "wk", "wv", "wo")}
    cC = P.dram("cC", [128, 20], F32, kind="ExternalInput")
    h2T = P.dram("h2T", [D, TOK], F32, kind="ExternalOutput")
    cs = P.sb("cs", [128, 20]); P.dma("sp", cs.v, cC.v)
    ones = P.sb("ones", [128, 128]); P.memset(ones.v, 1.0)
    onesb = P.sb("onesb", [128, 128], BF16); P.memset(onesb.v, 1.0)
    stages = Rot([P.sb("stg", [128, 1024]) for _ in range(2)])
    bufA = P.sb("bufA", [128, 8, D], BF16); bufB = P.sb("bufB", [128, 8, D], BF16); bufC = P.sb("bufC", [128, 8, D], BF16)
    PS = Rot([P.ps("ps", [128, 512]) for _ in range(7)])
    ps_ss = P.ps("ps_ss", [128, 512])
    sqrot = Rot([P.sb("sq", [128, n]) for _ in range(2)])
    rstd = P.sb("rstd", [128, n]); tmp = P.sb("tmp", [128, n])
    flip = [0]

    def ev():
        flip[0] += 1
        return "dve" if flip[0] % 2 else "act"

    load_weight_bf(P, wd["wk"], D, D, bufA, gain=cs[:, 8:16], stages=stages, ncol=1024)
    load_weight_bf(P, wd["wv"], D, D, bufB, gain=cs[:, 8:16], stages=stages, ncol=1024)
    load_weight_bf(P, wd["wq"], D, D, bufC, gain=cs[:, 0:8], stages=stages, ncol=1024)
    mt = P.sb("mt", [128, 8, M]); P.dma("sp", mt.v, memT.v.rearrange("(c p) m -> p c m", p=128))
    memn = P.sb("memn", [128, 8, M], BF16)
    rms_rstd(P, [mt[:, c, :] for c in range(8)], 128, M, 1.0 / D, EPS, ones.v, sqrot, ps_ss, rstd[:, 0:M], tmp)
    for c in range(8):
        P.tt(memn[:, c, :], mt[:, c, :], rstd[:, 0:M], ALU.mult, eng="dve" if c % 2 else "pool")
    kraw = P.sb("kraw", [128, 8, M]); kn = P.sb("kn", [128, 8, M], BF16); vb = P.sb("vb", [128, 2, D], BF16)
    for oc in range(8):
        ps = PS.next()
        for kc in range(8):
            P.matmul(ps[:, 0:M], bufA[:, kc, oc * 128:(oc + 1) * 128], memn[:, kc, :], start=(kc == 0), stop=(kc == 7))
        P.copy(kraw[:, oc, :], ps[:, 0:M], eng=ev())
    for hd in range(4):
        rms_rstd(P, [kraw[:, 2 * hd, :], kraw[:, 2 * hd + 1, :]], 128, M, 1.0 / 256, EPS, ones.v, sqrot, ps_ss, rstd[:, 0:M], tmp)
        for dc in range(2):
            P.stt(kn[:, 2 * hd + dc, :], kraw[:, 2 * hd + dc, :], cs[:, 18 + dc:19 + dc], rstd[:, 0:M], ALU.mult, ALU.mult)
    for mc in range(2):
        for half in range(2):
            ps = PS.next()
            for kc in range(8):
                P.matmul(ps.v, memn[:, kc, mc * 128:(mc + 1) * 128], bufB[:, kc, half * 512:(half + 1) * 512],
                         start=(kc == 0), stop=(kc == 7))
            P.copy(vb[:, mc, half * 512:(half + 1) * 512], ps.v, eng=ev())
    load_weight_bf(P, wd["w_out"], D, D, bufA, gain=None, stages=stages, ncol=1024)
    load_weight_bf(P, wd["wo"], D, D, bufB, gain=None, stages=stages, ncol=1024)
    hts = Rot([P.sb("ht", [128, 8, n]) for _ in range(3)])
    yts = Rot([P.sb("yt", [128, 8, n]) for _ in range(2)])
    ybf = P.sb("ybf", [128, 8, n], BF16); xn = P.sb("xn", [128, 8, n], BF16)
    qraw = P.sb("qraw", [128, 8, n]); qn = P.sb("qn", [128, 8, n], BF16); ob = P.sb("ob", [128, 8, n], BF16)
    pTs = Rot([P.sb("pT", [128, n], BF16) for _ in range(4)])
    rs = Rot([P.sb("rs", [128, n]) for _ in range(2)])
    for ti in range(TOK // n):
        t0 = ti * n
        ht = hts.next(); yt = yts.next()
        P.dma("sp", ht.v, hT[:, t0:t0 + n].rearrange("(c p) n -> p c n", p=128))
        P.dma("pool", yt.v, yT[:, t0:t0 + n].rearrange("(c p) n -> p c n", p=128))
        for c in range(8):
            P.copy(ybf[:, c, :], yt[:, c, :], eng="pool" if c % 2 else "dve")
        for oc in range(8):
            ps = PS.next()
            for kc in range(8):
                P.matmul(ps[:, 0:n], bufA[:, kc, oc * 128:(oc + 1) * 128], ybf[:, kc, :], start=(kc == 0), stop=(kc == 7))
            P.tt(ht[:, oc, :], ps[:, 0:n], ht[:, oc, :], ALU.add)
        rms_rstd(P, [ht[:, c, :] for c in range(8)], 128, n, 1.0 / D, EPS, ones.v, sqrot, ps_ss, rstd.v, tmp)
        for c in range(8):
            P.tt(xn[:, c, :], ht[:, c, :], rstd.v, ALU.mult, eng="dve" if c % 2 else "pool")
        for oc in range(8):
            ps = PS.next()
            for kc in range(8):
                P.matmul(ps[:, 0:n], bufC[:, kc, oc * 128:(oc + 1) * 128], xn[:, kc, :], start=(kc == 0), stop=(kc == 7))
            P.copy(qraw[:, oc, :], ps[:, 0:n], eng=ev())
        for hd in range(4):
            rms_rstd(P, [qraw[:, 2 * hd, :], qraw[:, 2 * hd + 1, :]], 128, n, 1.0 / 256, EPS, ones.v, sqrot, ps_ss, rstd.v, tmp)
            for dc in range(2):
                P.stt(qn[:, 2 * hd + dc, :], qraw[:, 2 * hd + dc, :], cs[:, 16 + dc:17 + dc], rstd.v, ALU.mult, ALU.mult)
            pts = []
            for mc in range(2):
                ps = PS.next()
                for dc in range(2):
                    P.matmul(ps[:, 0:n], kn[:, 2 * hd + dc, mc * 128:(mc + 1) * 128], qn[:, 2 * hd + dc, :],
                             start=(dc == 0), stop=(dc == 1))
                pt = pTs.next()
                P.act(pt.v, ps[:, 0:n], AF.Exp, scale=1.0 / 16)
                pts.append(pt)
            ps = PS.next()
            for mc in range(2):
                P.matmul(ps[:, 0:n], onesb.v, pts[mc].v, start=(mc == 0), stop=(mc == 1))
            r = rs.next()
            P.recip(r.v, ps[:, 0:n])
            for dc in range(2):
                ps = PS.next()
                for mc in range(2):
                    P.matmul(ps[:, 0:n], vb[:, mc, (2 * hd + dc) * 128:(2 * hd + dc + 1) * 128], pts[mc].v,
                             start=(mc == 0), stop=(mc == 1))
                P.tt(ob[:, 2 * hd + dc, :], ps[:, 0:n], r.v, ALU.mult)
        for oc in range(8):
            ps = PS.next()
            for kc in range(8):
                P.matmul(ps[:, 0:n], bufB[:, kc, oc * 128:(oc + 1) * 128], ob[:, kc, :], start=(kc == 0), stop=(kc == 7))
            P.tt(ht[:, oc, :], ps[:, 0:n], ht[:, oc, :], ALU.add)
        P.dma("sp", h2T[:, t0:t0 + n].rearrange("(c p) n -> p c n", p=128), ht.v)
    P.emit()
    return nc


def vec_pc(v):
    return np.ascontiguousarray(np.asarray(v, np.float32).reshape(-1, 128).T)


def prep_C1(hT, yT, mem_b, prm):
    cC = np.concatenate([vec_pc(prm["xa_norm_g"]), vec_pc(prm["mem_norm_g"]), vec_pc(prm["xa_q_norm"]),
                         vec_pc(prm["xa_k_norm"])], 1)
    return {"hT": hT, "yT": yT, "memT": np.ascontiguousarray(mem_b.T), "w_out": prm["w_out"], "wq": prm["xa_wq"],
            "wk": prm["xa_wk"], "wv": prm["xa_wv"], "wo": prm["xa_wo"], "cC": np.ascontiguousarray(cC)}


def build_C2():
    nc = new_nc()
    P = Prog(nc)
    TOK = 4096
    n = 256
    NF = D_FF // 128
    h2T = P.dram("h2T", [D, 2 + TOK], F32, kind="ExternalInput")
    w_up = P.dram("w_up", [D, 2 * D_FF], F32, kind="ExternalInput")
    w_dn = P.dram("w_dn", [D_FF, D], F32, kind="ExternalInput")
    cF = P.dram("cF", [128, 8 + NF * 4], F32, kind="ExternalInput")
    h3T = P.dram("h3T", [D, TOK], F32, kind="ExternalOutput")
    cs = P.sb("cs", [128, 8 + NF * 4]); P.dma("sp", cs.v, cF.v)
    ones = P.sb("ones", [128, 128]); P.memset(ones.v, 1.0)
    stages = Rot([P.sb("stg", [128, 1408]) for _ in range(3)])
    wup = P.sb("wup", [128, 8, 2 * D_FF], BF16)
    wdn = P.sb("wdn", [128, NF, D], BF16)
    load_weight_bf(P, w_up, D, 2 * D_FF, wup, gain=cs[:, 0:8], stages=stages, ncol=1408)
    load_weight_bf(P, w_dn, D_FF, D, wdn, gain=None, stages=stages, ncol=1024)
    PS = Rot([P.ps("ps", [128, 512]) for _ in range(7)])
    ps_ss = P.ps("ps_ss", [128, 512])
    sqrot = Rot([P.sb("sq", [128, n]) for _ in range(2)])
    rstd = P.sb("rstd", [128, n]); tmp = P.sb("tmp", [128, n])
    hts = Rot([P.sb("ht", [128, 8, n]) for _ in range(3)])
    xns = Rot([P.sb("xn", [128, 8, n], BF16) for _ in range(2)])
    asb = Rot([P.sb("asb", [128, 2 + n]) for _ in range(3)])
    cc = Rot([P.sb("cc", [128, n]) for _ in range(2)])
    gl = Rot([P.sb("gl", [128, n]) for _ in range(2)])
    gbf = [P.sb("gbf", [128, n], BF16) for _ in range(NF)]
    aprev = P.sb("aprev", [128, NF, 2])

    def norm(c0, w):
        ht = hts.next(); xn = xns.next()
        P.dma("sp", ht[:, :, 0:w], h2T[:, c0:c0 + w].rearrange("(c p) n -> p c n", p=128))
        rms_rstd(P, [ht[:, c, 0:w] for c in range(8)], 128, w, 1.0 / D, EPS, ones.v, sqrot, ps_ss, rstd[:, 0:w], tmp)
        for c in range(8):
            P.tt(xn[:, c, 0:w], ht[:, c, 0:w], rstd[:, 0:w], ALU.mult, eng="dve" if c % 2 else "pool")
        return (c0, w, ht, xn)

    def body(cur, full, nxt):
        c0, w, ht, xn = cur
        res = None
        for f in range(NF):
            psa = PS.next()
            for kc in range(8):
                P.matmul(psa[:, 0:w], wup[:, kc, f * 128:(f + 1) * 128], xn[:, kc, 0:w], start=(kc == 0), stop=(kc == 7))
            if not full:
                P.copy(aprev[:, f, :], psa[:, 0:2], eng="act")
                continue
            psb = PS.next()
            for kc in range(8):
                P.matmul(psb[:, 0:w], wup[:, kc, D_FF + f * 128:D_FF + (f + 1) * 128], xn[:, kc, 0:w],
                         start=(kc == 0), stop=(kc == 7))
            if f == 3 and nxt is not None:
                res = norm(*nxt)
            a = asb.next()
            P.copy(a[:, 2:2 + w], psa[:, 0:w], eng="act")
            P.copy(a[:, 0:2], aprev[:, f, :], eng="pool")
            P.copy(aprev[:, f, :], a[:, w:w + 2], eng="pool")
            cb = 8 + f * 4
            c = cc.next()
            P.ts(c.v, a[:, 2:2 + w], cs[:, cb + 2:cb + 3], ALU.mult, cs[:, cb + 3:cb + 4], ALU.add)
            P.stt(c.v, a[:, 1:1 + w], cs[:, cb + 1:cb + 2], c.v, ALU.mult, ALU.add)
            P.stt(c.v, a[:, 0:w], cs[:, cb + 0:cb + 1], c.v, ALU.mult, ALU.add)
            g = gl.next()
            P.act(g.v, c.v, AF.Gelu)
            P.tt(gbf[f].v, psb[:, 0:w], g.v, ALU.mult)
        if not full:
            return norm(*nxt) if nxt is not None else None
        for oc in range(8):
            ps = PS.next()
            for f in range(NF):
                P.matmul(ps[:, 0:w], wdn[:, f, oc * 128:(oc + 1) * 128], gbf[f].v, start=(f == 0), stop=(f == NF - 1))
            P.tt(ht[:, oc, :], ps[:, 0:w], ht[:, oc, :], ALU.add)
        P.dma("pool", h3T[:, c0 - 2:c0 - 2 + w].rearrange("(c p) n -> p c n", p=128), ht.v)
        return res

    cur = norm(0, 2)
    ntl = TOK // n
    cur = body(cur, False, (2, n))
    for ti in range(ntl):
        nxt = (2 + (ti + 1) * n, n) if ti + 1 < ntl else None
        cur = body(cur, True, nxt)
    P.emit()
    return nc


def prep_C2(h2T_halo, prm):
    NF = D_FF // 128
    cw = prm["ffn_conv_w"]
    cols = [vec_pc(prm["ffn_norm_g"])]
    per = np.zeros((128, NF, 4), np.float32)
    for j in range(3):
        per[:, :, j] = cw[j].reshape(NF, 128).T
    per[:, :, 3] = prm["ffn_conv_b"].reshape(NF, 128).T
    cF = np.concatenate([cols[0], per.reshape(128, NF * 4)], 1)
    return {"h2T": h2T_halo, "w_up": prm["ffn_w_up"], "w_dn": prm["ffn_w_down"], "cF": np.ascontiguousarray(cF)}


_NC_CACHE = {}


def _get_nc(name):
    if name not in _NC_CACHE:
        _NC_CACHE[name] = {"A": build_A, "Bp": build_Bp, "Bd": build_Bd, "Br": build_Br, "C1": build_C1,
                           "C2": build_C2}[name]()
    return _NC_CACHE[name]


def _run(name, in_maps):
    nc = _get_nc(name)
    res = run_bass_kernel_spmd(nc, in_maps, core_ids=list(range(8)))
    return res.results


PARAM_KEYS = ("mix_norm_g", "w_in", "pool_w", "pool_scale", "rwkv_mu", "rwkv_w0", "rwkv_w2", "rwkv_a0", "rwkv_a2",
              "rwkv_g2", "rwkv_k_k", "rwkv_k_a", "rwkv_r_k", "rwkv_ln_w", "rwkv_ln_b", "diff_q_norm", "diff_k_norm",
              "diff_lq1", "diff_lk1", "diff_lq2", "diff_lk2", "diff_subln", "w_out", "xa_norm_g", "mem_norm_g",
              "xa_wq", "xa_wk", "xa_wv", "xa_wo", "xa_q_norm", "xa_k_norm", "ffn_norm_g", "ffn_w_up", "ffn_conv_w",
              "ffn_conv_b", "ffn_w_down")


def kernel(x, mem, **params):
    x = np.asarray(x, np.float32)
    mem = np.asarray(mem, np.float32)
    B, S, _ = x.shape
    TOK = S // 2
    depth = np.asarray(params["w_in"]).shape[0]
    hT = [np.ascontiguousarray(x[c // 2, (c % 2) * TOK:(c % 2 + 1) * TOK].T) for c in range(8)]
    for l in range(depth):
        prm = {k: np.ascontiguousarray(np.asarray(params[k][l], np.float32)) for k in PARAM_KEYS}
        cA = vec_pc(prm["mix_norm_g"])
        rA = _run("A", [{"hT": hT[c], "w_in": prm["w_in"], "cA": cA} for c in range(8)])
        zb = [np.concatenate([rA[2 * b]["zT"].T, rA[2 * b + 1]["zT"].T], axis=0) for b in range(B)]
        del rA
        rP = _run("Bp", [prep_Bp(zb[c // 2][:, 0:256], c % 2, prm["pool_w"], prm["pool_scale"]) for c in range(8)])
        rR = _run("Br", [prep_Br(zb[c // 2], c % 2, prm) for c in range(8)])
        rD = _run("Bd", [prep_Bd(zb[c // 2], c % 2, l, prm) for c in range(8)])
        del zb
        yT = []
        for c in range(8):
            b, hh = c // 2, c % 2
            y = np.empty((D, TOK), np.float32)
            y[0:256] = rP[c]["ypT"]
            for h2 in range(2):
                cc = 2 * b + h2
                y[256 + h2 * 128:256 + (h2 + 1) * 128] = rR[cc]["yr"][:, hh * TOK:(hh + 1) * TOK]
                for i_, hd_ in enumerate(diff_heads(h2)):
                    y[512 + hd_ * 128:512 + (hd_ + 1) * 128] = rD[cc]["yd"][hh * TOK:(hh + 1) * TOK, i_ * 128:(i_ + 1) * 128].T
            yT.append(y)
        del rP, rR, rD
        r1 = _run("C1", [prep_C1(hT[c], yT[c], mem[c // 2], prm) for c in range(8)])
        del yT
        h2 = [r1[c]["h2T"] for c in range(8)]
        del r1
        ins = []
        for c in range(8):
            hh = c % 2
            hal = np.zeros((D, 2 + TOK), np.float32)
            hal[:, 2:] = h2[c]
            if hh == 1:
                hal[:, 0:2] = h2[c - 1][:, TOK - 2:TOK]
            ins.append(prep_C2(hal, prm))
        r2 = _run("C2", ins)
        hT = [np.ascontiguousarray(r2[c]["h3T"]) for c in range(8)]
        del r2, h2, ins
    out = np.empty((B, S, D), np.float32)
    for c in range(8):
        out[c // 2, (c % 2) * TOK:(c % 2 + 1) * TOK] = hT[c].T
    return out
```

```python
import math
import numpy as np
from contextlib import ExitStack
import concourse.bass as bass
import concourse.mybir as mybir
from concourse.bass_utils import run_bass_kernel_spmd

F32 = mybir.dt.float32
BF16 = mybir.dt.bfloat16
AF = mybir.ActivationFunctionType
ALU = mybir.AluOpType
AX = mybir.AxisListType

ENGS = ("pe", "act", "dve", "pool", "sp")
SAME_ENG_SYNC = True


class T:
    def __init__(self, P, h, name, space):
        self.P = P
        self.h = h
        self.name = name
        self.space = space
        self.w = None
        self.r = {}
        self.dsem = None
        self.dcnt = 0

    def __getitem__(self, idx):
        return V(self, self.h[idx])

    @property
    def v(self):
        return V(self, self.h[:])


class V:
    __slots__ = ("t", "ap")

    def __init__(self, t, ap):
        self.t = t
        self.ap = ap

    def __getitem__(self, idx):
        return V(self.t, self.ap[idx])

    def rearrange(self, pat, **kw):
        return V(self.t, self.ap.rearrange(pat, **kw))

    def bitcast(self, dt):
        return V(self.t, self.ap.bitcast(dt))


class Prog:
    def __init__(self, nc):
        self.nc = nc
        self.es = ExitStack()
        self.streams = {e: [] for e in ENGS}
        self.sems = {}
        self.cnt = {e: 0 for e in ENGS}
        self.known = {e: {} for e in ENGS}
        self.out_tokens = []
        self.ntile = 0
        for e in ("pe", "act", "dve", "pool"):
            self.sems[e] = self.es.enter_context(nc.semaphore("s_" + e))

    def sb(self, name, shape, dt=F32):
        self.ntile += 1
        h = self.es.enter_context(self.nc.sbuf_tensor(f"{name}_{self.ntile}", list(shape), dt))
        return T(self, h, name, "sb")

    def ps(self, name, shape, dt=F32):
        self.ntile += 1
        h = self.es.enter_context(self.nc.psum_tensor(f"{name}_{self.ntile}", list(shape), dt))
        return T(self, h, name, "ps")

    def dram(self, name, shape, dt=F32, kind="Internal"):
        h = self.nc.dram_tensor(name, list(shape), dt, kind=kind)
        return T(self, h.ap(), name, "dram")

    def scope(self):
        return _Scope(self)

    def _need(self, eng, tok, waits):
        if tok is None:
            return
        k, v = tok
        if k == eng and (eng == "pe" or not SAME_ENG_SYNC):
            return
        if self.known[eng].get(k, 0) >= v:
            return
        waits[k] = max(waits.get(k, 0), v)

    def _deps(self, eng, reads, writes):
        waits = {}
        for t in reads:
            self._need(eng, t.w, waits)
        for t in writes:
            self._need(eng, t.w, waits)
            for k, v in t.r.items():
                self._need(eng, (k, v), waits)
        for k, v in waits.items():
            self.known[eng][k] = v
        return list(waits.items())

    def _mark(self, tok, reads, writes):
        k, v = tok
        for t in writes:
            t.w = tok
            t.r = {}
        for t in reads:
            if t in writes:
                continue
            t.r[k] = max(t.r.get(k, 0), v)

    def op(self, eng, fn, reads=(), writes=()):
        reads = [x.t if isinstance(x, V) else x for x in reads]
        writes = [x.t if isinstance(x, V) else x for x in writes]
        waits = self._deps(eng, reads, writes)
        self.cnt[eng] += 1
        tok = (eng, self.cnt[eng])
        self.streams[eng].append((waits, fn, eng, 1))
        self._mark(tok, reads, writes)
        return tok

    def dma(self, q, out, in_, **kw):
        sbt = out.t if out.t.space != "dram" else in_.t
        if sbt.dsem is None:
            sbt.dsem = "d%d" % len(self.sems)
            self.sems[sbt.dsem] = self.es.enter_context(self.nc.semaphore(sbt.dsem))
        waits = self._deps(q, [in_.t], [out.t])
        sbt.dcnt += 16
        tok = (sbt.dsem, sbt.dcnt)
        oap, iap = out.ap, in_.ap
        self.streams[q].append((waits, lambda e: e.dma_start(out=oap, in_=iap, **kw), sbt.dsem, 16))
        self._mark(tok, [in_.t], [out.t])
        if out.t.space == "dram":
            self.out_tokens.append(tok)
        return tok

    def emit(self):
        nc = self.nc
        fin = {}
        for k, v in self.out_tokens:
            fin[k] = max(fin.get(k, 0), v)
        for e in ("pe", "act", "dve", "pool"):
            if self.cnt[e]:
                fin[e] = self.cnt[e]
        final_waits = list(fin.items())
        sems = self.sems
        streams = self.streams

        def replay(e, name):
            eng = {"pe": nc.tensor, "act": nc.scalar, "dve": nc.vector, "pool": nc.gpsimd, "sp": nc.sync}[name]
            for waits, fn, isem, iv in streams[name]:
                for k, v in waits:
                    eng.wait_ge(sems[k], v)
                ins = fn(eng)
                ins.then_inc(sems[isem], iv)
            if name == "sp":
                for k, v in final_waits:
                    eng.wait_ge(sems[k], v)

        with nc.Block() as block:
            @block.sync
            def _(e):
                replay(e, "sp")

            @block.tensor
            def _(e):
                replay(e, "pe")

            @block.scalar
            def _(e):
                replay(e, "act")

            @block.vector
            def _(e):
                replay(e, "dve")

            @block.gpsimd
            def _(e):
                replay(e, "pool")
        self.es.close()

    def matmul(self, out, lhsT, rhs, start=True, stop=True, **kw):
        o, l, r = out.ap, lhsT.ap, rhs.ap
        return self.op("pe", lambda e: e.matmul(o, l, r, start=start, stop=stop, **kw),
                       reads=[lhsT, rhs], writes=[out])

    def transpose(self, out, in_, ident):
        o, i, d = out.ap, in_.ap, ident.ap
        return self.op("pe", lambda e: e.transpose(o, i, d), reads=[in_, ident], writes=[out])

    def act(self, out, in_, func, bias=None, scale=1.0, accum_out=None, eng="act"):
        o, i = out.ap, in_.ap
        reads = [in_]
        writes = [out]
        kw = {}
        if bias is not None:
            if isinstance(bias, V):
                reads.append(bias)
                kw["bias"] = bias.ap
            else:
                kw["bias"] = bias
        if isinstance(scale, V):
            reads.append(scale)
            kw["scale"] = scale.ap
        else:
            kw["scale"] = scale
        if accum_out is not None:
            writes.append(accum_out)
            kw["accum_out"] = accum_out.ap
        return self.op(eng, lambda e: e.activation(o, i, func, **kw), reads=reads, writes=writes)

    def tt(self, out, in0, in1, op, eng="dve"):
        o, a, b = out.ap, in0.ap, in1.ap
        return self.op(eng, lambda e: e.tensor_tensor(o, a, b, op), reads=[in0, in1], writes=[out])

    def ts(self, out, in0, s1, op0, s2=None, op1=None, accum_out=None, eng="dve"):
        o, a = out.ap, in0.ap
        reads = [in0]
        writes = [out]
        s1a = s1.ap if isinstance(s1, V) else s1
        s2a = s2.ap if isinstance(s2, V) else s2
        if isinstance(s1, V):
            reads.append(s1)
        if isinstance(s2, V):
            reads.append(s2)
        kw = {}
        if op1 is not None:
            kw["op1"] = op1
        if accum_out is not None:
            writes.append(accum_out)
            kw["accum_out"] = accum_out.ap
        return self.op(eng, lambda e: e.tensor_scalar(o, a, s1a, s2a, op0, **kw), reads=reads, writes=writes)

    def stt(self, out, in0, scalar, in1, op0, op1, eng="dve"):
        o, a, b = out.ap, in0.ap, in1.ap
        reads = [in0, in1]
        sa = scalar.ap if isinstance(scalar, V) else scalar
        if isinstance(scalar, V):
            reads.append(scalar)
        return self.op(eng, lambda e: e.scalar_tensor_tensor(o, a, sa, b, op0, op1), reads=reads, writes=[out])

    def copy(self, out, in_, eng="dve"):
        o, i = out.ap, in_.ap
        if eng == "act":
            return self.op("act", lambda e: e.copy(o, i), reads=[in_], writes=[out])
        return self.op(eng, lambda e: e.tensor_copy(o, i), reads=[in_], writes=[out])

    def memset(self, out, val, eng="dve"):
        o = out.ap
        return self.op(eng, lambda e: e.memset(o, val), reads=[], writes=[out])

    def reduce(self, out, in_, op=None, axis=None, eng="dve"):
        o, i = out.ap, in_.ap
        op = op or ALU.add
        axis = axis or AX.X
        return self.op(eng, lambda e: e.tensor_reduce(o, i, axis, op), reads=[in_], writes=[out])


class _Scope:
    def __init__(self, P):
        self.P = P

    def __enter__(self):
        self.saved = self.P.es
        self.P.es = ExitStack()
        return self

    def __exit__(self, *a):
        self.P.es.close()
        self.P.es = self.saved
        return False


def _recip(self, out, in_):
    o, i = out.ap, in_.ap
    return self.op("dve", lambda e: e.reciprocal(o, i), reads=[in_], writes=[out])


Prog.recip = _recip


Prog.recip = _recip

D = 1024
NT = 512
P_IN = 2816
D_FF = 2816
EPS = 1e-6


def new_nc():
    return bass.Bass("TRN2", target_bir_lowering=False)


class Rot:
    def __init__(self, tiles):
        self.tiles = tiles
        self.i = 0

    def next(self):
        t = self.tiles[self.i % len(self.tiles)]
        self.i += 1
        return t


def load_weight_bf(P, wd, K, N, dst, gain=None, stages=None, ncol=2816, q=("sp", "act", "pool"),
                   engs=("dve", "act", "dve", "act", "pool")):
    i = 0
    for kc in range(K // 128):
        for c0 in range(0, N, ncol):
            cw = min(ncol, N - c0)
            st = stages.next()
            P.dma(q[i % len(q)], st[:, 0:cw], wd[kc * 128:(kc + 1) * 128, c0:c0 + cw])
            eng = engs[i % len(engs)]
            if gain is not None:
                if eng == "act":
                    P.act(dst[:, kc, c0:c0 + cw], st[:, 0:cw], AF.Copy, scale=gain[:, kc:kc + 1])
                else:
                    P.ts(dst[:, kc, c0:c0 + cw], st[:, 0:cw], gain[:, kc:kc + 1], ALU.mult, eng=eng)
            else:
                P.copy(dst[:, kc, c0:c0 + cw], st[:, 0:cw], eng=eng)
            i += 1


def rms_rstd(P, chunks, rows, n, inv_d, eps, ones, sqrot, ps, rstd, tmp):
    nch = len(chunks)
    for i, ch in enumerate(chunks):
        sq = sqrot.next()
        P.act(sq[0:rows, 0:n], ch, AF.Square)
        P.matmul(ps[:, 0:n], ones[0:rows, :], sq[0:rows, 0:n], start=(i == 0), stop=(i == nch - 1))
    M = ones.ap.shape[-1] if hasattr(ones.ap, "shape") else 128
    P.act(tmp[:, 0:n], ps[:, 0:n], AF.Ln, bias=eps, scale=inv_d)
    P.act(rstd, tmp[:, 0:n], AF.Exp, scale=-0.5)


def build_A():
    nc = new_nc()
    P = Prog(nc)
    TOK = 4096
    hT = P.dram("hT", [D, TOK], F32, kind="ExternalInput")
    w_in = P.dram("w_in", [D, P_IN], F32, kind="ExternalInput")
    cA = P.dram("cA", [128, 8], F32, kind="ExternalInput")
    zT = P.dram("zT", [P_IN, TOK], F32, kind="ExternalOutput")
    consts = P.sb("consts", [128, 8])
    P.dma("sp", consts.v, cA.v)
    ones = P.sb("ones", [128, 128])
    P.memset(ones.v, 1.0)
    wbf = P.sb("wbf", [128, 8, P_IN], BF16)
    stages = Rot([P.sb("stg", [128, 2816]) for _ in range(2)])
    load_weight_bf(P, wd=w_in, K=D, N=P_IN, dst=wbf, gain=consts, stages=stages)
    hts = Rot([P.sb("ht", [128, 8, NT]) for _ in range(2)])
    sqrot = Rot([P.sb("sq", [128, NT]) for _ in range(2)])
    ps_ss = P.ps("ps_ss", [128, NT])
    ps_z = Rot([P.ps("ps_z", [128, NT]) for _ in range(4)])
    rstd = P.sb("rstd", [128, NT])
    tmp = P.sb("tmp", [128, NT])
    xn = Rot([P.sb("xn", [128, 8, NT], BF16) for _ in range(2)])
    zo = Rot([P.sb("zo", [128, NT]) for _ in range(4)])

    def norm(ti):
        t0 = ti * NT
        ht = hts.next()
        P.dma("sp", ht.v, hT[:, t0:t0 + NT].rearrange("(c p) n -> p c n", p=128))
        rms_rstd(P, [ht[:, c, :] for c in range(8)], 128, NT, 1.0 / D, EPS, ones.v, sqrot, ps_ss, rstd.v, tmp)
        x = xn.next()
        for c in range(8):
            P.tt(x[:, c, :], ht[:, c, :], rstd.v, ALU.mult, eng="dve" if c % 2 == 0 else "pool")
        return x

    ntl = TOK // NT
    x = norm(0)
    for ti in range(ntl):
        t0 = ti * NT
        xnext = None
        for oc in range(P_IN // 128):
            ps = ps_z.next()
            for kc in range(8):
                P.matmul(ps.v, wbf[:, kc, oc * 128:(oc + 1) * 128], x[:, kc, :], start=(kc == 0), stop=(kc == 7))
            if oc == 2 and ti + 1 < ntl:
                xnext = norm(ti + 1)
            o = zo.next()
            if oc % 2 == 0:
                P.copy(o.v, ps.v, eng="act")
            else:
                P.copy(o.v, ps.v, eng="dve")
            P.dma("sp" if oc % 2 == 0 else "pool", zT[oc * 128:(oc + 1) * 128, t0:t0 + NT], o.v)
        x = xnext
    P.emit()
    return nc


POOL_WINDOWS = (2, 4, 8, 16)


def build_Bp():
    nc = new_nc()
    P = Prog(nc)
    TOK = 4096
    H = 16
    zpT = P.dram("zpT", [256, H + TOK], F32, kind="ExternalInput")
    wblk = P.dram("wblk", [2, 128, 128], F32, kind="ExternalInput")
    cP = P.dram("cP", [128, 2 + 2 + 32], F32, kind="ExternalInput")
    ypT = P.dram("ypT", [256, TOK], F32, kind="ExternalOutput")
    cs = P.sb("cs", [128, 36])
    P.dma("sp", cs.v, cP.v)
    wb = [P.sb("wb", [128, 128]) for _ in range(2)]
    for i in range(2):
        P.dma("sp", wb[i].v, wblk[i])
    W = H + NT
    us = Rot([P.sb("u", [128, W]) for _ in range(2)])
    s2 = P.sb("s2", [128, W]); s4 = P.sb("s4", [128, W]); s8 = P.sb("s8", [128, W]); s16 = P.sb("s16", [128, W])
    dd = Rot([P.sb("dd", [128, W]) for _ in range(2)])
    win = P.sb("win", [128, W])
    ps = Rot([P.ps("ps", [128, NT]) for _ in range(2)])
    yo = Rot([P.sb("yo", [128, NT]) for _ in range(2)])
    for ti in range(TOK // NT):
        t0 = ti * NT
        for gp in range(2):
            u = us.next()
            P.dma("sp", u.v, zpT[gp * 128:(gp + 1) * 128, t0:t0 + W])
            P.tt(s2[:, 1:W], u[:, 1:W], u[:, 0:W - 1], ALU.add)
            P.tt(s4[:, 3:W], s2[:, 3:W], s2[:, 1:W - 2], ALU.add, eng="pool")
            if gp == 0:
                lo, hi = s2, s4
            else:
                P.tt(s8[:, 7:W], s4[:, 7:W], s4[:, 3:W - 4], ALU.add)
                P.tt(s16[:, 15:W], s8[:, 15:W], s8[:, 7:W - 8], ALU.add, eng="pool")
                lo, hi = s8, s16
            d = dd.next()
            for (a, b, src) in ((0, 64, lo), (64, 128, hi)):
                wsrc = src
                if ti == 0:
                    P.tt(win[a:b, H:H + 16], src[a:b, H:H + 16], cs[a:b, 4 + gp * 16:4 + gp * 16 + 16], ALU.mult)
                    P.stt(d[a:b, H:H + 16], win[a:b, H:H + 16], cs[a:b, 2 + gp:3 + gp], u[a:b, H:H + 16], ALU.mult, ALU.subtract)
                    P.stt(d[a:b, H + 16:W], src[a:b, H + 16:W], cs[a:b, 2 + gp:3 + gp], u[a:b, H + 16:W], ALU.mult, ALU.subtract)
                else:
                    P.stt(d[a:b, H:W], src[a:b, H:W], cs[a:b, 2 + gp:3 + gp], u[a:b, H:W], ALU.mult, ALU.subtract)
            p = ps.next()
            P.matmul(p.v, wb[gp].v, d[:, H:W])
            y = yo.next()
            P.ts(y.v, p.v, cs[:, gp:gp + 1], ALU.mult)
            P.dma("pool", ypT[gp * 128:(gp + 1) * 128, t0:t0 + NT], y.v)
    P.emit()
    return nc


def prep_Bp(z_pool_b, hh, pool_w, pool_scale):
    TOK = 4096
    zp = np.zeros((256, 16 + TOK), np.float32)
    if hh == 0:
        zp[:, 16:] = z_pool_b[0:TOK].T
    else:
        zp[:, :] = z_pool_b[TOK - 16:2 * TOK].T
    wblk = np.zeros((2, 128, 128), np.float32)
    for g in range(4):
        i, o = g // 2, (g % 2) * 64
        wblk[i, o:o + 64, o:o + 64] = pool_w[g]
    cP = np.zeros((128, 36), np.float32)
    for gp in range(2):
        cP[:, gp] = pool_scale[gp * 128:(gp + 1) * 128]
        for half in range(2):
            w = POOL_WINDOWS[gp * 2 + half]
            cP[half * 64:(half + 1) * 64, 2 + gp] = 1.0 / w
            t = np.arange(16)
            corr = (w / np.minimum(t + 1, w)) if hh == 0 else np.ones(16)
            cP[half * 64:(half + 1) * 64, 4 + gp * 16:4 + gp * 16 + 16] = corr[None, :]
    return {"zpT": zp, "wblk": wblk, "cP": cP}


SEQ = 8192
ALIBI = [2.0 ** (-8.0 * (h + 1) / 4) for h in range(4)]
NEG = -30000.0
ALIBI_WIN = 12


def diff_heads(hh):
    return [hh, 3 - hh]


def build_Bd():
    nc = new_nc()
    P = Prog(nc)
    S = SEQ
    NKB = S // 128
    qin = P.dram("qin", [2, 2, 64, S], F32, kind="ExternalInput")
    kin = P.dram("kin", [2, 2, 64, S], F32, kind="ExternalInput")
    vin = P.dram("vin", [2, S, 128], F32, kind="ExternalInput")
    qrow = P.dram("qrow", [2, 1, S], BF16, kind="ExternalInput")
    btab = P.dram("btab", [2, 128, 64], F32, kind="ExternalInput")
    dtab = P.dram("dtab", [2, 128, 128], BF16, kind="ExternalInput")
    identb = P.dram("identb", [128, 128], BF16, kind="ExternalInput")
    cD = P.dram("cD", [128, 8], F32, kind="ExternalInput")
    lvec = P.dram("lvec", [128, 4, 64], F32, kind="ExternalInput")
    subln = P.dram("subln", [128, 128], F32, kind="ExternalInput")
    yd = P.dram("yd", [S, 256], F32, kind="ExternalOutput")

    cs = P.sb("cs", [128, 8]); P.dma("sp", cs.v, cD.v)
    lv = P.sb("lv", [128, 4, 64]); P.dma("sp", lv.v, lvec.v)
    sl = P.sb("sl", [128, 128]); P.dma("sp", sl.v, subln.v)
    idb = P.sb("idb", [128, 128], BF16); P.dma("sp", idb.v, identb.v)
    ones = P.sb("ones", [128, 128]); P.memset(ones.v, 1.0)
    lt = P.sb("lt", [128, 64]); e1 = P.sb("e1", [128, 1]); e2 = P.sb("e2", [128, 1]); nlam = P.sb("nlam", [128, 1])
    P.tt(lt.v, lv[:, 0, :], lv[:, 1, :], ALU.mult); P.reduce(e1.v, lt.v); P.act(e1.v, e1.v, AF.Exp)
    P.tt(lt.v, lv[:, 2, :], lv[:, 3, :], ALU.mult); P.reduce(e2.v, lt.v); P.act(e2.v, e2.v, AF.Exp)
    P.tt(nlam.v, e2.v, e1.v, ALU.subtract)
    P.tt(nlam.v, nlam.v, cs[:, 4:5], ALU.subtract)
    sls = P.sb("sls", [128, 128])
    P.ts(sls.v, sl.v, cs[:, 5:6], ALU.mult)

    QT = [P.sb("QT", [65, S], BF16) for _ in range(2)]
    KT = [P.sb("KT", [65, S], BF16) for _ in range(2)]
    VA = P.sb("VA", [128, NKB, 129], BF16)
    BT = P.sb("BT", [128, 64]); DT = P.sb("DT", [128, 128], BF16)
    qst = Rot([P.sb("qst", [64, NT]) for _ in range(8)])
    sqr = Rot([P.sb("sqr", [64, NT]) for _ in range(4)])
    tmpr = Rot([P.sb("tmpr", [64, NT]) for _ in range(4)]); rstdr = Rot([P.sb("rstd", [64, NT]) for _ in range(4)])
    vst = Rot([P.sb("vst", [128, 16, 128]) for _ in range(2)])
    ps_pre = P.ps("ps_pre", [64, NT])
    ps_s = Rot([P.ps("ps_s", [128, 2, NT]) for _ in range(2)])
    accs = [P.ps("acc", [128, 3, 129]) for _ in range(3)]
    pT = Rot([P.sb("pT", [128, 2, NT], BF16) for _ in range(3)])
    accsb = [P.sb("accsb", [128, 3, 129]) for _ in range(3)]
    fin = {k: Rot([P.sb(k, s) for _ in range(2)]) for k, s in
           (("r0", [128, 1]), ("r1", [128, 1]), ("o0", [128, 128]), ("o", [128, 128]), ("sqo", [128, 128]),
            ("ss", [128, 1]), ("out", [128, 128]))}

    def acc_slot(c, j):
        idx = c * 4 + j
        return idx // 3, idx % 3

    for i in range(2):
        P.dma("sp", BT.v, btab[i]); P.dma("sp", DT.v, dtab[i])
        its = []
        for c in range(2):
            P.dma("pool", QT[c][64:65, :], qrow[i])
            P.memset(KT[c][64:65, :], 1.0)
            for (src, dst, gcol) in ((qin, QT[c], c), (kin, KT[c], 2 + c)):
                for ti in range(S // NT):
                    its.append((src, dst, gcol, c, ti * NT))
        for g0 in range(0, len(its), 4):
            grp = its[g0:g0 + 4]
            sts = []
            for k_, (src, dst, gcol, c, t0) in enumerate(grp):
                st = qst.next()
                P.dma("sp" if k_ % 2 == 0 else "act", st.v, src[i, c, :, t0:t0 + NT])
                sts.append(st)
            sqs = []
            for k_, st in enumerate(sts):
                sq = sqr.next()
                P.tt(sq.v, st.v, st.v, ALU.mult, eng="pool" if k_ % 2 else "dve")
                sqs.append(sq)
            pss = []
            for k_, sq in enumerate(sqs):
                ps = ps_s.next() if k_ % 2 == 0 else pss[-1][0]
                view = ps[0:64, k_ % 2, :]
                P.matmul(view, ones[0:64, 0:64], sq.v)
                pss.append((ps, view))
            tms = []
            for (ps, view) in pss:
                tm_ = tmpr.next()
                P.act(tm_.v, view, AF.Ln, bias=EPS, scale=1.0 / 64)
                tms.append(tm_)
            rss = []
            for tm_ in tms:
                r_ = rstdr.next()
                P.act(r_.v, tm_.v, AF.Exp, scale=-0.5)
                rss.append(r_)
            for (src, dst, gcol, c, t0), st, r_ in zip(grp, sts, rss):
                P.stt(dst[0:64, t0:t0 + NT], st.v, cs[0:64, gcol:gcol + 1], r_.v, ALU.mult, ALU.mult)
        P.memset(VA[:, :, 128:129], 1.0)
        for vc in range(4):
            st = vst.next()
            P.dma("pool", st.v, vin[i, vc * 2048:(vc + 1) * 2048, :].rearrange("(kb p) d -> p kb d", p=128))
            P.copy(VA[:, vc * 16:(vc + 1) * 16, 0:128], st.v, eng="pool")
        for QB in range(S // NT):
            q0 = QB * NT
            tiles = []
            kb_lo = max(0, 4 * QB - ALIBI_WIN) if i == 0 else 0
            for kb in range(kb_lo, 4 * QB + 4):
                i2 = max(0, kb - 4 * QB)
                tiles.append((kb, i2))
            started = set()
            prev = None

            def emit_pv(info):
                kb, i2, pt = info
                for c in range(2):
                    for j in range(i2, 4):
                        bank, slot = acc_slot(c, j)
                        st_flag = bank not in started
                        started.add(bank)
                        P.matmul(accs[bank][:, slot, :], pt[:, c, (j - i2) * 128:(j - i2 + 1) * 128], VA[:, kb, :],
                                 start=st_flag, stop=False, skip_group_check=True)

            for (kb, i2) in tiles:
                N = (4 - i2) * 128
                diag = kb >= 4 * QB
                m = 4 * QB - kb + 3
                ps = ps_s.next()
                for c in range(2):
                    P.matmul(ps[:, c, 0:N], KT[c][0:65, kb * 128:(kb + 1) * 128], QT[c][0:65, q0 + i2 * 128:q0 + NT],
                             start=True, stop=not diag, skip_group_check=True)
                    if diag:
                        P.matmul(ps[:, c, 0:128], idb.v, DT.v, start=False, stop=True, skip_group_check=True)
                pt = pT.next()
                P.act(pt[:, :, 0:N], ps[:, :, 0:N], AF.Exp, bias=BT[:, m:m + 1], scale=0.125)
                if prev is not None:
                    emit_pv(prev)
                prev = (kb, i2, pt)
            emit_pv(prev)
            for b3 in range(3):
                P.copy(accsb[b3].v, accs[b3].v, eng="dve" if b3 != 1 else "act")
            for j in range(4):
                b0, s0 = acc_slot(0, j); b1, s1 = acc_slot(1, j)
                r0 = fin["r0"].next(); r1 = fin["r1"].next(); o0 = fin["o0"].next(); o = fin["o"].next()
                sqo = fin["sqo"].next(); ss = fin["ss"].next(); out = fin["out"].next()
                P.recip(r0.v, accsb[b0][:, s0, 128:129])
                P.recip(r1.v, accsb[b1][:, s1, 128:129])
                P.tt(r1.v, r1.v, nlam.v, ALU.mult)
                P.ts(o0.v, accsb[b0][:, s0, 0:128], r0.v, ALU.mult)
                P.stt(o.v, accsb[b1][:, s1, 0:128], r1.v, o0.v, ALU.mult, ALU.add)
                P.act(sqo.v, o.v, AF.Square, accum_out=ss.v)
                P.act(ss.v, ss.v, AF.Ln, bias=EPS, scale=1.0 / 128)
                P.act(ss.v, ss.v, AF.Exp, scale=-0.5)
                P.stt(out.v, o.v, ss.v, sls.v, ALU.mult, ALU.mult)
                P.dma("pool", yd[q0 + j * 128:q0 + (j + 1) * 128, i * 128:(i + 1) * 128], out.v)
    P.emit()
    return nc


def prep_Bd(z_b, hh, l, prm):
    import ml_dtypes
    S = SEQ
    base = 256 + 1024
    q = z_b[:, base:base + 512].reshape(S, 4, 2, 64)
    k = z_b[:, base + 512:base + 1024].reshape(S, 4, 2, 64)
    v = z_b[:, base + 1024:base + 1536].reshape(S, 4, 128)
    hs = diff_heads(hh)
    qin = np.ascontiguousarray(q[:, hs].transpose(1, 2, 3, 0))
    kin = np.ascontiguousarray(k[:, hs].transpose(1, 2, 3, 0))
    vin = np.ascontiguousarray(v[:, hs].transpose(1, 0, 2))
    li = 0.8 - 0.6 * math.exp(-0.3 * l)
    qrow = np.zeros((2, 1, S), np.float32)
    btab = np.zeros((2, 128, 64), np.float32)
    dtab = np.zeros((2, 128, 128), np.float32)
    ki = np.arange(128)[:, None]
    qi = np.arange(128)[None, :]
    for i, h in enumerate(hs):
        sl = ALIBI[h]
        j = (np.arange(S) // 128) % 4
        qrow[i, 0] = -sl * 128.0 * j * 8.0
        mm = np.arange(64)[None, :]
        btab[i] = sl * (ki - 128.0 * (mm - 3))
        vis = (ki // 64) <= (qi // 64)
        dpr = np.where(qi >= ki, 0.0, 2.0 * sl * (qi - ki))
        dtab[i] = np.where(vis, dpr * 8.0, NEG * 8.0)
    cD = np.zeros((128, 8), np.float32)
    cD[0:64, 0:2] = prm["diff_q_norm"].T
    cD[0:64, 2:4] = prm["diff_k_norm"].T
    cD[:, 4] = li
    cD[:, 5] = 1.0 - li
    lvec = np.stack([np.broadcast_to(prm[n], (128, 64)) for n in ("diff_lq1", "diff_lk1", "diff_lq2", "diff_lk2")], 1)
    return {"qin": qin, "kin": kin, "vin": vin, "qrow": qrow.astype(ml_dtypes.bfloat16), "btab": btab,
            "dtab": dtab.astype(ml_dtypes.bfloat16), "identb": np.eye(128, dtype=np.float32).astype(ml_dtypes.bfloat16),
            "cD": cD, "lvec": np.ascontiguousarray(lvec.astype(np.float32)),
            "subln": np.ascontiguousarray(np.broadcast_to(prm["diff_subln"], (128, 128)).astype(np.float32))}


RWKV_GN_EPS = 64e-5


def build_Br():
    nc = new_nc()
    P = Prog(nc)
    S = SEQ
    SC = 512
    NCH = SC // 128
    rkv = P.dram("rkv", [2, 3, 64, 1 + S], F32, kind="ExternalInput")
    lw = P.dram("lw", [64, 1 + S], F32, kind="ExternalInput")
    la = P.dram("la", [64, 1 + S], F32, kind="ExternalInput")
    lg = P.dram("lg", [128, 1 + S], F32, kind="ExternalInput")
    cR = P.dram("cR", [128, 32], F32, kind="ExternalInput")
    w2 = P.dram("w2", [2, 64, 64], F32, kind="ExternalInput")
    a2 = P.dram("a2", [2, 64, 64], F32, kind="ExternalInput")
    g2 = P.dram("g2", [2, 128, 64], F32, kind="ExternalInput")
    msk = P.dram("msk", [128, 384], F32, kind="ExternalInput")
    idn = P.dram("idn", [128, 128], F32, kind="ExternalInput")
    rmask = P.dram("rmask", [64, SC], F32, kind="ExternalInput")
    yr = P.dram("yr", [128, S], F32, kind="ExternalOutput")

    cs = P.sb("cs", [128, 32]); P.dma("sp", cs.v, cR.v)
    mk = P.sb("mk", [128, 384]); P.dma("sp", mk.v, msk.v)
    ident = P.sb("ident", [128, 128]); P.dma("sp", ident.v, idn.v)
    rm = P.sb("rm", [64, SC]); P.dma("sp", rm.v, rmask.v)
    w2s = [P.sb("w2s", [64, 64]) for _ in range(2)]
    a2s = [P.sb("a2s", [64, 64]) for _ in range(2)]
    g2s = [P.sb("g2s", [128, 64]) for _ in range(2)]
    for i in range(2):
        P.dma("sp", w2s[i].v, w2[i]); P.dma("sp", a2s[i].v, a2[i]); P.dma("sp", g2s[i].v, g2[i])
    ones = P.sb("ones", [64, 64]); P.memset(ones.v, 1.0)
    o64 = P.sb("o64", [64, 64]); P.memset(o64.v, 1.0 / 64)
    omka = P.sb("omka", [64, 2])
    for i in range(2):
        b = 4 + 12 * i
        P.ts(omka[:, i:i + 1], cs[0:64, b + 6:b + 7], -1.0, ALU.mult, 1.0, ALU.add)

    PS = Rot([P.ps("ps", [128, 512]) for _ in range(8)])

    def R(name, shape, n=2, dt=F32):
        return Rot([P.sb(name, shape, dt) for _ in range(n)])

    A = {}
    for nm in ("zr", "zk", "zv"):
        A[nm] = R(nm, [64, SC + 1], 2)
    A["zw"] = R("zw", [64, SC + 1], 2); A["za"] = R("za", [64, SC + 1], 2); A["zg"] = R("zg", [128, SC + 1], 2)
    for nm in ("d64", "rm_", "km_", "vm_", "tw", "adm", "ld", "asig", "g", "kk", "sq", "t1", "t2", "kkn", "k2", "av",
               "bv", "Lc", "eL", "eLp", "enL", "ex", "YT", "dlt", "yn", "bon", "yo"):
        A[nm] = R(nm, [64, SC], 2)
    A["d128"] = R("d128", [128, SC], 2); A["sg"] = R("sg", [128, SC], 2); A["gdm"] = R("gdm", [128, SC], 2)
    A["AR"] = R("AR", [64, NCH, 2, 128], 2); A["BK"] = R("BK", [64, NCH, 2, 128], 2); A["BKH"] = R("BKH", [64, 2, SC], 2)
    A["eLC"] = R("eLC", [64, NCH], 2)
    TM = R("TM", [128, 4, 64], 8, BF16); SB1 = R("SB1", [128, 256], 8, BF16); SB2 = R("SB2", [128, 256], 8, BF16)
    PX = R("PX", [128, 256], 8, BF16); UX = R("UX", [128, 256], 8, BF16); PTt = R("PTt", [128, 128], 8, BF16); MC = R("MC", [64, 64], 8); QH = R("QH", [64, 128], 8)
    ST = [[P.sb("ST", [64, 64]) for _ in range(2)] for _ in range(2)]
    sti = [0, 0]
    for i in range(2):
        P.memset(ST[i][0].v, 0.0)
    flip = [0]

    def ev():
        flip[0] += 1
        return "dve" if flip[0] % 2 else "act"

    def shift_mix(zt, rows, mucol, dtmp, out):
        P.tt(dtmp[0:rows, :], zt[0:rows, 0:SC], zt[0:rows, 1:SC + 1], ALU.subtract)
        P.stt(out[0:rows, :], dtmp[0:rows, :], cs[0:rows, mucol:mucol + 1], zt[0:rows, 1:SC + 1], ALU.mult, ALU.add)

    for sc in range(S // SC):
        t0 = sc * SC
        zw = A["zw"].next(); za = A["za"].next(); zg = A["zg"].next()
        P.dma("sp", zw.v, lw[:, t0:t0 + SC + 1]); P.dma("sp", za.v, la[:, t0:t0 + SC + 1]); P.dma("sp", zg.v, lg[:, t0:t0 + SC + 1])
        d64 = A["d64"].next(); d128 = A["d128"].next()
        tw = A["tw"].next(); adm = A["adm"].next(); sg = A["sg"].next(); gdm = A["gdm"].next()
        shift_mix(zw, 64, 0, d128, tw); P.act(tw.v, tw.v, AF.Tanh)
        shift_mix(za, 64, 1, d128, adm)
        shift_mix(zg, 128, 2, d128, gdm); P.act(sg.v, gdm.v, AF.Sigmoid)
        H = [None, None]

        def prep(i):
            b = 4 + 12 * i
            zr = A["zr"].next(); zk = A["zk"].next(); zv = A["zv"].next()
            P.dma("pool", zr.v, rkv[i, 0, :, t0:t0 + SC + 1]); P.dma("pool", zk.v, rkv[i, 1, :, t0:t0 + SC + 1])
            P.dma("pool", zv.v, rkv[i, 2, :, t0:t0 + SC + 1])
            rm_ = A["rm_"].next(); km_ = A["km_"].next(); vm_ = A["vm_"].next()
            d64 = A["d64"].next()
            shift_mix(zr, 64, b + 0, d64, rm_); shift_mix(zk, 64, b + 1, d64, km_); shift_mix(zv, 64, b + 2, d64, vm_)
            yield
            ps = PS.next(); ld = A["ld"].next()
            P.matmul(ps[0:64, :], w2s[i].v, tw.v)
            P.act(ld.v, ps[0:64, :], AF.Sigmoid, bias=cs[0:64, b + 3:b + 4])
            P.act(ld.v, ld.v, AF.Copy, scale=-math.exp(-0.5))
            yield
            ps = PS.next(); asig = A["asig"].next()
            P.matmul(ps[0:64, :], a2s[i].v, adm.v)
            P.act(asig.v, ps[0:64, :], AF.Sigmoid, bias=cs[0:64, b + 4:b + 5])
            yield
            ps = PS.next(); g = A["g"].next()
            P.matmul(ps[0:64, :], g2s[i].v, sg.v)
            P.copy(g.v, ps[0:64, :], eng="act")
            yield
            kk = A["kk"].next(); sq = A["sq"].next(); t1 = A["t1"].next(); kkn = A["kkn"].next()
            P.act(kk.v, km_.v, AF.Copy, scale=cs[0:64, b + 5:b + 6])
            P.tt(sq.v, kk.v, kk.v, ALU.mult, eng="pool")
            yield
            ps = PS.next()
            P.matmul(ps[0:64, :], ones.v, sq.v)
            P.ts(t1.v, ps[0:64, :], 1e-24, ALU.max)
            P.act(t1.v, t1.v, AF.Ln); P.act(t1.v, t1.v, AF.Exp, scale=-0.5)
            P.tt(kkn.v, kk.v, t1.v, ALU.mult)
            yield
            t2 = A["t2"].next(); k2 = A["k2"].next()
            P.ts(t2.v, asig.v, cs[0:64, b + 6:b + 7], ALU.mult, omka[:, i:i + 1], ALU.add)
            P.tt(k2.v, km_.v, t2.v, ALU.mult, eng="pool")
            yield
            bv = A["bv"].next()
            P.tt(bv.v, kkn.v, asig.v, ALU.mult, eng="pool")
            yield
            Lc = A["Lc"].next(); eL = A["eL"].next(); eLp = A["eLp"].next(); enL = A["enL"].next()
            lo, m_, x_ = Lc.h[:], rm.h[:], ld.h[:]
            P.op("dve", lambda e, lo=lo, m_=m_, x_=x_: e.tensor_tensor_scan(lo, m_, x_, 0.0, ALU.mult, ALU.add),
                 reads=[rm, ld], writes=[Lc])
            yield
            P.act(eL.v, Lc.v, AF.Exp)
            P.act(enL.v, Lc.v, AF.Exp, scale=-1.0)
            P.tt(eLp.v, Lc.v, ld.v, ALU.subtract)
            P.act(eLp.v, eLp.v, AF.Exp)
            yield
            AR = A["AR"].next(); BK = A["BK"].next(); BKH = A["BKH"].next(); eLC = A["eLC"].next()
            v3 = lambda t: t.v.rearrange("p (c t) -> p c t", t=128)
            P.stt(AR[:, :, 0, :], v3(kkn), -1.0, v3(eLp), ALU.mult, ALU.mult)
            P.tt(AR[:, :, 1, :], v3(rm_), v3(eL), ALU.mult, eng="pool")
            P.tt(BK[:, :, 0, :], v3(bv), v3(enL), ALU.mult)
            P.tt(BK[:, :, 1, :], v3(k2), v3(enL), ALU.mult, eng="pool")
            P.copy(eLC.v, v3(eL)[:, :, 127])
            yield
            ex = A["ex"].next()
            for c in range(NCH):
                P.act(ex[:, c * 128:(c + 1) * 128], Lc[:, c * 128:(c + 1) * 128], AF.Exp,
                      bias=Lc[:, c * 128 + 127:c * 128 + 128], scale=-1.0)
            yield
            P.tt(BKH[:, 0, :], bv.v, ex.v, ALU.mult)
            P.tt(BKH[:, 1, :], k2.v, ex.v, ALU.mult, eng="pool")
            YT = A["YT"].next()
            H[i] = (dict(i=i, b=b, AR=AR, BK=BK, BKH=BKH, eLC=eLC, vm_=vm_, rm_=rm_, k2=k2, g=g, YT=YT))
        gens = [prep(0), prep(1)]
        alive = True
        while alive:
            alive = False
            for g_ in gens:
                try:
                    next(g_)
                    alive = True
                except StopIteration:
                    pass
        for hd in H:
            i = hd["i"]; AR = hd["AR"]; BK = hd["BK"]; BKH = hd["BKH"]; vm_ = hd["vm_"]
            CH = []
            for c in range(NCH):
                cs_ = slice(c * 128, (c + 1) * 128)
                ch = dict(c=c, cs_=cs_, ARc=AR[:, c].rearrange("p a b -> p (a b)"), tm=TM.next(), sb1=SB1.next(),
                          sb2=SB2.next(), px=PX.next(), pt=PTt.next(), mc=MC.next(), qh=QH.next())
                CH.append(ch)
            for ch in CH:
                c = ch["c"]; cs_ = ch["cs_"]; ps = PS.next()
                P.transpose(ps[:, 0:64], AR[:, c, 0, :], ident[0:64, 0:64])
                P.transpose(ps[:, 64:128], vm_[:, cs_], ident[0:64, 0:64])
                P.transpose(ps[:, 128:192], BKH[:, 0, cs_], ident[0:64, 0:64])
                P.transpose(ps[:, 192:256], BKH[:, 1, cs_], ident[0:64, 0:64])
                P.copy(ch["tm"].v.rearrange("p a b -> p (a b)"), ps[:, 0:256], eng=ev())
            for ch in CH:
                c = ch["c"]; ps = PS.next()
                P.matmul(ps[:, 0:256], BK[:, c, 0, :], ch["ARc"])
                P.tt(ch["sb1"].v, ps[:, 0:256], mk[:, 0:256], ALU.mult)
            for ch in CH:
                c = ch["c"]; ps = PS.next()
                P.matmul(ps[:, 0:256], BK[:, c, 1, :], ch["ARc"])
                P.tt(ch["sb2"].v, ps[:, 0:256], mk[:, 0:256], ALU.mult)
            for ch in CH:
                c = ch["c"]; ps = PS.next()
                P.matmul(ps[:, 0:128], AR[:, c, 0, :], BK[:, c, 0, :])
                P.tt(ch["px"][:, 0:128], ps[:, 0:128], mk[:, 256:384], ALU.mult)
            for ch in CH:
                ps = PS.next()
                P.matmul(ps[:, 0:64], ch["sb2"][:, 0:128], ch["tm"][:, 1, :])
                P.copy(ch["px"][:, 192:256], ps[:, 0:64], eng=ev())
                P.copy(ch["px"][:, 128:192], ch["tm"][:, 0, :], eng="pool")
                P.copy(ch["pt"].v, ch["sb1"][:, 0:128], eng="pool")
            for j in range(7):
                for ch in CH:
                    px = ch["px"]; pt = ch["pt"]
                    ps = PS.next(); px2 = PX.next() if j < 6 else UX.next()
                    if j < 5:
                        P.matmul(ps[:, 0:256], pt.v, px.v)
                        P.copy(px2[:, 0:128], ps[:, 0:128], eng="act")
                    else:
                        P.matmul(ps[:, 128:256], pt.v, px[:, 128:256])
                    P.tt(px2[:, 128:256], ps[:, 128:256], px[:, 128:256], ALU.add)
                    ch["px2"] = px2
                for ch in CH:
                    if j < 6:
                        ps = PS.next(); pt2 = PTt.next()
                        P.matmul(ps[:, 0:128], ch["px"][:, 0:128], ch["pt"].v)
                        P.copy(pt2.v, ps[:, 0:128], eng=ev())
                        ch["pt"] = pt2
                    ch["px"] = ch["px2"]
            for ch in CH:
                c = ch["c"]
                Ua = ch["px"][:, 128:192]
                ps = PS.next()
                P.matmul(ps[0:64, 0:64], Ua, ch["tm"][:, 2, :])
                P.copy(ch["mc"].v, ps[0:64, 0:64], eng=ev())
                ps = PS.next()
                P.matmul(ps[0:64, 0:128], Ua, ch["sb1"][:, 128:256])
                P.tt(ch["qh"].v, ps[0:64, 0:128], AR[:, c, 1, :], ALU.add)
            hd["CH"] = CH
        for c in range(NCH):
            for hd in H:
                i = hd["i"]; ch = hd["CH"][c]; tm = ch["tm"]; sb1 = ch["sb1"]; sb2 = ch["sb2"]
                Uv = ch["px"][:, 192:256]
                st_old = ST[i][sti[i] % 2]; st_new = ST[i][(sti[i] + 1) % 2]; sti[i] += 1
                ps = PS.next()
                P.matmul(ps[0:64, 0:128], st_old.v, ch["qh"].v, start=True, stop=False)
                P.matmul(ps[0:64, 0:128], Uv, sb1[:, 128:256], start=False, stop=False)
                P.matmul(ps[0:64, 0:128], tm[:, 1, :], sb2[:, 128:256], start=False, stop=True)
                P.copy(hd["YT"][:, ch["cs_"]], ps[0:64, 0:128], eng="act")
                ps = PS.next()
                P.matmul(ps[0:64, 0:64], ch["mc"].v, st_old.v, start=True, stop=False)
                P.matmul(ps[0:64, 0:64], tm[:, 2, :], Uv, start=False, stop=False)
                P.matmul(ps[0:64, 0:64], tm[:, 3, :], tm[:, 1, :], start=False, stop=True)
                P.stt(st_new.v, st_old.v, hd["eLC"][:, c:c + 1], ps[0:64, 0:64], ALU.mult, ALU.add)
        for hd in H:
            i = hd["i"]; b = hd["b"]; YT = hd["YT"]; rm_ = hd["rm_"]; k2 = hd["k2"]; vm_ = hd["vm_"]; g = hd["g"]
            dlt = A["dlt"].next(); yn = A["yn"].next(); bon = A["bon"].next(); yo = A["yo"].next()
            ps = PS.next()
            P.matmul(ps[0:64, :], o64.v, YT.v)
            P.tt(dlt.v, YT.v, ps[0:64, :], ALU.subtract)
            sq2 = A["sq"].next()
            P.tt(sq2.v, dlt.v, dlt.v, ALU.mult, eng="pool")
            ps = PS.next()
            P.matmul(ps[0:64, :], o64.v, sq2.v)
            t3 = A["t1"].next()
            P.act(t3.v, ps[0:64, :], AF.Ln, bias=RWKV_GN_EPS)
            P.act(t3.v, t3.v, AF.Exp, scale=-0.5)
            P.tt(yn.v, dlt.v, t3.v, ALU.mult)
            P.ts(yn.v, yn.v, cs[0:64, b + 9:b + 10], ALU.mult, cs[0:64, b + 10:b + 11], ALU.add)
            t4 = A["t2"].next()
            P.stt(t4.v, rm_.v, cs[0:64, b + 8:b + 9], k2.v, ALU.mult, ALU.mult)
            ps = PS.next()
            P.matmul(ps[0:64, :], ones.v, t4.v)
            P.tt(bon.v, ps[0:64, :], vm_.v, ALU.mult)
            P.tt(yn.v, yn.v, bon.v, ALU.add, eng="pool")
            P.tt(yo.v, yn.v, g.v, ALU.mult, eng="pool")
            P.dma("sp", yr[i * 64:(i + 1) * 64, t0:t0 + SC], yo.v)
    P.emit()
    return nc


def prep_Br(z_b, hh, prm):
    S = SEQ
    zr = z_b[:, 256:1280]
    hs = [2 * hh, 2 * hh + 1]

    def fm(cols):
        out = np.zeros((cols.shape[1], 1 + S), np.float32)
        out[:, 1:] = cols.T
        return out
    rkv = np.stack([np.stack([fm(zr[:, o + h * 64:o + (h + 1) * 64]) for o in (0, 256, 512)]) for h in hs])
    mu = prm["rwkv_mu"]
    cR = np.zeros((128, 32), np.float32)
    cR[0:64, 0] = mu[768:832]; cR[0:64, 1] = mu[832:896]; cR[:, 2] = mu[896:1024]
    for i, h in enumerate(hs):
        b = 4 + 12 * i
        s = slice(h * 64, (h + 1) * 64)
        cR[0:64, b + 0] = mu[0:256][s]; cR[0:64, b + 1] = mu[256:512][s]; cR[0:64, b + 2] = mu[512:768][s]
        cR[0:64, b + 3] = prm["rwkv_w0"][s]; cR[0:64, b + 4] = prm["rwkv_a0"][s]
        cR[0:64, b + 5] = prm["rwkv_k_k"][s]; cR[0:64, b + 6] = prm["rwkv_k_a"][s]
        cR[0:64, b + 8] = prm["rwkv_r_k"][h]; cR[0:64, b + 9] = prm["rwkv_ln_w"][s]; cR[0:64, b + 10] = prm["rwkv_ln_b"][s]
    t = np.arange(128)
    su = (t[None, :] > t[:, None]).astype(np.float32)
    iu = (t[None, :] >= t[:, None]).astype(np.float32)
    slm = (t[:, None] > t[None, :]).astype(np.float32)
    rmask = np.ones((64, 512), np.float32); rmask[:, ::128] = 0.0
    return {"rkv": np.ascontiguousarray(rkv), "lw": fm(zr[:, 768:832]), "la": fm(zr[:, 832:896]), "lg": fm(zr[:, 896:1024]),
            "cR": cR,
            "w2": np.ascontiguousarray(np.stack([prm["rwkv_w2"][:, h * 64:(h + 1) * 64] for h in hs])),
            "a2": np.ascontiguousarray(np.stack([prm["rwkv_a2"][:, h * 64:(h + 1) * 64] for h in hs])),
            "g2": np.ascontiguousarray(np.stack([prm["rwkv_g2"][:, h * 64:(h + 1) * 64] for h in hs])),
            "msk": np.concatenate([su, iu, slm], 1), "idn": np.eye(128, dtype=np.float32), "rmask": rmask}


def build_C1():
    nc = new_nc()
    P = Prog(nc)
    TOK = 4096
    n = 256
    M = 256
    hT = P.dram("hT", [D, TOK], F32, kind="ExternalInput")
    yT = P.dram("yT", [D, TOK], F32, kind="ExternalInput")
    memT = P.dram("memT", [D, M], F32, kind="ExternalInput")
    wd = {k: P.dram(k, [D, D], F32, kind="ExternalInput") for k in ("w_out", "wq", "wk", "wv", "wo")}
    cC = P.dram("cC", [128, 20], F32, kind="ExternalInput")
    h2T = P.dram("h2T", [D, TOK], F32, kind="ExternalOutput")
    cs = P.sb("cs", [128, 20]); P.dma("sp", cs.v, cC.v)
    ones = P.sb("ones", [128, 128]); P.memset(ones.v, 1.0)
    onesb = P.sb("onesb", [128, 128], BF16); P.memset(onesb.v, 1.0)
    stages = Rot([P.sb("stg", [128, 1024]) for _ in range(2)])
    bufA = P.sb("bufA", [128, 8, D], BF16); bufB = P.sb("bufB", [128, 8, D], BF16); bufC = P.sb("bufC", [128, 8, D], BF16)
    PS = Rot([P.ps("ps", [128, 512]) for _ in range(7)])
    ps_ss = P.ps("ps_ss", [128, 512])
    sqrot = Rot([P.sb("sq", [128, n]) for _ in range(2)])
    rstd = P.sb("rstd", [128, n]); tmp = P.sb("tmp", [128, n])
    flip = [0]

    def ev():
        flip[0] += 1
        return "dve" if flip[0] % 2 else "act"

    load_weight_bf(P, wd["wk"], D, D, bufA, gain=cs[:, 8:16], stages=stages, ncol=1024)
    load_weight_bf(P, wd["wv"], D, D, bufB, gain=cs[:, 8:16], stages=stages, ncol=1024)
    load_weight_bf(P, wd["wq"], D, D, bufC, gain=cs[:, 0:8], stages=stages, ncol=1024)
    qraws = Rot([P.sb("qraw", [128, 8, n]) for _ in range(2)])
    mt = qraws.tiles[0]; P.dma("sp", mt.v, memT.v.rearrange("(c p) m -> p c m", p=128))
    memn = P.sb("memn", [128, 8, M], BF16)
    rms_rstd(P, [mt[:, c, :] for c in range(8)], 128, M, 1.0 / D, EPS, ones.v, sqrot, ps_ss, rstd[:, 0:M], tmp)
    for c in range(8):
        P.tt(memn[:, c, :], mt[:, c, :], rstd[:, 0:M], ALU.mult, eng="dve" if c % 2 else "pool")
    kraw = qraws.tiles[1]; kn = P.sb("kn", [128, 8, M], BF16); vb = P.sb("vb", [128, 2, D], BF16)
    for oc in range(8):
        ps = PS.next()
        for kc in range(8):
            P.matmul(ps[:, 0:M], bufA[:, kc, oc * 128:(oc + 1) * 128], memn[:, kc, :], start=(kc == 0), stop=(kc == 7))
        P.copy(kraw[:, oc, :], ps[:, 0:M], eng=ev())
    for hd in range(4):
        rms_rstd(P, [kraw[:, 2 * hd, :], kraw[:, 2 * hd + 1, :]], 128, M, 1.0 / 256, EPS, ones.v, sqrot, ps_ss, rstd[:, 0:M], tmp)
        for dc in range(2):
            P.stt(kn[:, 2 * hd + dc, :], kraw[:, 2 * hd + dc, :], cs[:, 18 + dc:19 + dc], rstd[:, 0:M], ALU.mult, ALU.mult)
    for mc in range(2):
        for half in range(2):
            ps = PS.next()
            for kc in range(8):
                P.matmul(ps.v, memn[:, kc, mc * 128:(mc + 1) * 128], bufB[:, kc, half * 512:(half + 1) * 512],
                         start=(kc == 0), stop=(kc == 7))
            P.copy(vb[:, mc, half * 512:(half + 1) * 512], ps.v, eng=ev())
    load_weight_bf(P, wd["w_out"], D, D, bufA, gain=None, stages=stages, ncol=1024)
    load_weight_bf(P, wd["wo"], D, D, bufB, gain=None, stages=stages, ncol=1024)
    hts = Rot([P.sb("ht", [128, 8, n]) for _ in range(3)])
    yts = Rot([P.sb("yt", [128, 8, n]) for _ in range(2)])
    ybfs = Rot([P.sb("ybf", [128, 8, n], BF16) for _ in range(2)]); xns = Rot([P.sb("xn", [128, 8, n], BF16) for _ in range(2)])
    qns = Rot([P.sb("qn", [128, 8, n], BF16) for _ in range(2)])
    obs = Rot([P.sb("ob", [128, 8, n], BF16) for _ in range(2)])
    pTs = Rot([P.sb("pT", [128, 2 * n], BF16) for _ in range(8)])
    rs = Rot([P.sb("rs", [128, n]) for _ in range(8)])
    sqrot4 = Rot([P.sb("sq4", [128, n]) for _ in range(4)])
    tmp4 = Rot([P.sb("tmp4", [128, n]) for _ in range(4)]); rstd4 = Rot([P.sb("rstd4", [128, n]) for _ in range(4)])
    rstdT = Rot([P.sb("rstdT", [128, n]) for _ in range(2)])

    def tile_gen(ti):
        t0 = ti * n
        ht = hts.next(); yt = yts.next(); ybf = ybfs.next(); xn = xns.next(); qraw = qraws.next(); qn = qns.next()
        ob = obs.next(); rstd_t = rstdT.next()
        P.dma("sp", ht.v, hT[:, t0:t0 + n].rearrange("(c p) n -> p c n", p=128))
        P.dma("pool", yt.v, yT[:, t0:t0 + n].rearrange("(c p) n -> p c n", p=128))
        for c in range(8):
            P.copy(ybf[:, c, :], yt[:, c, :], eng="pool" if c % 4 == 3 else ("act" if c % 2 else "dve"))
        yield
        for oc in range(8):
            ps = PS.next()
            for kc in range(8):
                P.matmul(ps[:, 0:n], bufA[:, kc, oc * 128:(oc + 1) * 128], ybf[:, kc, :], start=(kc == 0), stop=(kc == 7))
            P.tt(ht[:, oc, :], ps[:, 0:n], ht[:, oc, :], ALU.add)
        rms_rstd(P, [ht[:, c, :] for c in range(8)], 128, n, 1.0 / D, EPS, ones.v, sqrot, ps_ss, rstd_t.v, tmp)
        for c in range(8):
            P.tt(xn[:, c, :], ht[:, c, :], rstd_t.v, ALU.mult, eng="pool" if c % 4 == 3 else "dve")
        yield
        for oc in range(8):
            ps = PS.next()
            for kc in range(8):
                P.matmul(ps[:, 0:n], bufC[:, kc, oc * 128:(oc + 1) * 128], xn[:, kc, :], start=(kc == 0), stop=(kc == 7))
            P.copy(qraw[:, oc, :], ps[:, 0:n], eng=ev())
        hps = []
        for hd in range(4):
            ps = PS.next()
            for dc in range(2):
                sq = sqrot4.next()
                P.act(sq.v, qraw[:, 2 * hd + dc, :], AF.Square)
                P.matmul(ps[:, 0:n], ones.v, sq.v, start=(dc == 0), stop=(dc == 1))
            hps.append(ps)
        tms = []
        for hd in range(4):
            t_ = tmp4.next()
            P.act(t_.v, hps[hd][:, 0:n], AF.Ln, bias=EPS, scale=1.0 / 256)
            tms.append(t_)
        rsq = []
        for hd in range(4):
            r_ = rstd4.next()
            P.act(r_.v, tms[hd].v, AF.Exp, scale=-0.5)
            rsq.append(r_)
        for hd in range(4):
            for dc in range(2):
                P.stt(qn[:, 2 * hd + dc, :], qraw[:, 2 * hd + dc, :], cs[:, 16 + dc:17 + dc], rsq[hd].v, ALU.mult, ALU.mult)
        yield
        ptl = []
        for hd in range(4):
            ps = PS.next()
            for mc in range(2):
                for dc in range(2):
                    P.matmul(ps[:, mc * n:(mc + 1) * n], kn[:, 2 * hd + dc, mc * 128:(mc + 1) * 128], qn[:, 2 * hd + dc, :],
                             start=(mc == 0 and dc == 0), stop=(dc == 1), skip_group_check=True)
            pt = pTs.next()
            P.act(pt.v, ps.v, AF.Exp, scale=1.0 / 16)
            ptl.append(pt)
        yield
        rsl = []
        for hd in range(4):
            ps = PS.next()
            for mc in range(2):
                P.matmul(ps[:, 0:n], onesb.v, ptl[hd][:, mc * n:(mc + 1) * n], start=(mc == 0), stop=(mc == 1))
            r = rs.next()
            P.recip(r.v, ps[:, 0:n])
            rsl.append(r)
        yield
        for hd in range(4):
            ps = PS.next()
            for dc in range(2):
                for mc in range(2):
                    P.matmul(ps[:, dc * n:(dc + 1) * n], vb[:, mc, (2 * hd + dc) * 128:(2 * hd + dc + 1) * 128],
                             ptl[hd][:, mc * n:(mc + 1) * n], start=(dc == 0 and mc == 0), stop=(mc == 1),
                             skip_group_check=True)
            for dc in range(2):
                P.tt(ob[:, 2 * hd + dc, :], ps[:, dc * n:(dc + 1) * n], rsl[hd].v, ALU.mult)
        yield
        for oc in range(8):
            ps = PS.next()
            for kc in range(8):
                P.matmul(ps[:, 0:n], bufB[:, kc, oc * 128:(oc + 1) * 128], ob[:, kc, :], start=(kc == 0), stop=(kc == 7))
            P.tt(ht[:, oc, :], ps[:, 0:n], ht[:, oc, :], ALU.add)
        P.dma("sp", h2T[:, t0:t0 + n].rearrange("(c p) n -> p c n", p=128), ht.v)

    ntl = TOK // n
    live = []
    nxt_t = 0
    while live or nxt_t < ntl:
        while len(live) < 2 and nxt_t < ntl:
            live.append(tile_gen(nxt_t)); nxt_t += 1
        for g_ in list(live):
            try:
                next(g_)
            except StopIteration:
                live.remove(g_)
    P.emit()
    return nc


def vec_pc(v):
    return np.ascontiguousarray(np.asarray(v, np.float32).reshape(-1, 128).T)


def prep_C1(hT, yT, mem_b, prm):
    cC = np.concatenate([vec_pc(prm["xa_norm_g"]), vec_pc(prm["mem_norm_g"]), vec_pc(prm["xa_q_norm"]),
                         vec_pc(prm["xa_k_norm"])], 1)
    return {"hT": hT, "yT": yT, "memT": np.ascontiguousarray(mem_b.T), "w_out": prm["w_out"], "wq": prm["xa_wq"],
            "wk": prm["xa_wk"], "wv": prm["xa_wv"], "wo": prm["xa_wo"], "cC": np.ascontiguousarray(cC)}


def build_C2():
    nc = new_nc()
    P = Prog(nc)
    TOK = 4096
    n = 256
    NF = D_FF // 128
    h2T = P.dram("h2T", [D, 2 + TOK], F32, kind="ExternalInput")
    w_up = P.dram("w_up", [D, 2 * D_FF], F32, kind="ExternalInput")
    w_dn = P.dram("w_dn", [D_FF, D], F32, kind="ExternalInput")
    cF = P.dram("cF", [128, 8 + NF * 4], F32, kind="ExternalInput")
    h3T = P.dram("h3T", [D, TOK], F32, kind="ExternalOutput")
    cs = P.sb("cs", [128, 8 + NF * 4]); P.dma("sp", cs.v, cF.v)
    ones = P.sb("ones", [128, 128]); P.memset(ones.v, 1.0)
    stages = Rot([P.sb("stg", [128, 1408]) for _ in range(3)])
    wup = P.sb("wup", [128, 8, 2 * D_FF], BF16)
    wdn = P.sb("wdn", [128, NF, D], BF16)
    load_weight_bf(P, w_up, D, 2 * D_FF, wup, gain=cs[:, 0:8], stages=stages, ncol=1408)
    load_weight_bf(P, w_dn, D_FF, D, wdn, gain=None, stages=stages, ncol=1024)
    PS = Rot([P.ps("ps", [128, 512]) for _ in range(7)])
    ps_ss = P.ps("ps_ss", [128, 512])
    sqrot = Rot([P.sb("sq", [128, n]) for _ in range(2)])
    rstd = P.sb("rstd", [128, n]); tmp = P.sb("tmp", [128, n])
    hts = Rot([P.sb("ht", [128, 8, n]) for _ in range(3)])
    xns = Rot([P.sb("xn", [128, 8, n], BF16) for _ in range(2)])
    asb = Rot([P.sb("asb", [128, 2 + n]) for _ in range(3)])
    cc = Rot([P.sb("cc", [128, n]) for _ in range(2)])
    gl = Rot([P.sb("gl", [128, n]) for _ in range(2)])
    gbf = [P.sb("gbf", [128, n], BF16) for _ in range(NF)]
    aprev = P.sb("aprev", [128, NF, 2])

    def norm(c0, w):
        ht = hts.next(); xn = xns.next()
        P.dma("sp", ht[:, :, 0:w], h2T[:, c0:c0 + w].rearrange("(c p) n -> p c n", p=128))
        rms_rstd(P, [ht[:, c, 0:w] for c in range(8)], 128, w, 1.0 / D, EPS, ones.v, sqrot, ps_ss, rstd[:, 0:w], tmp)
        for c in range(8):
            P.tt(xn[:, c, 0:w], ht[:, c, 0:w], rstd[:, 0:w], ALU.mult, eng="dve" if c % 2 else "pool")
        return (c0, w, ht, xn)

    def body(cur, full, nxt):
        c0, w, ht, xn = cur
        res = None
        for f in range(NF):
            psa = PS.next()
            for kc in range(8):
                P.matmul(psa[:, 0:w], wup[:, kc, f * 128:(f + 1) * 128], xn[:, kc, 0:w], start=(kc == 0), stop=(kc == 7))
            if not full:
                P.copy(aprev[:, f, :], psa[:, 0:2], eng="act")
                continue
            psb = PS.next()
            for kc in range(8):
                P.matmul(psb[:, 0:w], wup[:, kc, D_FF + f * 128:D_FF + (f + 1) * 128], xn[:, kc, 0:w],
                         start=(kc == 0), stop=(kc == 7))
            if f == 3 and nxt is not None:
                res = norm(*nxt)
            a = asb.next()
            P.copy(a[:, 2:2 + w], psa[:, 0:w], eng="act")
            P.copy(a[:, 0:2], aprev[:, f, :], eng="pool")
            P.copy(aprev[:, f, :], a[:, w:w + 2], eng="pool")
            cb = 8 + f * 4
            c = cc.next()
            P.ts(c.v, a[:, 2:2 + w], cs[:, cb + 2:cb + 3], ALU.mult, cs[:, cb + 3:cb + 4], ALU.add)
            P.stt(c.v, a[:, 1:1 + w], cs[:, cb + 1:cb + 2], c.v, ALU.mult, ALU.add)
            P.stt(c.v, a[:, 0:w], cs[:, cb + 0:cb + 1], c.v, ALU.mult, ALU.add)
            g = gl.next()
            P.act(g.v, c.v, AF.Gelu)
            P.tt(gbf[f].v, psb[:, 0:w], g.v, ALU.mult)
        if not full:
            return norm(*nxt) if nxt is not None else None
        for oc in range(8):
            ps = PS.next()
            for f in range(NF):
                P.matmul(ps[:, 0:w], wdn[:, f, oc * 128:(oc + 1) * 128], gbf[f].v, start=(f == 0), stop=(f == NF - 1))
            P.tt(ht[:, oc, :], ps[:, 0:w], ht[:, oc, :], ALU.add)
        P.dma("pool", h3T[:, c0 - 2:c0 - 2 + w].rearrange("(c p) n -> p c n", p=128), ht.v)
        return res

    cur = norm(0, 2)
    ntl = TOK // n
    cur = body(cur, False, (2, n))
    for ti in range(ntl):
        nxt = (2 + (ti + 1) * n, n) if ti + 1 < ntl else None
        cur = body(cur, True, nxt)
    P.emit()
    return nc


def prep_C2(h2T_halo, prm):
    NF = D_FF // 128
    cw = prm["ffn_conv_w"]
    cols = [vec_pc(prm["ffn_norm_g"])]
    per = np.zeros((128, NF, 4), np.float32)
    for j in range(3):
        per[:, :, j] = cw[j].reshape(NF, 128).T
    per[:, :, 3] = prm["ffn_conv_b"].reshape(NF, 128).T
    cF = np.concatenate([cols[0], per.reshape(128, NF * 4)], 1)
    return {"h2T": h2T_halo, "w_up": prm["ffn_w_up"], "w_dn": prm["ffn_w_down"], "cF": np.ascontiguousarray(cF)}


_NC_CACHE = {}


def _get_nc(name):
    if name not in _NC_CACHE:
        _NC_CACHE[name] = {"A": build_A, "Bp": build_Bp, "Bd": build_Bd, "Br": build_Br, "C1": build_C1,
                           "C2": build_C2}[name]()
    return _NC_CACHE[name]


def _run(name, in_maps):
    nc = _get_nc(name)
    res = run_bass_kernel_spmd(nc, in_maps, core_ids=list(range(8)))
    return res.results


PARAM_KEYS = ("mix_norm_g", "w_in", "pool_w", "pool_scale", "rwkv_mu", "rwkv_w0", "rwkv_w2", "rwkv_a0", "rwkv_a2",
              "rwkv_g2", "rwkv_k_k", "rwkv_k_a", "rwkv_r_k", "rwkv_ln_w", "rwkv_ln_b", "diff_q_norm", "diff_k_norm",
              "diff_lq1", "diff_lk1", "diff_lq2", "diff_lk2", "diff_subln", "w_out", "xa_norm_g", "mem_norm_g",
              "xa_wq", "xa_wk", "xa_wv", "xa_wo", "xa_q_norm", "xa_k_norm", "ffn_norm_g", "ffn_w_up", "ffn_conv_w",
              "ffn_conv_b", "ffn_w_down")


def kernel(x, mem, **params):
    x = np.asarray(x, np.float32)
    mem = np.asarray(mem, np.float32)
    B, S, _ = x.shape
    TOK = S // 2
    depth = np.asarray(params["w_in"]).shape[0]
    hT = [np.ascontiguousarray(x[c // 2, (c % 2) * TOK:(c % 2 + 1) * TOK].T) for c in range(8)]
    for l in range(depth):
        prm = {k: np.ascontiguousarray(np.asarray(params[k][l], np.float32)) for k in PARAM_KEYS}
        cA = vec_pc(prm["mix_norm_g"])
        rA = _run("A", [{"hT": hT[c], "w_in": prm["w_in"], "cA": cA} for c in range(8)])
        zb = [np.concatenate([rA[2 * b]["zT"].T, rA[2 * b + 1]["zT"].T], axis=0) for b in range(B)]
        del rA
        rP = _run("Bp", [prep_Bp(zb[c // 2][:, 0:256], c % 2, prm["pool_w"], prm["pool_scale"]) for c in range(8)])
        rR = _run("Br", [prep_Br(zb[c // 2], c % 2, prm) for c in range(8)])
        rD = _run("Bd", [prep_Bd(zb[c // 2], c % 2, l, prm) for c in range(8)])
        del zb
        yT = []
        for c in range(8):
            b, hh = c // 2, c % 2
            y = np.empty((D, TOK), np.float32)
            y[0:256] = rP[c]["ypT"]
            for h2 in range(2):
                cc = 2 * b + h2
                y[256 + h2 * 128:256 + (h2 + 1) * 128] = rR[cc]["yr"][:, hh * TOK:(hh + 1) * TOK]
                for i_, hd_ in enumerate(diff_heads(h2)):
                    y[512 + hd_ * 128:512 + (hd_ + 1) * 128] = rD[cc]["yd"][hh * TOK:(hh + 1) * TOK, i_ * 128:(i_ + 1) * 128].T
            yT.append(y)
        del rP, rR, rD
        r1 = _run("C1", [prep_C1(hT[c], yT[c], mem[c // 2], prm) for c in range(8)])
        del yT
        h2 = [r1[c]["h2T"] for c in range(8)]
        del r1
        ins = []
        for c in range(8):
            hh = c % 2
            hal = np.zeros((D, 2 + TOK), np.float32)
            hal[:, 2:] = h2[c]
            if hh == 1:
                hal[:, 0:2] = h2[c - 1][:, TOK - 2:TOK]
            ins.append(prep_C2(hal, prm))
        r2 = _run("C2", ins)
        hT = [np.ascontiguousarray(r2[c]["h3T"]) for c in range(8)]
        del r2, h2, ins
    out = np.empty((B, S, D), np.float32)
    for c in range(8):
        out[c // 2, (c % 2) * TOK:(c % 2 + 1) * TOK] = hT[c].T
    return out
```

```python
import math
import numpy as np
from contextlib import ExitStack
import concourse.bass as bass
import concourse.mybir as mybir
from concourse.bass_utils import run_bass_kernel_spmd

F32 = mybir.dt.float32
BF16 = mybir.dt.bfloat16
AF = mybir.ActivationFunctionType
ALU = mybir.AluOpType
AX = mybir.AxisListType

ENGS = ("pe", "act", "dve", "pool", "sp")
SAME_ENG_SYNC = True


class T:
    def __init__(self, P, h, name, space):
        self.P = P
        self.h = h
        self.name = name
        self.space = space
        self.w = None
        self.r = {}
        self.dsem = None
        self.dcnt = 0

    def __getitem__(self, idx):
        return V(self, self.h[idx])

    @property
    def v(self):
        return V(self, self.h[:])


class V:
    __slots__ = ("t", "ap")

    def __init__(self, t, ap):
        self.t = t
        self.ap = ap

    def __getitem__(self, idx):
        return V(self.t, self.ap[idx])

    def rearrange(self, pat, **kw):
        return V(self.t, self.ap.rearrange(pat, **kw))

    def bitcast(self, dt):
        return V(self.t, self.ap.bitcast(dt))


class Prog:
    def __init__(self, nc):
        self.nc = nc
        self.es = ExitStack()
        self.streams = {e: [] for e in ENGS}
        self.sems = {}
        self.cnt = {e: 0 for e in ENGS}
        self.known = {e: {} for e in ENGS}
        self.out_tokens = []
        self.ntile = 0
        for e in ("pe", "act", "dve", "pool"):
            self.sems[e] = self.es.enter_context(nc.semaphore("s_" + e))

    def sb(self, name, shape, dt=F32):
        self.ntile += 1
        h = self.es.enter_context(self.nc.sbuf_tensor(f"{name}_{self.ntile}", list(shape), dt))
        return T(self, h, name, "sb")

    def ps(self, name, shape, dt=F32):
        self.ntile += 1
        h = self.es.enter_context(self.nc.psum_tensor(f"{name}_{self.ntile}", list(shape), dt))
        return T(self, h, name, "ps")

    def dram(self, name, shape, dt=F32, kind="Internal"):
        h = self.nc.dram_tensor(name, list(shape), dt, kind=kind)
        return T(self, h.ap(), name, "dram")

    def scope(self):
        return _Scope(self)

    def _need(self, eng, tok, waits):
        if tok is None:
            return
        k, v = tok
        if k == eng and (eng == "pe" or not SAME_ENG_SYNC):
            return
        if self.known[eng].get(k, 0) >= v:
            return
        waits[k] = max(waits.get(k, 0), v)

    def _deps(self, eng, reads, writes):
        waits = {}
        for t in reads:
            self._need(eng, t.w, waits)
        for t in writes:
            self._need(eng, t.w, waits)
            for k, v in t.r.items():
                self._need(eng, (k, v), waits)
        for k, v in waits.items():
            self.known[eng][k] = v
        return list(waits.items())

    def _mark(self, tok, reads, writes):
        k, v = tok
        for t in writes:
            t.w = tok
            t.r = {}
        for t in reads:
            if t in writes:
                continue
            t.r[k] = max(t.r.get(k, 0), v)

    def op(self, eng, fn, reads=(), writes=()):
        reads = [x.t if isinstance(x, V) else x for x in reads]
        writes = [x.t if isinstance(x, V) else x for x in writes]
        waits = self._deps(eng, reads, writes)
        self.cnt[eng] += 1
        tok = (eng, self.cnt[eng])
        self.streams[eng].append((waits, fn, eng, 1))
        self._mark(tok, reads, writes)
        return tok

    def dma(self, q, out, in_, **kw):
        sbt = out.t if out.t.space != "dram" else in_.t
        if sbt.dsem is None:
            sbt.dsem = "d%d" % len(self.sems)
            self.sems[sbt.dsem] = self.es.enter_context(self.nc.semaphore(sbt.dsem))
        waits = self._deps(q, [in_.t], [out.t])
        sbt.dcnt += 16
        tok = (sbt.dsem, sbt.dcnt)
        oap, iap = out.ap, in_.ap
        self.streams[q].append((waits, lambda e: e.dma_start(out=oap, in_=iap, **kw), sbt.dsem, 16))
        self._mark(tok, [in_.t], [out.t])
        if out.t.space == "dram":
            self.out_tokens.append(tok)
        return tok

    def emit(self):
        nc = self.nc
        fin = {}
        for k, v in self.out_tokens:
            fin[k] = max(fin.get(k, 0), v)
        for e in ("pe", "act", "dve", "pool"):
            if self.cnt[e]:
                fin[e] = self.cnt[e]
        final_waits = list(fin.items())
        sems = self.sems
        streams = self.streams

        def replay(e, name):
            eng = {"pe": nc.tensor, "act": nc.scalar, "dve": nc.vector, "pool": nc.gpsimd, "sp": nc.sync}[name]
            for waits, fn, isem, iv in streams[name]:
                for k, v in waits:
                    eng.wait_ge(sems[k], v)
                ins = fn(eng)
                ins.then_inc(sems[isem], iv)
            if name == "sp":
                for k, v in final_waits:
                    eng.wait_ge(sems[k], v)

        with nc.Block() as block:
            @block.sync
            def _(e):
                replay(e, "sp")

            @block.tensor
            def _(e):
                replay(e, "pe")

            @block.scalar
            def _(e):
                replay(e, "act")

            @block.vector
            def _(e):
                replay(e, "dve")

            @block.gpsimd
            def _(e):
                replay(e, "pool")
        self.es.close()

    def matmul(self, out, lhsT, rhs, start=True, stop=True, **kw):
        o, l, r = out.ap, lhsT.ap, rhs.ap
        return self.op("pe", lambda e: e.matmul(o, l, r, start=start, stop=stop, **kw),
                       reads=[lhsT, rhs], writes=[out])

    def transpose(self, out, in_, ident):
        o, i, d = out.ap, in_.ap, ident.ap
        return self.op("pe", lambda e: e.transpose(o, i, d), reads=[in_, ident], writes=[out])

    def act(self, out, in_, func, bias=None, scale=1.0, accum_out=None, eng="act"):
        o, i = out.ap, in_.ap
        reads = [in_]
        writes = [out]
        kw = {}
        if bias is not None:
            if isinstance(bias, V):
                reads.append(bias)
                kw["bias"] = bias.ap
            else:
                kw["bias"] = bias
        if isinstance(scale, V):
            reads.append(scale)
            kw["scale"] = scale.ap
        else:
            kw["scale"] = scale
        if accum_out is not None:
            writes.append(accum_out)
            kw["accum_out"] = accum_out.ap
        return self.op(eng, lambda e: e.activation(o, i, func, **kw), reads=reads, writes=writes)

    def tt(self, out, in0, in1, op, eng="dve"):
        o, a, b = out.ap, in0.ap, in1.ap
        return self.op(eng, lambda e: e.tensor_tensor(o, a, b, op), reads=[in0, in1], writes=[out])

    def ts(self, out, in0, s1, op0, s2=None, op1=None, accum_out=None, eng="dve"):
        o, a = out.ap, in0.ap
        reads = [in0]
        writes = [out]
        s1a = s1.ap if isinstance(s1, V) else s1
        s2a = s2.ap if isinstance(s2, V) else s2
        if isinstance(s1, V):
            reads.append(s1)
        if isinstance(s2, V):
            reads.append(s2)
        kw = {}
        if op1 is not None:
            kw["op1"] = op1
        if accum_out is not None:
            writes.append(accum_out)
            kw["accum_out"] = accum_out.ap
        return self.op(eng, lambda e: e.tensor_scalar(o, a, s1a, s2a, op0, **kw), reads=reads, writes=writes)

    def stt(self, out, in0, scalar, in1, op0, op1, eng="dve"):
        o, a, b = out.ap, in0.ap, in1.ap
        reads = [in0, in1]
        sa = scalar.ap if isinstance(scalar, V) else scalar
        if isinstance(scalar, V):
            reads.append(scalar)
        return self.op(eng, lambda e: e.scalar_tensor_tensor(o, a, sa, b, op0, op1), reads=reads, writes=[out])

    def copy(self, out, in_, eng="dve"):
        o, i = out.ap, in_.ap
        if eng == "act":
            return self.op("act", lambda e: e.copy(o, i), reads=[in_], writes=[out])
        return self.op(eng, lambda e: e.tensor_copy(o, i), reads=[in_], writes=[out])

    def memset(self, out, val, eng="dve"):
        o = out.ap
        return self.op(eng, lambda e: e.memset(o, val), reads=[], writes=[out])

    def reduce(self, out, in_, op=None, axis=None, eng="dve"):
        o, i = out.ap, in_.ap
        op = op or ALU.add
        axis = axis or AX.X
        return self.op(eng, lambda e: e.tensor_reduce(o, i, axis, op), reads=[in_], writes=[out])


class _Scope:
    def __init__(self, P):
        self.P = P

    def __enter__(self):
        self.saved = self.P.es
        self.P.es = ExitStack()
        return self

    def __exit__(self, *a):
        self.P.es.close()
        self.P.es = self.saved
        return False


def _recip(self, out, in_):
    o, i = out.ap, in_.ap
    return self.op("dve", lambda e: e.reciprocal(o, i), reads=[in_], writes=[out])


Prog.recip = _recip


Prog.recip = _recip

D = 1024
NT = 512
P_IN = 2816
D_FF = 2816
EPS = 1e-6


def new_nc():
    return bass.Bass("TRN2", target_bir_lowering=False)


class Rot:
    def __init__(self, tiles):
        self.tiles = tiles
        self.i = 0

    def next(self):
        t = self.tiles[self.i % len(self.tiles)]
        self.i += 1
        return t


def load_weight_bf(P, wd, K, N, dst, gain=None, stages=None, ncol=2816, q=("sp", "act", "pool"),
                   engs=("dve", "act", "dve", "act", "pool")):
    i = 0
    for kc in range(K // 128):
        for c0 in range(0, N, ncol):
            cw = min(ncol, N - c0)
            st = stages.next()
            P.dma(q[i % len(q)], st[:, 0:cw], wd[kc * 128:(kc + 1) * 128, c0:c0 + cw])
            eng = engs[i % len(engs)]
            if gain is not None:
                if eng == "act":
                    P.act(dst[:, kc, c0:c0 + cw], st[:, 0:cw], AF.Copy, scale=gain[:, kc:kc + 1])
                else:
                    P.ts(dst[:, kc, c0:c0 + cw], st[:, 0:cw], gain[:, kc:kc + 1], ALU.mult, eng=eng)
            else:
                P.copy(dst[:, kc, c0:c0 + cw], st[:, 0:cw], eng=eng)
            i += 1


def rms_rstd(P, chunks, rows, n, inv_d, eps, ones, sqrot, ps, rstd, tmp):
    nch = len(chunks)
    for i, ch in enumerate(chunks):
        sq = sqrot.next()
        P.act(sq[0:rows, 0:n], ch, AF.Square)
        P.matmul(ps[:, 0:n], ones[0:rows, :], sq[0:rows, 0:n], start=(i == 0), stop=(i == nch - 1))
    M = ones.ap.shape[-1] if hasattr(ones.ap, "shape") else 128
    P.act(tmp[:, 0:n], ps[:, 0:n], AF.Ln, bias=eps, scale=inv_d)
    P.act(rstd, tmp[:, 0:n], AF.Exp, scale=-0.5)


def build_A():
    nc = new_nc()
    P = Prog(nc)
    TOK = 4096
    hT = P.dram("hT", [D, TOK], F32, kind="ExternalInput")
    w_in = P.dram("w_in", [D, P_IN], F32, kind="ExternalInput")
    cA = P.dram("cA", [128, 8], F32, kind="ExternalInput")
    zT = P.dram("zT", [P_IN, TOK], F32, kind="ExternalOutput")
    consts = P.sb("consts", [128, 8])
    P.dma("sp", consts.v, cA.v)
    ones = P.sb("ones", [128, 128])
    P.memset(ones.v, 1.0)
    wbf = P.sb("wbf", [128, 8, P_IN], BF16)
    stages = Rot([P.sb("stg", [128, 2816]) for _ in range(2)])
    load_weight_bf(P, wd=w_in, K=D, N=P_IN, dst=wbf, gain=consts, stages=stages)
    hts = Rot([P.sb("ht", [128, 8, NT]) for _ in range(2)])
    sqrot = Rot([P.sb("sq", [128, NT]) for _ in range(2)])
    ps_ss = P.ps("ps_ss", [128, NT])
    ps_z = Rot([P.ps("ps_z", [128, NT]) for _ in range(4)])
    rstd = P.sb("rstd", [128, NT])
    tmp = P.sb("tmp", [128, NT])
    xn = Rot([P.sb("xn", [128, 8, NT], BF16) for _ in range(2)])
    zo = Rot([P.sb("zo", [128, NT]) for _ in range(4)])

    def norm(ti):
        t0 = ti * NT
        ht = hts.next()
        P.dma("sp", ht.v, hT[:, t0:t0 + NT].rearrange("(c p) n -> p c n", p=128))
        rms_rstd(P, [ht[:, c, :] for c in range(8)], 128, NT, 1.0 / D, EPS, ones.v, sqrot, ps_ss, rstd.v, tmp)
        x = xn.next()
        for c in range(8):
            P.tt(x[:, c, :], ht[:, c, :], rstd.v, ALU.mult, eng="dve" if c % 2 == 0 else "pool")
        return x

    ntl = TOK // NT
    x = norm(0)
    for ti in range(ntl):
        t0 = ti * NT
        xnext = None
        for oc in range(P_IN // 128):
            ps = ps_z.next()
            for kc in range(8):
                P.matmul(ps.v, wbf[:, kc, oc * 128:(oc + 1) * 128], x[:, kc, :], start=(kc == 0), stop=(kc == 7))
            if oc == 2 and ti + 1 < ntl:
                xnext = norm(ti + 1)
            o = zo.next()
            if oc % 2 == 0:
                P.copy(o.v, ps.v, eng="act")
            else:
                P.copy(o.v, ps.v, eng="dve")
            P.dma("sp" if oc % 2 == 0 else "pool", zT[oc * 128:(oc + 1) * 128, t0:t0 + NT], o.v)
        x = xnext
    P.emit()
    return nc


POOL_WINDOWS = (2, 4, 8, 16)


def build_Bp():
    nc = new_nc()
    P = Prog(nc)
    TOK = 4096
    H = 16
    zpT = P.dram("zpT", [256, H + TOK], F32, kind="ExternalInput")
    wblk = P.dram("wblk", [2, 128, 128], F32, kind="ExternalInput")
    cP = P.dram("cP", [128, 2 + 2 + 32], F32, kind="ExternalInput")
    ypT = P.dram("ypT", [256, TOK], F32, kind="ExternalOutput")
    cs = P.sb("cs", [128, 36])
    P.dma("sp", cs.v, cP.v)
    wb = [P.sb("wb", [128, 128]) for _ in range(2)]
    for i in range(2):
        P.dma("sp", wb[i].v, wblk[i])
    W = H + NT
    us = Rot([P.sb("u", [128, W]) for _ in range(2)])
    s2 = P.sb("s2", [128, W]); s4 = P.sb("s4", [128, W]); s8 = P.sb("s8", [128, W]); s16 = P.sb("s16", [128, W])
    dd = Rot([P.sb("dd", [128, W]) for _ in range(2)])
    win = P.sb("win", [128, W])
    ps = Rot([P.ps("ps", [128, NT]) for _ in range(2)])
    yo = Rot([P.sb("yo", [128, NT]) for _ in range(2)])
    for ti in range(TOK // NT):
        t0 = ti * NT
        for gp in range(2):
            u = us.next()
            P.dma("sp", u.v, zpT[gp * 128:(gp + 1) * 128, t0:t0 + W])
            P.tt(s2[:, 1:W], u[:, 1:W], u[:, 0:W - 1], ALU.add)
            P.tt(s4[:, 3:W], s2[:, 3:W], s2[:, 1:W - 2], ALU.add, eng="pool")
            if gp == 0:
                lo, hi = s2, s4
            else:
                P.tt(s8[:, 7:W], s4[:, 7:W], s4[:, 3:W - 4], ALU.add)
                P.tt(s16[:, 15:W], s8[:, 15:W], s8[:, 7:W - 8], ALU.add, eng="pool")
                lo, hi = s8, s16
            d = dd.next()
            for (a, b, src) in ((0, 64, lo), (64, 128, hi)):
                wsrc = src
                if ti == 0:
                    P.tt(win[a:b, H:H + 16], src[a:b, H:H + 16], cs[a:b, 4 + gp * 16:4 + gp * 16 + 16], ALU.mult)
                    P.stt(d[a:b, H:H + 16], win[a:b, H:H + 16], cs[a:b, 2 + gp:3 + gp], u[a:b, H:H + 16], ALU.mult, ALU.subtract)
                    P.stt(d[a:b, H + 16:W], src[a:b, H + 16:W], cs[a:b, 2 + gp:3 + gp], u[a:b, H + 16:W], ALU.mult, ALU.subtract)
                else:
                    P.stt(d[a:b, H:W], src[a:b, H:W], cs[a:b, 2 + gp:3 + gp], u[a:b, H:W], ALU.mult, ALU.subtract)
            p = ps.next()
            P.matmul(p.v, wb[gp].v, d[:, H:W])
            y = yo.next()
            P.ts(y.v, p.v, cs[:, gp:gp + 1], ALU.mult)
            P.dma("pool", ypT[gp * 128:(gp + 1) * 128, t0:t0 + NT], y.v)
    P.emit()
    return nc


def prep_Bp(z_pool_b, hh, pool_w, pool_scale):
    TOK = 4096
    zp = np.zeros((256, 16 + TOK), np.float32)
    if hh == 0:
        zp[:, 16:] = z_pool_b[0:TOK].T
    else:
        zp[:, :] = z_pool_b[TOK - 16:2 * TOK].T
    wblk = np.zeros((2, 128, 128), np.float32)
    for g in range(4):
        i, o = g // 2, (g % 2) * 64
        wblk[i, o:o + 64, o:o + 64] = pool_w[g]
    cP = np.zeros((128, 36), np.float32)
    for gp in range(2):
        cP[:, gp] = pool_scale[gp * 128:(gp + 1) * 128]
        for half in range(2):
            w = POOL_WINDOWS[gp * 2 + half]
            cP[half * 64:(half + 1) * 64, 2 + gp] = 1.0 / w
            t = np.arange(16)
            corr = (w / np.minimum(t + 1, w)) if hh == 0 else np.ones(16)
            cP[half * 64:(half + 1) * 64, 4 + gp * 16:4 + gp * 16 + 16] = corr[None, :]
    return {"zpT": zp, "wblk": wblk, "cP": cP}


SEQ = 8192
ALIBI = [2.0 ** (-8.0 * (h + 1) / 4) for h in range(4)]
NEG = -30000.0
ALIBI_WIN = 12


def diff_heads(hh):
    return [hh, 3 - hh]


def build_Bd():
    nc = new_nc()
    P = Prog(nc)
    S = SEQ
    NKB = S // 128
    qin = P.dram("qin", [2, 2, 64, S], F32, kind="ExternalInput")
    kin = P.dram("kin", [2, 2, 64, S], F32, kind="ExternalInput")
    vin = P.dram("vin", [2, S, 128], F32, kind="ExternalInput")
    qrow = P.dram("qrow", [2, 1, S], BF16, kind="ExternalInput")
    btab = P.dram("btab", [2, 128, 64], F32, kind="ExternalInput")
    dtab = P.dram("dtab", [2, 128, 128], BF16, kind="ExternalInput")
    identb = P.dram("identb", [128, 128], BF16, kind="ExternalInput")
    cD = P.dram("cD", [128, 8], F32, kind="ExternalInput")
    lvec = P.dram("lvec", [128, 4, 64], F32, kind="ExternalInput")
    subln = P.dram("subln", [128, 128], F32, kind="ExternalInput")
    yd = P.dram("yd", [S, 256], F32, kind="ExternalOutput")

    cs = P.sb("cs", [128, 8]); P.dma("sp", cs.v, cD.v)
    lv = P.sb("lv", [128, 4, 64]); P.dma("sp", lv.v, lvec.v)
    sl = P.sb("sl", [128, 128]); P.dma("sp", sl.v, subln.v)
    idb = P.sb("idb", [128, 128], BF16); P.dma("sp", idb.v, identb.v)
    ones = P.sb("ones", [128, 128]); P.memset(ones.v, 1.0)
    lt = P.sb("lt", [128, 64]); e1 = P.sb("e1", [128, 1]); e2 = P.sb("e2", [128, 1]); nlam = P.sb("nlam", [128, 1])
    P.tt(lt.v, lv[:, 0, :], lv[:, 1, :], ALU.mult); P.reduce(e1.v, lt.v); P.act(e1.v, e1.v, AF.Exp)
    P.tt(lt.v, lv[:, 2, :], lv[:, 3, :], ALU.mult); P.reduce(e2.v, lt.v); P.act(e2.v, e2.v, AF.Exp)
    P.tt(nlam.v, e2.v, e1.v, ALU.subtract)
    P.tt(nlam.v, nlam.v, cs[:, 4:5], ALU.subtract)
    sls = P.sb("sls", [128, 128])
    P.ts(sls.v, sl.v, cs[:, 5:6], ALU.mult)

    QT = [P.sb("QT", [65, S], BF16) for _ in range(2)]
    KT = [P.sb("KT", [65, S], BF16) for _ in range(2)]
    VA = P.sb("VA", [128, NKB, 129], BF16)
    BT = P.sb("BT", [128, 64]); DT = P.sb("DT", [128, 128], BF16)
    qst = Rot([P.sb("qst", [64, NT]) for _ in range(8)])
    sqr = Rot([P.sb("sqr", [64, NT]) for _ in range(4)])
    tmpr = Rot([P.sb("tmpr", [64, NT]) for _ in range(4)]); rstdr = Rot([P.sb("rstd", [64, NT]) for _ in range(4)])
    vst = Rot([P.sb("vst", [128, 16, 128]) for _ in range(2)])
    ps_pre = P.ps("ps_pre", [64, NT])
    ps_s = Rot([P.ps("ps_s", [128, 2, NT]) for _ in range(2)])
    accs = [P.ps("acc", [128, 3, 129]) for _ in range(3)]
    pT = Rot([P.sb("pT", [128, 2, NT], BF16) for _ in range(3)])
    accsb = [P.sb("accsb", [128, 3, 129]) for _ in range(3)]
    fin = {k: Rot([P.sb(k, s) for _ in range(2)]) for k, s in
           (("r0", [128, 1]), ("r1", [128, 1]), ("o0", [128, 128]), ("o", [128, 128]), ("sqo", [128, 128]),
            ("ss", [128, 1]), ("out", [128, 128]))}

    def acc_slot(c, j):
        idx = c * 4 + j
        return idx // 3, idx % 3

    for i in range(2):
        P.dma("sp", BT.v, btab[i]); P.dma("sp", DT.v, dtab[i])
        its = []
        for c in range(2):
            P.dma("pool", QT[c][64:65, :], qrow[i])
            P.memset(KT[c][64:65, :], 1.0)
            for (src, dst, gcol) in ((qin, QT[c], c), (kin, KT[c], 2 + c)):
                for ti in range(S // NT):
                    its.append((src, dst, gcol, c, ti * NT))
        def stageA(grp):
            sts = []; pss = []
            ps = ps_s.next()
            for k_, (src, dst, gcol, c, t0) in enumerate(grp):
                st = qst.next()
                P.dma("sp" if k_ % 2 == 0 else "act", st.v, src[i, c, :, t0:t0 + NT])
                sts.append(st)
            for k_, st in enumerate(sts):
                sq = sqr.next()
                P.tt(sq.v, st.v, st.v, ALU.mult, eng="pool" if k_ % 2 else "dve")
                view = ps[0:64, k_, :]
                P.matmul(view, ones[0:64, 0:64], sq.v)
                pss.append(view)
            return grp, sts, pss

        def stageB(ctx):
            grp, sts, pss = ctx
            tms = []
            for view in pss:
                tm_ = tmpr.next()
                P.act(tm_.v, view, AF.Ln, bias=EPS, scale=1.0 / 64)
                tms.append(tm_)
            rss = []
            for tm_ in tms:
                r_ = rstdr.next()
                P.act(r_.v, tm_.v, AF.Exp, scale=-0.5)
                rss.append(r_)
            for (src, dst, gcol, c, t0), st, r_ in zip(grp, sts, rss):
                P.stt(dst[0:64, t0:t0 + NT], st.v, cs[0:64, gcol:gcol + 1], r_.v, ALU.mult, ALU.mult)

        P.memset(VA[:, :, 128:129], 1.0)
        for vc in range(4):
            st = vst.next()
            P.dma("pool", st.v, vin[i, vc * 2048:(vc + 1) * 2048, :].rearrange("(kb p) d -> p kb d", p=128))
            P.copy(VA[:, vc * 16:(vc + 1) * 16, 0:128], st.v, eng="dve" if vc % 2 == 0 else "act")
        groups = [its[g0:g0 + 2] for g0 in range(0, len(its), 2)]
        prev_ctx = stageA(groups[0])
        for grp in groups[1:]:
            cur_ctx = stageA(grp)
            stageB(prev_ctx)
            prev_ctx = cur_ctx
        stageB(prev_ctx)
        for QB in range(S // NT):
            q0 = QB * NT
            tiles = []
            kb_lo = max(0, 4 * QB - ALIBI_WIN) if i == 0 else 0
            for kb in range(kb_lo, 4 * QB + 4):
                i2 = max(0, kb - 4 * QB)
                tiles.append((kb, i2))
            started = set()
            prev = None

            def emit_pv(info):
                kb, i2, pt = info
                for c in range(2):
                    for j in range(i2, 4):
                        bank, slot = acc_slot(c, j)
                        st_flag = bank not in started
                        started.add(bank)
                        P.matmul(accs[bank][:, slot, :], pt[:, c, (j - i2) * 128:(j - i2 + 1) * 128], VA[:, kb, :],
                                 start=st_flag, stop=False, skip_group_check=True)

            for (kb, i2) in tiles:
                N = (4 - i2) * 128
                diag = kb >= 4 * QB
                m = 4 * QB - kb + 3
                ps = ps_s.next()
                for c in range(2):
                    P.matmul(ps[:, c, 0:N], KT[c][0:65, kb * 128:(kb + 1) * 128], QT[c][0:65, q0 + i2 * 128:q0 + NT],
                             start=True, stop=not diag, skip_group_check=True)
                    if diag:
                        P.matmul(ps[:, c, 0:128], idb.v, DT.v, start=False, stop=True, skip_group_check=True)
                pt = pT.next()
                P.act(pt[:, :, 0:N], ps[:, :, 0:N], AF.Exp, bias=BT[:, m:m + 1], scale=0.125)
                if prev is not None:
                    emit_pv(prev)
                prev = (kb, i2, pt)
            emit_pv(prev)
            for b3 in range(3):
                P.copy(accsb[b3].v, accs[b3].v, eng="dve" if b3 != 1 else "act")
            for j in range(4):
                b0, s0 = acc_slot(0, j); b1, s1 = acc_slot(1, j)
                r0 = fin["r0"].next(); r1 = fin["r1"].next(); o0 = fin["o0"].next(); o = fin["o"].next()
                sqo = fin["sqo"].next(); ss = fin["ss"].next(); out = fin["out"].next()
                P.recip(r0.v, accsb[b0][:, s0, 128:129])
                P.recip(r1.v, accsb[b1][:, s1, 128:129])
                P.tt(r1.v, r1.v, nlam.v, ALU.mult)
                P.ts(o0.v, accsb[b0][:, s0, 0:128], r0.v, ALU.mult)
                P.stt(o.v, accsb[b1][:, s1, 0:128], r1.v, o0.v, ALU.mult, ALU.add)
                P.act(sqo.v, o.v, AF.Square, accum_out=ss.v)
                P.act(ss.v, ss.v, AF.Ln, bias=EPS, scale=1.0 / 128)
                P.act(ss.v, ss.v, AF.Exp, scale=-0.5)
                P.stt(out.v, o.v, ss.v, sls.v, ALU.mult, ALU.mult)
                P.dma("pool", yd[q0 + j * 128:q0 + (j + 1) * 128, i * 128:(i + 1) * 128], out.v)
    P.emit()
    return nc


def prep_Bd(z_b, hh, l, prm):
    import ml_dtypes
    S = SEQ
    base = 256 + 1024
    q = z_b[:, base:base + 512].reshape(S, 4, 2, 64)
    k = z_b[:, base + 512:base + 1024].reshape(S, 4, 2, 64)
    v = z_b[:, base + 1024:base + 1536].reshape(S, 4, 128)
    hs = diff_heads(hh)
    qin = np.ascontiguousarray(q[:, hs].transpose(1, 2, 3, 0))
    kin = np.ascontiguousarray(k[:, hs].transpose(1, 2, 3, 0))
    vin = np.ascontiguousarray(v[:, hs].transpose(1, 0, 2))
    li = 0.8 - 0.6 * math.exp(-0.3 * l)
    qrow = np.zeros((2, 1, S), np.float32)
    btab = np.zeros((2, 128, 64), np.float32)
    dtab = np.zeros((2, 128, 128), np.float32)
    ki = np.arange(128)[:, None]
    qi = np.arange(128)[None, :]
    for i, h in enumerate(hs):
        sl = ALIBI[h]
        j = (np.arange(S) // 128) % 4
        qrow[i, 0] = -sl * 128.0 * j * 8.0
        mm = np.arange(64)[None, :]
        btab[i] = sl * (ki - 128.0 * (mm - 3))
        vis = (ki // 64) <= (qi // 64)
        dpr = np.where(qi >= ki, 0.0, 2.0 * sl * (qi - ki))
        dtab[i] = np.where(vis, dpr * 8.0, NEG * 8.0)
    cD = np.zeros((128, 8), np.float32)
    cD[0:64, 0:2] = prm["diff_q_norm"].T
    cD[0:64, 2:4] = prm["diff_k_norm"].T
    cD[:, 4] = li
    cD[:, 5] = 1.0 - li
    lvec = np.stack([np.broadcast_to(prm[n], (128, 64)) for n in ("diff_lq1", "diff_lk1", "diff_lq2", "diff_lk2")], 1)
    return {"qin": qin, "kin": kin, "vin": vin, "qrow": qrow.astype(ml_dtypes.bfloat16), "btab": btab,
            "dtab": dtab.astype(ml_dtypes.bfloat16), "identb": np.eye(128, dtype=np.float32).astype(ml_dtypes.bfloat16),
            "cD": cD, "lvec": np.ascontiguousarray(lvec.astype(np.float32)),
            "subln": np.ascontiguousarray(np.broadcast_to(prm["diff_subln"], (128, 128)).astype(np.float32))}


RWKV_GN_EPS = 64e-5


def build_Br():
    nc = new_nc()
    P = Prog(nc)
    S = SEQ
    SC = 512
    NCH = SC // 128
    rkv = P.dram("rkv", [2, 3, 64, 1 + S], F32, kind="ExternalInput")
    lw = P.dram("lw", [64, 1 + S], F32, kind="ExternalInput")
    la = P.dram("la", [64, 1 + S], F32, kind="ExternalInput")
    lg = P.dram("lg", [128, 1 + S], F32, kind="ExternalInput")
    cR = P.dram("cR", [128, 32], F32, kind="ExternalInput")
    w2 = P.dram("w2", [2, 64, 64], F32, kind="ExternalInput")
    a2 = P.dram("a2", [2, 64, 64], F32, kind="ExternalInput")
    g2 = P.dram("g2", [2, 128, 64], F32, kind="ExternalInput")
    msk = P.dram("msk", [128, 384], F32, kind="ExternalInput")
    idn = P.dram("idn", [128, 128], F32, kind="ExternalInput")
    rmask = P.dram("rmask", [64, SC], F32, kind="ExternalInput")
    yr = P.dram("yr", [128, S], F32, kind="ExternalOutput")

    cs = P.sb("cs", [128, 32]); P.dma("sp", cs.v, cR.v)
    mk = P.sb("mk", [128, 384]); P.dma("sp", mk.v, msk.v)
    ident = P.sb("ident", [128, 128]); P.dma("sp", ident.v, idn.v)
    rm = P.sb("rm", [64, SC]); P.dma("sp", rm.v, rmask.v)
    w2s = [P.sb("w2s", [64, 64]) for _ in range(2)]
    a2s = [P.sb("a2s", [64, 64]) for _ in range(2)]
    g2s = [P.sb("g2s", [128, 64]) for _ in range(2)]
    for i in range(2):
        P.dma("sp", w2s[i].v, w2[i]); P.dma("sp", a2s[i].v, a2[i]); P.dma("sp", g2s[i].v, g2[i])
    ones = P.sb("ones", [64, 64]); P.memset(ones.v, 1.0)
    o64 = P.sb("o64", [64, 64]); P.memset(o64.v, 1.0 / 64)
    omka = P.sb("omka", [64, 2])
    for i in range(2):
        b = 4 + 12 * i
        P.ts(omka[:, i:i + 1], cs[0:64, b + 6:b + 7], -1.0, ALU.mult, 1.0, ALU.add)

    PS = Rot([P.ps("ps", [128, 512]) for _ in range(8)])

    def R(name, shape, n=2, dt=F32):
        return Rot([P.sb(name, shape, dt) for _ in range(n)])

    A = {}
    for nm in ("zr", "zk", "zv"):
        A[nm] = R(nm, [64, SC + 1], 2)
    A["zw"] = R("zw", [64, SC + 1], 2); A["za"] = R("za", [64, SC + 1], 2); A["zg"] = R("zg", [128, SC + 1], 2)
    for nm in ("d64", "rm_", "km_", "vm_", "tw", "adm", "ld", "asig", "g", "kk", "sq", "t1", "t2", "kkn", "k2", "av",
               "bv", "Lc", "eL", "eLp", "enL", "ex", "YT", "dlt", "yn", "bon", "yo"):
        A[nm] = R(nm, [64, SC], 2)
    A["d128"] = R("d128", [128, SC], 2); A["sg"] = R("sg", [128, SC], 2); A["gdm"] = R("gdm", [128, SC], 2)
    A["AR"] = R("AR", [64, NCH, 2, 128], 2); A["BK"] = R("BK", [64, NCH, 2, 128], 2); A["BKH"] = R("BKH", [64, 2, SC], 2)
    A["eLC"] = R("eLC", [64, NCH], 2)
    TM = R("TM", [128, 4, 64], 8, BF16); SB1 = R("SB1", [128, 256], 8, BF16); SB2 = R("SB2", [128, 256], 8, BF16)
    PX = R("PX", [128, 256], 8, BF16); UX = R("UX", [128, 256], 8, BF16); PTt = R("PTt", [128, 128], 8, BF16); MC = R("MC", [64, 64], 8); QH = R("QH", [64, 128], 8)
    ST = [[P.sb("ST", [64, 64]) for _ in range(2)] for _ in range(2)]
    sti = [0, 0]
    for i in range(2):
        P.memset(ST[i][0].v, 0.0)
    flip = [0]

    def ev():
        flip[0] += 1
        return "dve" if flip[0] % 2 else "act"

    def shift_mix(zt, rows, mucol, dtmp, out):
        P.tt(dtmp[0:rows, :], zt[0:rows, 0:SC], zt[0:rows, 1:SC + 1], ALU.subtract)
        P.stt(out[0:rows, :], dtmp[0:rows, :], cs[0:rows, mucol:mucol + 1], zt[0:rows, 1:SC + 1], ALU.mult, ALU.add)

    for sc in range(S // SC):
        t0 = sc * SC
        zw = A["zw"].next(); za = A["za"].next(); zg = A["zg"].next()
        P.dma("sp", zw.v, lw[:, t0:t0 + SC + 1]); P.dma("sp", za.v, la[:, t0:t0 + SC + 1]); P.dma("sp", zg.v, lg[:, t0:t0 + SC + 1])
        d64 = A["d64"].next(); d128 = A["d128"].next()
        tw = A["tw"].next(); adm = A["adm"].next(); sg = A["sg"].next(); gdm = A["gdm"].next()
        shift_mix(zw, 64, 0, d128, tw); P.act(tw.v, tw.v, AF.Tanh)
        shift_mix(za, 64, 1, d128, adm)
        shift_mix(zg, 128, 2, d128, gdm); P.act(sg.v, gdm.v, AF.Sigmoid)
        H = [None, None]

        def prep(i):
            b = 4 + 12 * i
            zr = A["zr"].next(); zk = A["zk"].next(); zv = A["zv"].next()
            P.dma("pool", zr.v, rkv[i, 0, :, t0:t0 + SC + 1]); P.dma("pool", zk.v, rkv[i, 1, :, t0:t0 + SC + 1])
            P.dma("pool", zv.v, rkv[i, 2, :, t0:t0 + SC + 1])
            rm_ = A["rm_"].next(); km_ = A["km_"].next(); vm_ = A["vm_"].next()
            d64 = A["d64"].next()
            shift_mix(zr, 64, b + 0, d64, rm_); shift_mix(zk, 64, b + 1, d64, km_); shift_mix(zv, 64, b + 2, d64, vm_)
            yield
            ps = PS.next(); ld = A["ld"].next()
            P.matmul(ps[0:64, :], w2s[i].v, tw.v)
            P.act(ld.v, ps[0:64, :], AF.Sigmoid, bias=cs[0:64, b + 3:b + 4])
            P.act(ld.v, ld.v, AF.Copy, scale=-math.exp(-0.5))
            yield
            ps = PS.next(); asig = A["asig"].next()
            P.matmul(ps[0:64, :], a2s[i].v, adm.v)
            P.act(asig.v, ps[0:64, :], AF.Sigmoid, bias=cs[0:64, b + 4:b + 5])
            yield
            ps = PS.next(); g = A["g"].next()
            P.matmul(ps[0:64, :], g2s[i].v, sg.v)
            P.copy(g.v, ps[0:64, :], eng="act")
            yield
            kk = A["kk"].next(); sq = A["sq"].next(); t1 = A["t1"].next(); kkn = A["kkn"].next()
            P.act(kk.v, km_.v, AF.Copy, scale=cs[0:64, b + 5:b + 6])
            P.tt(sq.v, kk.v, kk.v, ALU.mult, eng="pool")
            yield
            ps = PS.next()
            P.matmul(ps[0:64, :], ones.v, sq.v)
            P.ts(t1.v, ps[0:64, :], 1e-24, ALU.max)
            P.act(t1.v, t1.v, AF.Ln); P.act(t1.v, t1.v, AF.Exp, scale=-0.5)
            P.tt(kkn.v, kk.v, t1.v, ALU.mult)
            yield
            t2 = A["t2"].next(); k2 = A["k2"].next()
            P.ts(t2.v, asig.v, cs[0:64, b + 6:b + 7], ALU.mult, omka[:, i:i + 1], ALU.add)
            P.tt(k2.v, km_.v, t2.v, ALU.mult, eng="pool")
            yield
            bv = A["bv"].next()
            P.tt(bv.v, kkn.v, asig.v, ALU.mult, eng="pool")
            yield
            Lc = A["Lc"].next(); eL = A["eL"].next(); eLp = A["eLp"].next(); enL = A["enL"].next()
            lo, m_, x_ = Lc.h[:], rm.h[:], ld.h[:]
            P.op("dve", lambda e, lo=lo, m_=m_, x_=x_: e.tensor_tensor_scan(lo, m_, x_, 0.0, ALU.mult, ALU.add),
                 reads=[rm, ld], writes=[Lc])
            yield
            P.act(eL.v, Lc.v, AF.Exp)
            P.act(enL.v, Lc.v, AF.Exp, scale=-1.0)
            P.tt(eLp.v, Lc.v, ld.v, ALU.subtract)
            P.act(eLp.v, eLp.v, AF.Exp)
            yield
            AR = A["AR"].next(); BK = A["BK"].next(); BKH = A["BKH"].next(); eLC = A["eLC"].next()
            v3 = lambda t: t.v.rearrange("p (c t) -> p c t", t=128)
            P.stt(AR[:, :, 0, :], v3(kkn), -1.0, v3(eLp), ALU.mult, ALU.mult)
            P.tt(AR[:, :, 1, :], v3(rm_), v3(eL), ALU.mult, eng="pool")
            P.tt(BK[:, :, 0, :], v3(bv), v3(enL), ALU.mult)
            P.tt(BK[:, :, 1, :], v3(k2), v3(enL), ALU.mult, eng="pool")
            P.copy(eLC.v, v3(eL)[:, :, 127])
            yield
            ex = A["ex"].next()
            for c in range(NCH):
                P.act(ex[:, c * 128:(c + 1) * 128], Lc[:, c * 128:(c + 1) * 128], AF.Exp,
                      bias=Lc[:, c * 128 + 127:c * 128 + 128], scale=-1.0)
            yield
            P.tt(BKH[:, 0, :], bv.v, ex.v, ALU.mult)
            P.tt(BKH[:, 1, :], k2.v, ex.v, ALU.mult, eng="pool")
            YT = A["YT"].next()
            H[i] = (dict(i=i, b=b, AR=AR, BK=BK, BKH=BKH, eLC=eLC, vm_=vm_, rm_=rm_, k2=k2, g=g, YT=YT))
        gens = [prep(0), prep(1)]
        alive = True
        while alive:
            alive = False
            for g_ in gens:
                try:
                    next(g_)
                    alive = True
                except StopIteration:
                    pass
        for hd in H:
            i = hd["i"]; AR = hd["AR"]; BK = hd["BK"]; BKH = hd["BKH"]; vm_ = hd["vm_"]
            CH = []
            for c in range(NCH):
                cs_ = slice(c * 128, (c + 1) * 128)
                ch = dict(c=c, cs_=cs_, ARc=AR[:, c].rearrange("p a b -> p (a b)"), tm=TM.next(), sb1=SB1.next(),
                          sb2=SB2.next(), px=PX.next(), pt=PTt.next(), mc=MC.next(), qh=QH.next())
                CH.append(ch)
            for ch in CH:
                c = ch["c"]; cs_ = ch["cs_"]; ps = PS.next()
                P.transpose(ps[:, 0:64], AR[:, c, 0, :], ident[0:64, 0:64])
                P.transpose(ps[:, 64:128], vm_[:, cs_], ident[0:64, 0:64])
                P.transpose(ps[:, 128:192], BKH[:, 0, cs_], ident[0:64, 0:64])
                P.transpose(ps[:, 192:256], BKH[:, 1, cs_], ident[0:64, 0:64])
                P.copy(ch["tm"].v.rearrange("p a b -> p (a b)"), ps[:, 0:256], eng=ev())
            for ch in CH:
                c = ch["c"]; ps = PS.next()
                P.matmul(ps[:, 0:256], BK[:, c, 0, :], ch["ARc"])
                P.tt(ch["sb1"].v, ps[:, 0:256], mk[:, 0:256], ALU.mult)
            for ch in CH:
                c = ch["c"]; ps = PS.next()
                P.matmul(ps[:, 0:256], BK[:, c, 1, :], ch["ARc"])
                P.tt(ch["sb2"].v, ps[:, 0:256], mk[:, 0:256], ALU.mult)
            for ch in CH:
                c = ch["c"]; ps = PS.next()
                P.matmul(ps[:, 0:128], AR[:, c, 0, :], BK[:, c, 0, :])
                P.tt(ch["px"][:, 0:128], ps[:, 0:128], mk[:, 256:384], ALU.mult)
            for ch in CH:
                ps = PS.next()
                P.matmul(ps[:, 0:64], ch["sb2"][:, 0:128], ch["tm"][:, 1, :])
                P.copy(ch["px"][:, 192:256], ps[:, 0:64], eng=ev())
                P.copy(ch["px"][:, 128:192], ch["tm"][:, 0, :], eng="pool")
                P.copy(ch["pt"].v, ch["sb1"][:, 0:128], eng="pool")
            for j in range(7):
                for ch in CH:
                    px = ch["px"]; pt = ch["pt"]
                    ps = PS.next(); px2 = PX.next() if j < 6 else UX.next()
                    if j < 5:
                        P.matmul(ps[:, 0:256], pt.v, px.v)
                        P.copy(px2[:, 0:128], ps[:, 0:128], eng="act")
                    else:
                        P.matmul(ps[:, 128:256], pt.v, px[:, 128:256])
                    P.tt(px2[:, 128:256], ps[:, 128:256], px[:, 128:256], ALU.add)
                    ch["px2"] = px2
                for ch in CH:
                    if j < 6:
                        ps = PS.next(); pt2 = PTt.next()
                        P.matmul(ps[:, 0:128], ch["px"][:, 0:128], ch["pt"].v)
                        P.copy(pt2.v, ps[:, 0:128], eng=ev())
                        ch["pt"] = pt2
                    ch["px"] = ch["px2"]
            for ch in CH:
                c = ch["c"]
                Ua = ch["px"][:, 128:192]
                ps = PS.next()
                P.matmul(ps[0:64, 0:64], Ua, ch["tm"][:, 2, :])
                P.copy(ch["mc"].v, ps[0:64, 0:64], eng=ev())
                ps = PS.next()
                P.matmul(ps[0:64, 0:128], Ua, ch["sb1"][:, 128:256])
                P.tt(ch["qh"].v, ps[0:64, 0:128], AR[:, c, 1, :], ALU.add)
            hd["CH"] = CH
        for c in range(NCH):
            for hd in H:
                i = hd["i"]; ch = hd["CH"][c]; tm = ch["tm"]; sb1 = ch["sb1"]; sb2 = ch["sb2"]
                Uv = ch["px"][:, 192:256]
                st_old = ST[i][sti[i] % 2]; st_new = ST[i][(sti[i] + 1) % 2]; sti[i] += 1
                ps = PS.next()
                P.matmul(ps[0:64, 0:128], st_old.v, ch["qh"].v, start=True, stop=False)
                P.matmul(ps[0:64, 0:128], Uv, sb1[:, 128:256], start=False, stop=False)
                P.matmul(ps[0:64, 0:128], tm[:, 1, :], sb2[:, 128:256], start=False, stop=True)
                P.copy(hd["YT"][:, ch["cs_"]], ps[0:64, 0:128], eng="act")
                ps = PS.next()
                P.matmul(ps[0:64, 0:64], ch["mc"].v, st_old.v, start=True, stop=False)
                P.matmul(ps[0:64, 0:64], tm[:, 2, :], Uv, start=False, stop=False)
                P.matmul(ps[0:64, 0:64], tm[:, 3, :], tm[:, 1, :], start=False, stop=True)
                P.stt(st_new.v, st_old.v, hd["eLC"][:, c:c + 1], ps[0:64, 0:64], ALU.mult, ALU.add)
        for hd in H:
            i = hd["i"]; b = hd["b"]; YT = hd["YT"]; rm_ = hd["rm_"]; k2 = hd["k2"]; vm_ = hd["vm_"]; g = hd["g"]
            dlt = A["dlt"].next(); yn = A["yn"].next(); bon = A["bon"].next(); yo = A["yo"].next()
            ps = PS.next()
            P.matmul(ps[0:64, :], o64.v, YT.v)
            P.tt(dlt.v, YT.v, ps[0:64, :], ALU.subtract)
            sq2 = A["sq"].next()
            P.tt(sq2.v, dlt.v, dlt.v, ALU.mult, eng="pool")
            ps = PS.next()
            P.matmul(ps[0:64, :], o64.v, sq2.v)
            t3 = A["t1"].next()
            P.act(t3.v, ps[0:64, :], AF.Ln, bias=RWKV_GN_EPS)
            P.act(t3.v, t3.v, AF.Exp, scale=-0.5)
            P.tt(yn.v, dlt.v, t3.v, ALU.mult)
            P.ts(yn.v, yn.v, cs[0:64, b + 9:b + 10], ALU.mult, cs[0:64, b + 10:b + 11], ALU.add)
            t4 = A["t2"].next()
            P.stt(t4.v, rm_.v, cs[0:64, b + 8:b + 9], k2.v, ALU.mult, ALU.mult)
            ps = PS.next()
            P.matmul(ps[0:64, :], ones.v, t4.v)
            P.tt(bon.v, ps[0:64, :], vm_.v, ALU.mult)
            P.tt(yn.v, yn.v, bon.v, ALU.add, eng="pool")
            P.tt(yo.v, yn.v, g.v, ALU.mult, eng="pool")
            P.dma("sp", yr[i * 64:(i + 1) * 64, t0:t0 + SC], yo.v)
    P.emit()
    return nc


def prep_Br(z_b, hh, prm):
    S = SEQ
    zr = z_b[:, 256:1280]
    hs = [2 * hh, 2 * hh + 1]

    def fm(cols):
        out = np.zeros((cols.shape[1], 1 + S), np.float32)
        out[:, 1:] = cols.T
        return out
    rkv = np.stack([np.stack([fm(zr[:, o + h * 64:o + (h + 1) * 64]) for o in (0, 256, 512)]) for h in hs])
    mu = prm["rwkv_mu"]
    cR = np.zeros((128, 32), np.float32)
    cR[0:64, 0] = mu[768:832]; cR[0:64, 1] = mu[832:896]; cR[:, 2] = mu[896:1024]
    for i, h in enumerate(hs):
        b = 4 + 12 * i
        s = slice(h * 64, (h + 1) * 64)
        cR[0:64, b + 0] = mu[0:256][s]; cR[0:64, b + 1] = mu[256:512][s]; cR[0:64, b + 2] = mu[512:768][s]
        cR[0:64, b + 3] = prm["rwkv_w0"][s]; cR[0:64, b + 4] = prm["rwkv_a0"][s]
        cR[0:64, b + 5] = prm["rwkv_k_k"][s]; cR[0:64, b + 6] = prm["rwkv_k_a"][s]
        cR[0:64, b + 8] = prm["rwkv_r_k"][h]; cR[0:64, b + 9] = prm["rwkv_ln_w"][s]; cR[0:64, b + 10] = prm["rwkv_ln_b"][s]
    t = np.arange(128)
    su = (t[None, :] > t[:, None]).astype(np.float32)
    iu = (t[None, :] >= t[:, None]).astype(np.float32)
    slm = (t[:, None] > t[None, :]).astype(np.float32)
    rmask = np.ones((64, 512), np.float32); rmask[:, ::128] = 0.0
    return {"rkv": np.ascontiguousarray(rkv), "lw": fm(zr[:, 768:832]), "la": fm(zr[:, 832:896]), "lg": fm(zr[:, 896:1024]),
            "cR": cR,
            "w2": np.ascontiguousarray(np.stack([prm["rwkv_w2"][:, h * 64:(h + 1) * 64] for h in hs])),
            "a2": np.ascontiguousarray(np.stack([prm["rwkv_a2"][:, h * 64:(h + 1) * 64] for h in hs])),
            "g2": np.ascontiguousarray(np.stack([prm["rwkv_g2"][:, h * 64:(h + 1) * 64] for h in hs])),
            "msk": np.concatenate([su, iu, slm], 1), "idn": np.eye(128, dtype=np.float32), "rmask": rmask}


def build_C1():
    nc = new_nc()
    P = Prog(nc)
    TOK = 4096
    n = 256
    M = 256
    hT = P.dram("hT", [D, TOK], F32, kind="ExternalInput")
    yT = P.dram("yT", [D, TOK], F32, kind="ExternalInput")
    memT = P.dram("memT", [D, M], F32, kind="ExternalInput")
    wd = {k: P.dram(k, [D, D], F32, kind="ExternalInput") for k in ("w_out", "wq", "wk", "wv", "wo")}
    cC = P.dram("cC", [128, 20], F32, kind="ExternalInput")
    h2T = P.dram("h2T", [D, TOK], F32, kind="ExternalOutput")
    cs = P.sb("cs", [128, 20]); P.dma("sp", cs.v, cC.v)
    ones = P.sb("ones", [128, 128]); P.memset(ones.v, 1.0)
    onesb = P.sb("onesb", [128, 128], BF16); P.memset(onesb.v, 1.0)
    stages = Rot([P.sb("stg", [128, 1024]) for _ in range(2)])
    bufA = P.sb("bufA", [128, 8, D], BF16); bufB = P.sb("bufB", [128, 8, D], BF16); bufC = P.sb("bufC", [128, 8, D], BF16)
    PS = Rot([P.ps("ps", [128, 512]) for _ in range(7)])
    ps_ss = P.ps("ps_ss", [128, 512])
    sqrot = Rot([P.sb("sq", [128, n]) for _ in range(2)])
    rstd = P.sb("rstd", [128, n]); tmp = P.sb("tmp", [128, n])
    flip = [0]

    def ev():
        flip[0] += 1
        return "dve" if flip[0] % 2 else "act"

    load_weight_bf(P, wd["wk"], D, D, bufA, gain=cs[:, 8:16], stages=stages, ncol=1024)
    load_weight_bf(P, wd["wv"], D, D, bufB, gain=cs[:, 8:16], stages=stages, ncol=1024)
    load_weight_bf(P, wd["wq"], D, D, bufC, gain=cs[:, 0:8], stages=stages, ncol=1024)
    qraws = Rot([P.sb("qraw", [128, 8, n]) for _ in range(2)])
    mt = qraws.tiles[0]; P.dma("sp", mt.v, memT.v.rearrange("(c p) m -> p c m", p=128))
    memn = P.sb("memn", [128, 8, M], BF16)
    rms_rstd(P, [mt[:, c, :] for c in range(8)], 128, M, 1.0 / D, EPS, ones.v, sqrot, ps_ss, rstd[:, 0:M], tmp)
    for c in range(8):
        P.tt(memn[:, c, :], mt[:, c, :], rstd[:, 0:M], ALU.mult, eng="dve" if c % 2 else "pool")
    kraw = qraws.tiles[1]; kn = P.sb("kn", [128, 8, M], BF16); vb = P.sb("vb", [128, 2, D], BF16)
    for oc in range(8):
        ps = PS.next()
        for kc in range(8):
            P.matmul(ps[:, 0:M], bufA[:, kc, oc * 128:(oc + 1) * 128], memn[:, kc, :], start=(kc == 0), stop=(kc == 7))
        P.copy(kraw[:, oc, :], ps[:, 0:M], eng=ev())
    for hd in range(4):
        rms_rstd(P, [kraw[:, 2 * hd, :], kraw[:, 2 * hd + 1, :]], 128, M, 1.0 / 256, EPS, ones.v, sqrot, ps_ss, rstd[:, 0:M], tmp)
        for dc in range(2):
            P.stt(kn[:, 2 * hd + dc, :], kraw[:, 2 * hd + dc, :], cs[:, 18 + dc:19 + dc], rstd[:, 0:M], ALU.mult, ALU.mult)
    for mc in range(2):
        for half in range(2):
            ps = PS.next()
            for kc in range(8):
                P.matmul(ps.v, memn[:, kc, mc * 128:(mc + 1) * 128], bufB[:, kc, half * 512:(half + 1) * 512],
                         start=(kc == 0), stop=(kc == 7))
            P.copy(vb[:, mc, half * 512:(half + 1) * 512], ps.v, eng=ev())
    load_weight_bf(P, wd["w_out"], D, D, bufA, gain=None, stages=stages, ncol=1024)
    load_weight_bf(P, wd["wo"], D, D, bufB, gain=None, stages=stages, ncol=1024)
    hts = Rot([P.sb("ht", [128, 8, n]) for _ in range(3)])
    yts = Rot([P.sb("yt", [128, 8, n]) for _ in range(2)])
    ybfs = Rot([P.sb("ybf", [128, 8, n], BF16) for _ in range(2)]); xns = Rot([P.sb("xn", [128, 8, n], BF16) for _ in range(2)])
    qns = Rot([P.sb("qn", [128, 8, n], BF16) for _ in range(2)])
    obs = Rot([P.sb("ob", [128, 8, n], BF16) for _ in range(2)])
    pTs = Rot([P.sb("pT", [128, 2 * n], BF16) for _ in range(8)])
    rs = Rot([P.sb("rs", [128, n]) for _ in range(8)])
    sqrot4 = Rot([P.sb("sq4", [128, n]) for _ in range(4)])
    tmp4 = Rot([P.sb("tmp4", [128, n]) for _ in range(4)]); rstd4 = Rot([P.sb("rstd4", [128, n]) for _ in range(4)])
    rstdT = Rot([P.sb("rstdT", [128, n]) for _ in range(2)])

    def tile_gen(ti):
        t0 = ti * n
        ht = hts.next(); yt = yts.next(); ybf = ybfs.next(); xn = xns.next(); qraw = qraws.next(); qn = qns.next()
        ob = obs.next(); rstd_t = rstdT.next()
        P.dma("sp", ht.v, hT[:, t0:t0 + n].rearrange("(c p) n -> p c n", p=128))
        P.dma("pool", yt.v, yT[:, t0:t0 + n].rearrange("(c p) n -> p c n", p=128))
        for c in range(8):
            P.copy(ybf[:, c, :], yt[:, c, :], eng="pool" if c % 4 == 3 else ("act" if c % 2 else "dve"))
        yield
        for oc in range(8):
            ps = PS.next()
            for kc in range(8):
                P.matmul(ps[:, 0:n], bufA[:, kc, oc * 128:(oc + 1) * 128], ybf[:, kc, :], start=(kc == 0), stop=(kc == 7))
            P.tt(ht[:, oc, :], ps[:, 0:n], ht[:, oc, :], ALU.add)
        rms_rstd(P, [ht[:, c, :] for c in range(8)], 128, n, 1.0 / D, EPS, ones.v, sqrot, ps_ss, rstd_t.v, tmp)
        for c in range(8):
            P.tt(xn[:, c, :], ht[:, c, :], rstd_t.v, ALU.mult, eng="pool" if c % 4 == 3 else "dve")
        yield
        for oc in range(8):
            ps = PS.next()
            for kc in range(8):
                P.matmul(ps[:, 0:n], bufC[:, kc, oc * 128:(oc + 1) * 128], xn[:, kc, :], start=(kc == 0), stop=(kc == 7))
            P.copy(qraw[:, oc, :], ps[:, 0:n], eng=ev())
        hps = []
        for hd in range(4):
            ps = PS.next()
            for dc in range(2):
                sq = sqrot4.next()
                P.act(sq.v, qraw[:, 2 * hd + dc, :], AF.Square)
                P.matmul(ps[:, 0:n], ones.v, sq.v, start=(dc == 0), stop=(dc == 1))
            hps.append(ps)
        tms = []
        for hd in range(4):
            t_ = tmp4.next()
            P.act(t_.v, hps[hd][:, 0:n], AF.Ln, bias=EPS, scale=1.0 / 256)
            tms.append(t_)
        rsq = []
        for hd in range(4):
            r_ = rstd4.next()
            P.act(r_.v, tms[hd].v, AF.Exp, scale=-0.5)
            rsq.append(r_)
        for hd in range(4):
            for dc in range(2):
                P.stt(qn[:, 2 * hd + dc, :], qraw[:, 2 * hd + dc, :], cs[:, 16 + dc:17 + dc], rsq[hd].v, ALU.mult, ALU.mult)
        yield
        ptl = []
        for hd in range(4):
            ps = PS.next()
            for mc in range(2):
                for dc in range(2):
                    P.matmul(ps[:, mc * n:(mc + 1) * n], kn[:, 2 * hd + dc, mc * 128:(mc + 1) * 128], qn[:, 2 * hd + dc, :],
                             start=(mc == 0 and dc == 0), stop=(dc == 1), skip_group_check=True)
            pt = pTs.next()
            P.act(pt.v, ps.v, AF.Exp, scale=1.0 / 16)
            ptl.append(pt)
        yield
        rsl = []
        for hd in range(4):
            ps = PS.next()
            for mc in range(2):
                P.matmul(ps[:, 0:n], onesb.v, ptl[hd][:, mc * n:(mc + 1) * n], start=(mc == 0), stop=(mc == 1))
            r = rs.next()
            P.recip(r.v, ps[:, 0:n])
            rsl.append(r)
        yield
        for hd in range(4):
            ps = PS.next()
            for dc in range(2):
                for mc in range(2):
                    P.matmul(ps[:, dc * n:(dc + 1) * n], vb[:, mc, (2 * hd + dc) * 128:(2 * hd + dc + 1) * 128],
                             ptl[hd][:, mc * n:(mc + 1) * n], start=(dc == 0 and mc == 0), stop=(mc == 1),
                             skip_group_check=True)
            for dc in range(2):
                P.tt(ob[:, 2 * hd + dc, :], ps[:, dc * n:(dc + 1) * n], rsl[hd].v, ALU.mult)
        yield
        for oc in range(8):
            ps = PS.next()
            for kc in range(8):
                P.matmul(ps[:, 0:n], bufB[:, kc, oc * 128:(oc + 1) * 128], ob[:, kc, :], start=(kc == 0), stop=(kc == 7))
            P.tt(ht[:, oc, :], ps[:, 0:n], ht[:, oc, :], ALU.add)
        P.dma("sp", h2T[:, t0:t0 + n].rearrange("(c p) n -> p c n", p=128), ht.v)

    ntl = TOK // n
    live = []
    nxt_t = 0
    while live or nxt_t < ntl:
        while len(live) < 2 and nxt_t < ntl:
            live.append(tile_gen(nxt_t)); nxt_t += 1
        for g_ in list(live):
            try:
                next(g_)
            except StopIteration:
                live.remove(g_)
    P.emit()
    return nc


def vec_pc(v):
    return np.ascontiguousarray(np.asarray(v, np.float32).reshape(-1, 128).T)


def prep_C1(hT, yT, mem_b, prm):
    cC = np.concatenate([vec_pc(prm["xa_norm_g"]), vec_pc(prm["mem_norm_g"]), vec_pc(prm["xa_q_norm"]),
                         vec_pc(prm["xa_k_norm"])], 1)
    return {"hT": hT, "yT": yT, "memT": np.ascontiguousarray(mem_b.T), "w_out": prm["w_out"], "wq": prm["xa_wq"],
            "wk": prm["xa_wk"], "wv": prm["xa_wv"], "wo": prm["xa_wo"], "cC": np.ascontiguousarray(cC)}


def build_C2():
    nc = new_nc()
    P = Prog(nc)
    TOK = 4096
    n = 256
    NF = D_FF // 128
    h2T = P.dram("h2T", [D, 2 + TOK], F32, kind="ExternalInput")
    w_up = P.dram("w_up", [D, 2 * D_FF], F32, kind="ExternalInput")
    w_dn = P.dram("w_dn", [D_FF, D], F32, kind="ExternalInput")
    cF = P.dram("cF", [128, 8 + NF * 4], F32, kind="ExternalInput")
    h3T = P.dram("h3T", [D, TOK], F32, kind="ExternalOutput")
    cs = P.sb("cs", [128, 8 + NF * 4]); P.dma("sp", cs.v, cF.v)
    ones = P.sb("ones", [128, 128]); P.memset(ones.v, 1.0)
    stages = Rot([P.sb("stg", [128, 1408]) for _ in range(3)])
    wup = P.sb("wup", [128, 8, 2 * D_FF], BF16)
    wdn = P.sb("wdn", [128, NF, D], BF16)
    load_weight_bf(P, w_up, D, 2 * D_FF, wup, gain=cs[:, 0:8], stages=stages, ncol=1408)
    load_weight_bf(P, w_dn, D_FF, D, wdn, gain=None, stages=stages, ncol=1024)
    PS = Rot([P.ps("ps", [128, 512]) for _ in range(7)])
    ps_ss = P.ps("ps_ss", [128, 512])
    sqrot = Rot([P.sb("sq", [128, n]) for _ in range(2)])
    rstd = P.sb("rstd", [128, n]); tmp = P.sb("tmp", [128, n])
    hts = Rot([P.sb("ht", [128, 8, n]) for _ in range(3)])
    xns = Rot([P.sb("xn", [128, 8, n], BF16) for _ in range(2)])
    asb = Rot([P.sb("asb", [128, 2 + n]) for _ in range(3)])
    cc = Rot([P.sb("cc", [128, n]) for _ in range(2)])
    gl = Rot([P.sb("gl", [128, n]) for _ in range(2)])
    gbf = [P.sb("gbf", [128, n], BF16) for _ in range(NF)]
    aprev = P.sb("aprev", [128, NF, 2])

    def norm(c0, w):
        ht = hts.next(); xn = xns.next()
        P.dma("sp", ht[:, :, 0:w], h2T[:, c0:c0 + w].rearrange("(c p) n -> p c n", p=128))
        rms_rstd(P, [ht[:, c, 0:w] for c in range(8)], 128, w, 1.0 / D, EPS, ones.v, sqrot, ps_ss, rstd[:, 0:w], tmp)
        for c in range(8):
            P.tt(xn[:, c, 0:w], ht[:, c, 0:w], rstd[:, 0:w], ALU.mult, eng="dve" if c % 2 else "pool")
        return (c0, w, ht, xn)

    def body(cur, full, nxt):
        c0, w, ht, xn = cur
        res = None
        for f in range(NF):
            psa = PS.next()
            for kc in range(8):
                P.matmul(psa[:, 0:w], wup[:, kc, f * 128:(f + 1) * 128], xn[:, kc, 0:w], start=(kc == 0), stop=(kc == 7))
            if not full:
                P.copy(aprev[:, f, :], psa[:, 0:2], eng="act")
                continue
            psb = PS.next()
            for kc in range(8):
                P.matmul(psb[:, 0:w], wup[:, kc, D_FF + f * 128:D_FF + (f + 1) * 128], xn[:, kc, 0:w],
                         start=(kc == 0), stop=(kc == 7))
            if f == 3 and nxt is not None:
                res = norm(*nxt)
            a = asb.next()
            P.copy(a[:, 2:2 + w], psa[:, 0:w], eng="act")
            P.copy(a[:, 0:2], aprev[:, f, :], eng="pool")
            P.copy(aprev[:, f, :], a[:, w:w + 2], eng="pool")
            cb = 8 + f * 4
            c = cc.next()
            P.ts(c.v, a[:, 2:2 + w], cs[:, cb + 2:cb + 3], ALU.mult, cs[:, cb + 3:cb + 4], ALU.add)
            P.stt(c.v, a[:, 1:1 + w], cs[:, cb + 1:cb + 2], c.v, ALU.mult, ALU.add)
            P.stt(c.v, a[:, 0:w], cs[:, cb + 0:cb + 1], c.v, ALU.mult, ALU.add)
            g = gl.next()
            P.act(g.v, c.v, AF.Gelu)
            P.tt(gbf[f].v, psb[:, 0:w], g.v, ALU.mult)
        if not full:
            return norm(*nxt) if nxt is not None else None
        for oc in range(8):
            ps = PS.next()
            for f in range(NF):
                P.matmul(ps[:, 0:w], wdn[:, f, oc * 128:(oc + 1) * 128], gbf[f].v, start=(f == 0), stop=(f == NF - 1))
            P.tt(ht[:, oc, :], ps[:, 0:w], ht[:, oc, :], ALU.add)
        P.dma("pool", h3T[:, c0 - 2:c0 - 2 + w].rearrange("(c p) n -> p c n", p=128), ht.v)
        return res

    cur = norm(0, 2)
    ntl = TOK // n
    cur = body(cur, False, (2, n))
    for ti in range(ntl):
        nxt = (2 + (ti + 1) * n, n) if ti + 1 < ntl else None
        cur = body(cur, True, nxt)
    P.emit()
    return nc


def prep_C2(h2T_halo, prm):
    NF = D_FF // 128
    cw = prm["ffn_conv_w"]
    cols = [vec_pc(prm["ffn_norm_g"])]
    per = np.zeros((128, NF, 4), np.float32)
    for j in range(3):
        per[:, :, j] = cw[j].reshape(NF, 128).T
    per[:, :, 3] = prm["ffn_conv_b"].reshape(NF, 128).T
    cF = np.concatenate([cols[0], per.reshape(128, NF * 4)], 1)
    return {"h2T": h2T_halo, "w_up": prm["ffn_w_up"], "w_dn": prm["ffn_w_down"], "cF": np.ascontiguousarray(cF)}


_NC_CACHE = {}


def _get_nc(name):
    if name not in _NC_CACHE:
        _NC_CACHE[name] = {"A": build_A, "Bp": build_Bp, "Bd": build_Bd, "Br": build_Br, "C1": build_C1,
                           "C2": build_C2}[name]()
    return _NC_CACHE[name]


def _run(name, in_maps):
    nc = _get_nc(name)
    res = run_bass_kernel_spmd(nc, in_maps, core_ids=list(range(8)))
    return res.results


PARAM_KEYS = ("mix_norm_g", "w_in", "pool_w", "pool_scale", "rwkv_mu", "rwkv_w0", "rwkv_w2", "rwkv_a0", "rwkv_a2",
              "rwkv_g2", "rwkv_k_k", "rwkv_k_a", "rwkv_r_k", "rwkv_ln_w", "rwkv_ln_b", "diff_q_norm", "diff_k_norm",
              "diff_lq1", "diff_lk1", "diff_lq2", "diff_lk2", "diff_subln", "w_out", "xa_norm_g", "mem_norm_g",
              "xa_wq", "xa_wk", "xa_wv", "xa_wo", "xa_q_norm", "xa_k_norm", "ffn_norm_g", "ffn_w_up", "ffn_conv_w",
              "ffn_conv_b", "ffn_w_down")


def kernel(x, mem, **params):
    x = np.asarray(x, np.float32)
    mem = np.asarray(mem, np.float32)
    B, S, _ = x.shape
    TOK = S // 2
    depth = np.asarray(params["w_in"]).shape[0]
    hT = [np.ascontiguousarray(x[c // 2, (c % 2) * TOK:(c % 2 + 1) * TOK].T) for c in range(8)]
    for l in range(depth):
        prm = {k: np.ascontiguousarray(np.asarray(params[k][l], np.float32)) for k in PARAM_KEYS}
        cA = vec_pc(prm["mix_norm_g"])
        rA = _run("A", [{"hT": hT[c], "w_in": prm["w_in"], "cA": cA} for c in range(8)])
        zb = [np.concatenate([rA[2 * b]["zT"].T, rA[2 * b + 1]["zT"].T], axis=0) for b in range(B)]
        del rA
        rP = _run("Bp", [prep_Bp(zb[c // 2][:, 0:256], c % 2, prm["pool_w"], prm["pool_scale"]) for c in range(8)])
        rR = _run("Br", [prep_Br(zb[c // 2], c % 2, prm) for c in range(8)])
        rD = _run("Bd", [prep_Bd(zb[c // 2], c % 2, l, prm) for c in range(8)])
        del zb
        yT = []
        for c in range(8):
            b, hh = c // 2, c % 2
            y = np.empty((D, TOK), np.float32)
            y[0:256] = rP[c]["ypT"]
            for h2 in range(2):
                cc = 2 * b + h2
                y[256 + h2 * 128:256 + (h2 + 1) * 128] = rR[cc]["yr"][:, hh * TOK:(hh + 1) * TOK]
                for i_, hd_ in enumerate(diff_heads(h2)):
                    y[512 + hd_ * 128:512 + (hd_ + 1) * 128] = rD[cc]["yd"][hh * TOK:(hh + 1) * TOK, i_ * 128:(i_ + 1) * 128].T
            yT.append(y)
        del rP, rR, rD
        r1 = _run("C1", [prep_C1(hT[c], yT[c], mem[c // 2], prm) for c in range(8)])
        del yT
        h2 = [r1[c]["h2T"] for c in range(8)]
        del r1
        ins = []
        for c in range(8):
            hh = c % 2
            hal = np.zeros((D, 2 + TOK), np.float32)
            hal[:, 2:] = h2[c]
            if hh == 1:
                hal[:, 0:2] = h2[c - 1][:, TOK - 2:TOK]
            ins.append(prep_C2(hal, prm))
        r2 = _run("C2", ins)
        hT = [np.ascontiguousarray(r2[c]["h3T"]) for c in range(8)]
        del r2, h2, ins
    out = np.empty((B, S, D), np.float32)
    for c in range(8):
        out[c // 2, (c % 2) * TOK:(c % 2 + 1) * TOK] = hT[c].T
    return out
```
